# Optimizing a Trainium2 kernel written in Bass

```python
import numpy as np
import jax, jax.numpy as jnp
from jax import lax

D_MODEL = 1024
BATCH = 8
SEQ = 4096
DEPTH = 2

N_MIXERS = 2
N_HEADS = 16
HEAD_DIM = D_MODEL // N_HEADS
Q_BLOCK = 128
NSA_KV_GROUPS = 2
NSA_HPG = N_HEADS // NSA_KV_GROUPS
NSA_KV_W = NSA_KV_GROUPS * HEAD_DIM
CMP_LEN = 32
CMP_STRIDE = 16
CMP_HIDDEN = 2 * HEAD_DIM
SLC_LEN = 64
SLC_TOP_N = 16
WINDOW = 512
NSA_IN_W = D_MODEL + 6 * NSA_KV_W + 3 * N_HEADS
ROPE_THETA = 10000.0
D_FF = 2816
CONV_W = 3
RMS_EPS = 1e-6
NEG = -1e30
FORCE_BONUS = 1e4

kernel_name = "hybrid_stickbreak_nsa_convffn"


def rmsnorm(x, g):
    xf = x.astype(jnp.float32)
    y = xf * lax.rsqrt(jnp.mean(xf * xf, axis=-1, keepdims=True) + RMS_EPS)
    return (y * g.astype(jnp.float32)).astype(x.dtype)


def rope(x, pos):
    half = HEAD_DIM // 2
    inv = ROPE_THETA ** (-jnp.arange(half, dtype=jnp.float32) / half)
    ang = pos.astype(jnp.float32)[..., None] * inv
    cos = jnp.cos(ang)[:, :, None, :]
    sin = jnp.sin(ang)[:, :, None, :]
    xf = x.astype(jnp.float32)
    x1, x2 = xf[..., :half], xf[..., half:]
    return jnp.concatenate([x1 * cos - x2 * sin, x2 * cos + x1 * sin], axis=-1).astype(x.dtype)


def stick_breaking_attention(h, w_in, w_out):
    B, S, _ = h.shape
    nb = S // Q_BLOCK
    scale = HEAD_DIM ** -0.5
    q, k, v = jnp.split(h @ w_in, 3, axis=-1)
    q, k, v = [t.reshape(B, S, N_HEADS, HEAD_DIM).transpose(0, 2, 1, 3) for t in (q, k, v)]
    q_blocks = q.reshape(B, N_HEADS, nb, Q_BLOCK, HEAD_DIM).transpose(2, 0, 1, 3, 4)
    key_pos = jnp.arange(S)

    def block(args):
        qb, q0 = args
        z = jnp.einsum('bhqd,bhkd->bhqk', qb, k).astype(jnp.float32) * scale
        qpos = q0 + jnp.arange(Q_BLOCK)
        strict = key_pos[None, :] < qpos[:, None]
        log_beta = jax.nn.log_sigmoid(z)
        log_1m = jnp.where(strict, jax.nn.log_sigmoid(-z), 0.0)
        after = lax.cumsum(log_1m, axis=3, reverse=True) - log_1m
        a = jnp.where(strict, jnp.exp(log_beta + after), 0.0)
        return jnp.einsum('bhqk,bhkd->bhqd', a.astype(v.dtype), v)

    o = lax.map(block, (q_blocks, jnp.arange(nb, dtype=jnp.int32) * Q_BLOCK))
    o = o.transpose(1, 0, 3, 2, 4).reshape(B, S, D_MODEL)
    return o @ w_out


def native_sparse_attention(h, pos, w_in, cmp_pos_k, cmp_pos_v, cmp_k_w1, cmp_k_w2,
                            cmp_v_w1, cmp_v_w2, w_out):
    B, S, _ = h.shape
    G, HG, dh = NSA_KV_GROUPS, NSA_HPG, HEAD_DIM
    nb = S // Q_BLOCK
    n_cmp = (S - CMP_LEN) // CMP_STRIDE + 1
    n_slc = S // SLC_LEN
    top_n = min(SLC_TOP_N, n_slc)
    scale = dh ** -0.5

    splits = np.cumsum([D_MODEL] + [NSA_KV_W] * 6).tolist()
    q, kc, vc, ks, vs, kw, vw, gates = jnp.split(h @ w_in, splits, axis=-1)
    q = rope(q.reshape(B, S, N_HEADS, dh), pos)
    kvh = lambda t: t.reshape(B, S, G, dh)
    ks = rope(kvh(ks), pos)
    kw = rope(kvh(kw), pos)
    vs, vw = kvh(vs), kvh(vw)

    idx = np.arange(n_cmp)[:, None] * CMP_STRIDE + np.arange(CMP_LEN)[None, :]
    cmp_end = idx[:, -1]

    def compress(t, pe, w1, w2):
        blk = kvh(t)[:, idx] + pe[None, None, :, None, :]
        blk = blk.transpose(0, 1, 3, 2, 4).reshape(B, n_cmp, G, CMP_LEN * dh)
        return jax.nn.gelu(blk @ w1) @ w2

    kc = rope(compress(kc, cmp_pos_k, cmp_k_w1, cmp_k_w2), pos[:, cmp_end]).transpose(0, 2, 1, 3)
    vc = compress(vc, cmp_pos_v, cmp_v_w1, cmp_v_w2).transpose(0, 2, 1, 3)
    cmp_end_j = jnp.asarray(cmp_end)

    c0 = np.arange(n_cmp)[:, None] * CMP_STRIDE
    s0 = np.arange(n_slc)[None, :] * SLC_LEN
    overlap = np.clip(np.minimum(c0 + CMP_LEN, s0 + SLC_LEN) - np.maximum(c0, s0), 0, None) / CMP_LEN
    overlap = jnp.asarray(overlap, jnp.float32)

    ks_blocks = ks.transpose(0, 2, 1, 3).reshape(B, G, n_slc, SLC_LEN, dh)
    vs_blocks = vs.transpose(0, 2, 1, 3).reshape(B, G, n_slc, SLC_LEN, dh)
    pad = ((0, 0), (0, 0), (WINDOW, 0), (0, 0))
    kw_pad = jnp.pad(kw.transpose(0, 2, 1, 3), pad)
    vw_pad = jnp.pad(vw.transpose(0, 2, 1, 3), pad)

    q_blocks = q.reshape(B, nb, Q_BLOCK, G, HG, dh).transpose(1, 0, 3, 4, 2, 5)
    g_all = jax.nn.sigmoid(gates.astype(jnp.float32)).reshape(B, nb, Q_BLOCK, G, HG, 3)
    g_blocks = g_all.transpose(1, 0, 3, 4, 2, 5)
    bi = jnp.arange(B)[:, None, None, None]
    gi = jnp.arange(G)[None, :, None, None]
    blk_ids = jnp.arange(n_slc)

    def block(args):
        qb, gb, q0 = args
        qpos = q0 + jnp.arange(Q_BLOCK)
        s_c = jnp.einsum('bghqd,bgcd->bghqc', qb, kc).astype(jnp.float32) * scale
        m_c = cmp_end_j[None, :] <= qpos[:, None]
        p_c = jax.nn.softmax(jnp.where(m_c, s_c, NEG), axis=-1) * m_c
        o_c = jnp.einsum('bghqc,bgcd->bghqd', p_c.astype(vc.dtype), vc)
        imp = jnp.einsum('bghqc,cn->bgqn', p_c, overlap)
        cur = qpos // SLC_LEN
        forced = (blk_ids[None] == 0) | (blk_ids[None] == cur[:, None]) | (blk_ids[None] == cur[:, None] - 1)
        causal_blk = blk_ids[None] * SLC_LEN <= qpos[:, None]
        score = jnp.where(causal_blk, imp + FORCE_BONUS * forced, NEG)
        _, sel = lax.top_k(score, top_n)
        k_sel = ks_blocks[bi, gi, sel]
        v_sel = vs_blocks[bi, gi, sel]
        tok = sel[..., None] * SLC_LEN + jnp.arange(SLC_LEN)
        m_s = tok <= qpos[None, None, :, None, None]
        s_s = jnp.einsum('bghqd,bgqnld->bghqnl', qb, k_sel).astype(jnp.float32) * scale
        p_s = jax.nn.softmax(jnp.where(m_s[:, :, None], s_s, NEG), axis=(-2, -1))
        o_s = jnp.einsum('bghqnl,bgqnld->bghqd', p_s.astype(v_sel.dtype), v_sel)
        k_win = lax.dynamic_slice_in_dim(kw_pad, q0, WINDOW + Q_BLOCK, axis=2)
        v_win = lax.dynamic_slice_in_dim(vw_pad, q0, WINDOW + Q_BLOCK, axis=2)
        kpos = q0 - WINDOW + jnp.arange(WINDOW + Q_BLOCK)
        diff = qpos[:, None] - kpos[None, :]
        m_w = (diff >= 0) & (diff < WINDOW) & (kpos[None, :] >= 0)
        s_w = jnp.einsum('bghqd,bgkd->bghqk', qb, k_win).astype(jnp.float32) * scale
        p_w = jax.nn.softmax(jnp.where(m_w, s_w, NEG), axis=-1)
        o_w = jnp.einsum('bghqk,bgkd->bghqd', p_w.astype(v_win.dtype), v_win)
        out = gb[..., 0:1] * o_c + gb[..., 1:2] * o_s + gb[..., 2:3] * o_w
        return out.astype(qb.dtype)

    o = lax.map(block, (q_blocks, g_blocks, jnp.arange(nb, dtype=jnp.int32) * Q_BLOCK))
    o = o.transpose(1, 0, 4, 2, 3, 5).reshape(B, S, D_MODEL)
    return o @ w_out


def conv_ffn(h, w_up, conv_w, conv_b, w_down):
    S = h.shape[1]
    u = h @ w_up
    up = jnp.pad(u, ((0, 0), (CONV_W - 1, 0), (0, 0)))
    c = conv_b
    for j in range(CONV_W):
        c = c + up[:, j:j + S] * conv_w[j]
    gate, val = jnp.split(c, 2, axis=-1)
    return (jax.nn.silu(gate) * val) @ w_down


def setup_inputs(seed: int = 0) -> dict:
    key = jax.random.key(seed)
    ks = jax.random.split(key, 24)
    n_sba = len(range(0, DEPTH, N_MIXERS))
    n_nsa = len(range(1, DEPTH, N_MIXERS))
    f32 = jnp.float32
    nrm = lambda k, shape, fan_in: jax.random.normal(k, shape, f32) * fan_in ** -0.5
    gain = lambda k, shape: 1.0 + 0.05 * jax.random.normal(k, shape, f32)
    x = jax.random.normal(ks[0], (BATCH, SEQ, D_MODEL), f32)
    offs = jax.random.randint(ks[1], (BATCH, 1), 0, 1024, dtype=jnp.int32)
    positions = (jnp.arange(SEQ, dtype=jnp.int32)[None, :] + offs).astype(jnp.int32)
    return {
        "x": x,
        "positions": positions,
        "norm_mix": gain(ks[2], (DEPTH, D_MODEL)),
        "sba_w_in": nrm(ks[3], (n_sba, D_MODEL, 3 * D_MODEL), D_MODEL),
        "sba_w_out": nrm(ks[4], (n_sba, D_MODEL, D_MODEL), D_MODEL),
        "nsa_w_in": nrm(ks[5], (n_nsa, D_MODEL, NSA_IN_W), D_MODEL),
        "nsa_cmp_pos_k": 0.1 * jax.random.normal(ks[6], (n_nsa, CMP_LEN, HEAD_DIM), f32),
        "nsa_cmp_pos_v": 0.1 * jax.random.normal(ks[7], (n_nsa, CMP_LEN, HEAD_DIM), f32),
        "nsa_cmp_k_w1": nrm(ks[8], (n_nsa, CMP_LEN * HEAD_DIM, CMP_HIDDEN), CMP_LEN * HEAD_DIM),
        "nsa_cmp_k_w2": nrm(ks[9], (n_nsa, CMP_HIDDEN, HEAD_DIM), CMP_HIDDEN),
        "nsa_cmp_v_w1": nrm(ks[10], (n_nsa, CMP_LEN * HEAD_DIM, CMP_HIDDEN), CMP_LEN * HEAD_DIM),
        "nsa_cmp_v_w2": nrm(ks[11], (n_nsa, CMP_HIDDEN, HEAD_DIM), CMP_HIDDEN),
        "nsa_w_out": nrm(ks[12], (n_nsa, D_MODEL, D_MODEL), D_MODEL),
        "norm_ffn": gain(ks[13], (DEPTH, D_MODEL)),
        "ffn_w_up": nrm(ks[14], (DEPTH, D_MODEL, 2 * D_FF), D_MODEL),
        "ffn_conv_w": nrm(ks[15], (DEPTH, CONV_W, 2 * D_FF), CONV_W),
        "ffn_conv_b": 0.02 * jax.random.normal(ks[16], (DEPTH, 2 * D_FF), f32),
        "ffn_w_down": nrm(ks[17], (DEPTH, D_FF, D_MODEL), D_FF),
        "norm_final": gain(ks[18], (D_MODEL,)),
    }


def reference(x, positions, norm_mix, sba_w_in, sba_w_out, nsa_w_in, nsa_cmp_pos_k,
              nsa_cmp_pos_v, nsa_cmp_k_w1, nsa_cmp_k_w2, nsa_cmp_v_w1, nsa_cmp_v_w2,
              nsa_w_out, norm_ffn, ffn_w_up, ffn_conv_w, ffn_conv_b, ffn_w_down, norm_final):
    for i in range(DEPTH):
        hn = rmsnorm(x, norm_mix[i])
        j = i // N_MIXERS
        if i % N_MIXERS == 0:
            mix = stick_breaking_attention(hn, sba_w_in[j], sba_w_out[j])
        else:
            mix = native_sparse_attention(hn, positions, nsa_w_in[j], nsa_cmp_pos_k[j],
                                          nsa_cmp_pos_v[j], nsa_cmp_k_w1[j], nsa_cmp_k_w2[j],
                                          nsa_cmp_v_w1[j], nsa_cmp_v_w2[j], nsa_w_out[j])
        x = x + mix
        x = x + conv_ffn(rmsnorm(x, norm_ffn[i]), ffn_w_up[i], ffn_conv_w[i],
                         ffn_conv_b[i], ffn_w_down[i])
    return rmsnorm(x, norm_final)
```

```python
from contextlib import ExitStack
import numpy as np
import ml_dtypes
import concourse.bass as bass
import concourse.mybir as mybir
from concourse.bass_utils import run_bass_kernel_spmd

F32 = mybir.dt.float32
BF16 = mybir.dt.bfloat16
I32 = mybir.dt.int32
AF = mybir.ActivationFunctionType
ALU = mybir.AluOpType
AX = mybir.AxisListType

DBG = {}
S = 4096
D = 1024
NT = S // 128
DFF = 2816
NPAIR = DFF // 128
EPS = 1e-6
NEGM = -30000.0


def _flat(ts):
    for t in ts:
        if t is None:
            continue
        if isinstance(t, tuple) and len(t) == 3 and isinstance(t[1], int):
            yield t
        else:
            yield from _flat(t)


class Buf:
    def __init__(self, name=""):
        self.name = name
        self.w = None
        self.r = {}

    def rd_deps(self):
        return [self.w]

    def wr_deps(self):
        return [self.w] + list(self.r.values())

    def note_read(self, tok):
        if tok is None:
            return
        k = tok[2]
        if k not in self.r or self.r[k][1] < tok[1]:
            self.r[k] = tok

    def note_write(self, tok):
        self.w = tok
        self.r = {}


class Eng:
    def __init__(self, kb, name, eng):
        self.kb = kb
        self.name = name
        self.eng = eng
        self.sem = kb.newsem("e_" + name)
        self.n = 0
        self.seen = {}

    def wait(self, *toks):
        for t in _flat(toks):
            sem, val, key = t
            if self.seen.get(key, 0) >= val:
                continue
            self.seen[key] = val
            self.eng.wait_ge(sem, val)

    def done(self, ins):
        self.n += 1
        ins.then_inc(self.sem, 1)
        return (self.sem, self.n, self.name)


class Slot:
    def __init__(self, kb, name):
        self.sem = kb.newsem("d_" + name)
        self.val = 0
        self.key = "d_" + name

    def done(self, ins):
        self.val += 16
        ins.then_inc(self.sem, 16)
        return (self.sem, self.val, self.key)


class KB:
    def __init__(self):
        self.nc = bass.Bass("TRN2", target_bir_lowering=False)
        self.root = ExitStack()
        self.nsem = 0
        nc = self.nc
        self.pe = Eng(self, "pe", nc.tensor)
        self.act = Eng(self, "act", nc.scalar)
        self.dve = Eng(self, "dve", nc.vector)
        self.pool = Eng(self, "pool", nc.gpsimd)
        self.sp = Eng(self, "sp", nc.sync)
        self.engs = [self.pe, self.act, self.dve, self.pool, self.sp]
        self.slots = []
        self.uid = 0

    def newsem(self, name):
        self.nsem += 1
        return self.root.enter_context(self.nc.semaphore(name))

    def slot(self, name):
        s = Slot(self, name)
        self.slots.append(s)
        return s

    def name(self, p):
        self.uid += 1
        return f"{p}_{self.uid}"

    def op(self, E, make, rd=(), wr=(), sig=True, extra=()):
        deps = list(extra)
        for b in rd:
            deps.append(b.w)
        for b in wr:
            deps.extend(b.wr_deps())
        E.wait(deps)
        ins = make()
        tok = E.done(ins) if sig else None
        if tok is not None:
            for b in rd:
                b.note_read(tok)
            for b in wr:
                b.note_write(tok)
        return tok

    def dma(self, Q, slot, out, in_, rd=(), wr=(), extra=()):
        deps = list(extra)
        for b in rd:
            deps.append(b.w)
        for b in wr:
            deps.extend(b.wr_deps())
        Q.wait(deps)
        ins = Q.eng.dma_start(out=out, in_=in_)
        tok = slot.done(ins)
        for b in rd:
            b.note_read(tok)
        for b in wr:
            b.note_write(tok)
        return tok

    def batch_end(self, slot, bufs):
        tok = (slot.sem, slot.val, slot.key)
        for b in bufs:
            b.w = tok

    def mm_group(self, out_ap, obuf, terms, sig=True):
        pe = self.pe
        deps = list(obuf.wr_deps())
        for (_, _, bufs) in terms:
            for b in bufs:
                deps.append(b.w)
        pe.wait(deps)
        n = len(terms)
        ins = None
        for i, (l, r, _) in enumerate(terms):
            ins = self.nc.tensor.matmul(out_ap, l, r, start=(i == 0), stop=(i == n - 1))
        tok = pe.done(ins)
        for (_, _, bufs) in terms:
            for b in bufs:
                b.note_read(tok)
        obuf.note_write(tok)
        return tok

    def barrier(self):
        toks = []
        for e in self.engs:
            if e.n > 0:
                toks.append((e.sem, e.n, e.name))
        for s in self.slots:
            if s.val > 0:
                toks.append((s.sem, s.val, s.key))
        for e in self.engs:
            e.wait(toks)


def phase_convert(kb, jobs):
    nc = kb.nc
    CH = 2048
    with ExitStack() as st:
        NB = 3
        tin = [st.enter_context(nc.sbuf_tensor(kb.name("cvi"), [128, CH], F32)) for _ in range(NB)]
        tout = [st.enter_context(nc.sbuf_tensor(kb.name("cvo"), [128, CH], BF16)) for _ in range(NB)]
        bin_ = [Buf() for _ in range(NB)]
        bout = [Buf() for _ in range(NB)]
        sin = [kb.slot(kb.name("cvin")) for _ in range(NB)]
        sout = [kb.slot(kb.name("cvout")) for _ in range(NB)]
        i = 0
        for (src, dst, dbuf) in jobs:
            R, Fd = src.shape[0], src.shape[1]
            for c0 in range(0, Fd, CH):
                w = min(CH, Fd - c0)
                k = i % NB
                kb.dma(kb.sp, sin[k], tin[k][0:R, 0:w], src[:, c0:c0 + w], wr=[bin_[k]])
                sel = i % 3
                if sel == 0:
                    kb.op(kb.dve, lambda: nc.vector.tensor_copy(tout[k][0:R, 0:w], tin[k][0:R, 0:w]),
                          rd=[bin_[k]], wr=[bout[k]])
                elif sel == 1:
                    kb.op(kb.pool, lambda: nc.gpsimd.tensor_copy(tout[k][0:R, 0:w], tin[k][0:R, 0:w]),
                          rd=[bin_[k]], wr=[bout[k]])
                else:
                    kb.op(kb.act, lambda: nc.scalar.copy(tout[k][0:R, 0:w], tin[k][0:R, 0:w]),
                          rd=[bin_[k]], wr=[bout[k]])
                kb.dma(kb.pool, sout[k], dst[:, c0:c0 + w], tout[k][0:R, 0:w], rd=[bout[k]], wr=[dbuf])
                i += 1
        kb.barrier()


class BgConv:
    def __init__(self, kb, stack, jobs, CH=512, NB=2):
        nc = kb.nc
        self.kb = kb
        self.NB = NB
        self.tin = [stack.enter_context(nc.sbuf_tensor(kb.name("bgi"), [128, CH], F32)) for _ in range(NB)]
        self.tout = [stack.enter_context(nc.sbuf_tensor(kb.name("bgo"), [128, CH], BF16)) for _ in range(NB)]
        self.bin = [Buf() for _ in range(NB)]
        self.bout = [Buf() for _ in range(NB)]
        self.sin = [kb.slot(kb.name("bgin")) for _ in range(NB)]
        self.sout = [kb.slot(kb.name("bgout")) for _ in range(NB)]
        self.tiles = []
        for (src, dst, dbuf) in jobs:
            R, Fd = src.shape[0], src.shape[1]
            for c0 in range(0, Fd, CH):
                w = min(CH, Fd - c0)
                self.tiles.append((src[:, c0:c0 + w], dst[:, c0:c0 + w], R, w, dbuf))
        self.pos = 0

    def emit(self, n):
        kb = self.kb
        nc = kb.nc
        for _ in range(n):
            if self.pos >= len(self.tiles):
                return
            src, dst, R, w, dbuf = self.tiles[self.pos]
            k = self.pos % self.NB
            self.pos += 1
            kb.dma(kb.sp, self.sin[k], self.tin[k][0:R, 0:w], src, wr=[self.bin[k]])
            kb.op(kb.pool, lambda: nc.gpsimd.tensor_copy(self.tout[k][0:R, 0:w], self.tin[k][0:R, 0:w]),
                  rd=[self.bin[k]], wr=[self.bout[k]])
            kb.dma(kb.pool, self.sout[k], dst, self.tout[k][0:R, 0:w], rd=[self.bout[k]], wr=[dbuf])

    def flush(self):
        self.emit(len(self.tiles))


def norm_block(kb, xts, xbufs, grow, gbuf, hns, hnbufs, sq, sqbuf, st, stbuf):
    nc = kb.nc
    n = len(xts)
    for i in range(n):
        kb.op(kb.act, lambda: nc.scalar.activation(out=sq, in_=xts[i], func=AF.Square, accum_out=st[:, i:i + 1]),
              rd=[xbufs[i]], wr=[sqbuf, stbuf])
    kb.op(kb.dve, lambda: nc.vector.tensor_scalar(st[:, 4:4 + n], st[:, 0:n], 1.0 / D, EPS, ALU.mult, ALU.add),
          rd=[stbuf], wr=[stbuf])
    kb.op(kb.act, lambda: nc.scalar.activation(out=st[:, 8:8 + n], in_=st[:, 4:4 + n], func=AF.Sqrt),
          rd=[stbuf], wr=[stbuf])
    kb.op(kb.dve, lambda: nc.vector.reciprocal(st[:, 4:4 + n], st[:, 8:8 + n]), rd=[stbuf], wr=[stbuf])
    for i in range(n):
        kb.op(kb.dve, lambda: nc.vector.scalar_tensor_tensor(out=hns[i], in0=xts[i], scalar=st[:, 4 + i:5 + i],
                                                             in1=grow, op0=ALU.mult, op1=ALU.mult),
              rd=[xbufs[i], stbuf, gbuf], wr=[hnbufs[i]])


def transpose_tile(kb, C, hn, hnbuf, pst, pstbuf, dst_ap, dstbuf, evac_eng):
    nc = kb.nc
    pe = kb.pe
    pe.wait(pstbuf.wr_deps(), hnbuf.w)
    ins = None
    for kc in range(8):
        ins = nc.tensor.transpose(pst[:, kc, :], hn[:, kc * 128:(kc + 1) * 128], C["ident"])
    tok = pe.done(ins)
    hnbuf.note_read(tok)
    pstbuf.note_write(tok)
    if evac_eng is kb.act:
        kb.op(kb.act, lambda: nc.scalar.copy(dst_ap, pst[:, :, :]), rd=[pstbuf], wr=[dstbuf])
    else:
        kb.op(kb.dve, lambda: nc.vector.tensor_copy(dst_ap, pst[:, :, :]), rd=[pstbuf], wr=[dstbuf])


def phase_ffn(kb, C, layer, xin, xin_bufs, xout, xout_bufs, wup_d, wup_buf, wdn_d, wdn_buf,
              cwb_d, grow_d, final_grow_d=None):
    nc = kb.nc
    with ExitStack() as st:
        sb = lambda n, shp, dt: st.enter_context(nc.sbuf_tensor(kb.name(n), shp, dt))
        ps = lambda n, shp, dt: st.enter_context(nc.psum_tensor(kb.name(n), shp, dt))
        wdn = sb("wdn", [128, NPAIR, 1024], BF16)
        wdn_b = Buf()
        cw = sb("cw", [128, 4, 44], F32)
        cw_b = Buf()
        grow = sb("grow", [128, 1024], F32)
        grow_b = Buf()
        NW = 4
        wup = [sb("wup", [128, 8, 256], BF16) for _ in range(NW)]
        wup_b = [Buf() for _ in range(NW)]
        wup_s = [kb.slot(kb.name("wup")) for _ in range(NW)]
        hnT = [sb("hnT", [128, 8, 512], BF16) for _ in range(2)]
        hnT_b = [[Buf() for _ in range(4)] for _ in range(2)]
        G2 = [sb("G", [128, NPAIR, 512], BF16) for _ in range(2)]
        G2_b = [[Buf() for _ in range(NPAIR)] for _ in range(2)]
        T0 = [sb("T0", [128, 512], F32) for _ in range(4)]
        T0_b = [Buf() for _ in range(4)]
        xt = [sb("xt", [128, 1024], F32) for _ in range(4)]
        xt_b = [Buf() for _ in range(4)]
        xt_s = [kb.slot(kb.name("xt")) for _ in range(4)]
        hn = [sb("hn", [128, 1024], BF16) for _ in range(4)]
        hn_b = [Buf() for _ in range(4)]
        sq = sb("sq", [128, 1024], BF16)
        sq_b = Buf()
        st2 = sb("st2", [128, 12], F32)
        st2_b = Buf()
        U = [sb("U", [128, 514], F32) for _ in range(4)]
        U_b = [Buf() for _ in range(4)]
        Cc = [sb("Cc", [128, 512], F32) for _ in range(4)]
        Cc_b = [Buf() for _ in range(4)]
        Sg = [sb("Sg", [128, 512], F32) for _ in range(2)]
        Sg_b = [Buf() for _ in range(2)]
        halo = sb("halo", [128, 44, 2], F32)
        halo_b = [Buf() for _ in range(44)]
        xr = [sb("xr", [128, 1024], F32) for _ in range(2)]
        xr_b = [Buf() for _ in range(2)]
        xr_s = [kb.slot(kb.name("xr")) for _ in range(2)]
        NXO = 2 if final_grow_d is None else 4
        xo = [sb("xo", [128, 1024], F32) for _ in range(NXO)]
        xo_b = [Buf() for _ in range(NXO)]
        xo_s = [kb.slot(kb.name("xo")) for _ in range(4)]
        if final_grow_d is not None:
            fgrow = sb("fgrow", [128, 1024], F32)
            fgrow_b = Buf()
            fst = sb("fst", [128, 12], F32)
            fst_b = Buf()
        pst = ps("pst", [128, 8, 128], BF16)
        pst_b = Buf()
        pu = [ps("pu", [128, 512], F32) for _ in range(4)]
        pu_b = [Buf() for _ in range(4)]
        po = [ps("po", [128, 512], F32) for _ in range(2)]
        po_b = [Buf() for _ in range(2)]

        s_misc = kb.slot(kb.name("misc"))
        for c0_, c1_ in ((0, 6), (6, 11), (11, 17), (17, 22)):
            kb.dma(kb.pool, s_misc, wdn[:, c0_:c1_, :], wdn_d[c0_:c1_].rearrange("c p n -> p c n"), rd=[wdn_buf],
                   wr=[wdn_b])
        kb.dma(kb.pool, s_misc, cw[:, :, :], cwb_d.rearrange("p (j c) -> p j c", j=4), wr=[cw_b])
        kb.dma(kb.pool, s_misc, grow[:, :], grow_d, wr=[grow_b])
        if final_grow_d is not None:
            kb.dma(kb.pool, s_misc, fgrow[:, :], final_grow_d, wr=[fgrow_b])
            kb.batch_end(s_misc, [fgrow_b])
        kb.batch_end(s_misc, [wdn_b, cw_b, grow_b])
        kb.op(kb.dve, lambda: nc.vector.memset(halo[:, :, :], 0.0), wr=halo_b)

        wi = 0
        ui = 0
        oi = 0
        NBLK = S // 512

        def norm_stage(tb):
            hb = tb % 2
            for t4 in range(4):
                tt = tb * 4 + t4
                kb.dma(kb.sp, xt_s[t4], xt[t4][:, :], xin[tt * 128:(tt + 1) * 128, :], rd=[xin_bufs[tt]],
                       wr=[xt_b[t4]])
            norm_block(kb, [xt[i][:, :] for i in range(4)], xt_b, grow[:, :], grow_b,
                       [hn[i][:, :] for i in range(4)], hn_b, sq[:, :], sq_b, st2[:, :], st2_b)
            for t4 in range(4):
                transpose_tile(kb, C, hn[t4], hn_b[t4], pst, pst_b, hnT[hb][:, :, t4 * 128:(t4 + 1) * 128],
                               hnT_b[hb][t4], kb.act)

        def down_group(tb, gi):
            nonlocal oi
            t4, nh = divmod(gi, 2)
            tt = tb * 4 + t4
            Gd = G2[tb % 2]
            Gd_b = G2_b[tb % 2]
            k = t4 % 2
            ko = t4 % NXO
            if nh == 0:
                kb.dma(kb.sp, xr_s[k], xr[k][:, :], xin[tt * 128:(tt + 1) * 128, :], rd=[xin_bufs[tt]], wr=[xr_b[k]])
            terms = [(Gd[:, fc, t4 * 128:(t4 + 1) * 128], wdn[:, fc, nh * 512:(nh + 1) * 512], [Gd_b[fc], wdn_b])
                     for fc in range(NPAIR)]
            kb.mm_group(po[nh][:, :], po_b[nh], terms)
            kb.op(kb.dve, lambda: nc.vector.tensor_tensor(xo[ko][:, nh * 512:(nh + 1) * 512], po[nh][:, :],
                                                          xr[k][:, nh * 512:(nh + 1) * 512], ALU.add),
                  rd=[po_b[nh], xr_b[k]], wr=[xo_b[ko]])
            if nh == 1 and final_grow_d is None:
                kb.dma(kb.pool, xo_s[ko], xout[tt * 128:(tt + 1) * 128, :], xo[ko][:, :], rd=[xo_b[ko]],
                       wr=[xout_bufs[tt]])
            if gi == 7 and final_grow_d is not None:
                norm_block(kb, [xo[i][:, :] for i in range(4)], xo_b, fgrow[:, :], fgrow_b,
                           [xo[i][:, :] for i in range(4)], xo_b, sq[:, :], sq_b, fst[:, :], fst_b)
                for t4_ in range(4):
                    tt_ = tb * 4 + t4_
                    kb.dma(kb.pool, xo_s[t4_], xout[tt_ * 128:(tt_ + 1) * 128, :], xo[t4_][:, :], rd=[xo_b[t4_]],
                           wr=[xout_bufs[tt_]])

        norm_stage(0)
        for tb in range(NBLK):
            hb = tb % 2
            Gw = G2[tb % 2]
            Gw_b = G2_b[tb % 2]
            for j in range(NPAIR):
                k = wi % NW
                wi += 1
                kb.dma(kb.sp, wup_s[k], wup[k][:, :, :], wup_d[j].rearrange("p (kc n) -> p kc n", kc=8),
                       rd=[wup_buf], wr=[wup_b[k]])
                cs = []
                for gv in range(2):
                    u = ui % 4
                    ui += 1
                    ch = gv * NPAIR + j
                    terms = [(wup[k][:, kc, gv * 128:(gv + 1) * 128], hnT[hb][:, kc, :], [wup_b[k]] + hnT_b[hb])
                             for kc in range(8)]
                    kb.mm_group(pu[u][:, :], pu_b[u], terms)
                    kb.op(kb.act, lambda: nc.scalar.activation(out=Cc[u][:, :], in_=pu[u][:, :], func=AF.Identity,
                                                               bias=cw[:, 3, ch:ch + 1], scale=cw[:, 2, ch:ch + 1]),
                          rd=[pu_b[u], cw_b], wr=[Cc_b[u]])
                    kb.op(kb.act, lambda: nc.scalar.copy(U[u][:, 2:514], pu[u][:, :]), rd=[pu_b[u]], wr=[U_b[u]])
                    kb.op(kb.pool, lambda: nc.gpsimd.tensor_copy(U[u][:, 0:2], halo[:, ch, :]),
                          rd=[halo_b[ch]], wr=[U_b[u]])
                    kb.op(kb.pool, lambda: nc.gpsimd.tensor_copy(halo[:, ch, :], U[u][:, 512:514]),
                          rd=[U_b[u]], wr=[halo_b[ch]])
                    kb.op(kb.pool, lambda: nc.gpsimd.tensor_scalar(T0[u][:, :], U[u][:, 0:512], cw[:, 0, ch:ch + 1], 0.0,
                                                                   ALU.mult, ALU.add),
                          rd=[U_b[u], cw_b], wr=[T0_b[u]])
                    kb.op(kb.dve, lambda: nc.vector.scalar_tensor_tensor(out=Cc[u][:, :], in0=U[u][:, 1:513],
                                                                         scalar=cw[:, 1, ch:ch + 1], in1=Cc[u][:, :],
                                                                         op0=ALU.mult, op1=ALU.add),
                          rd=[U_b[u], cw_b], wr=[Cc_b[u]])
                    kb.op(kb.dve, lambda: nc.vector.tensor_tensor(Cc[u][:, :], Cc[u][:, :], T0[u][:, :], ALU.add),
                          rd=[T0_b[u]], wr=[Cc_b[u]])
                    cs.append(u)
                ug, uv = cs
                sgi = j % 2
                kb.op(kb.act, lambda: nc.scalar.activation(out=Sg[sgi][:, :], in_=Cc[ug][:, :], func=AF.Silu),
                      rd=[Cc_b[ug]], wr=[Sg_b[sgi]])
                kb.op(kb.dve, lambda: nc.vector.tensor_tensor(Gw[:, j, :], Sg[sgi][:, :], Cc[uv][:, :], ALU.mult),
                      rd=[Sg_b[sgi], Cc_b[uv]], wr=[Gw_b[j]])
                if tb > 0 and j % 3 == 0 and j // 3 < 8:
                    down_group(tb - 1, j // 3)
            if tb + 1 < NBLK:
                norm_stage(tb + 1)
        for gi in range(8):
            down_group(NBLK - 1, gi)
        kb.barrier()


def host_wup(w):
    a = np.asarray(w, np.float32).reshape(8, 128, 2, NPAIR, 128)
    a = a.transpose(3, 1, 0, 2, 4)
    return np.ascontiguousarray(a).reshape(NPAIR * 128, 2048)


def host_cwb(cw, cb):
    a = np.concatenate([np.asarray(cw, np.float32), np.asarray(cb, np.float32)[None, :]], axis=0)
    a = a.reshape(4, 44, 128).transpose(2, 0, 1)
    return np.ascontiguousarray(a).reshape(128, 4 * 44)


def host_row(g):
    return np.ascontiguousarray(np.broadcast_to(np.asarray(g, np.float32)[None, :], (128, D)))


def host_chunks(w):
    w = np.asarray(w, np.float32)
    n = w.shape[1] // 128
    a = w.reshape(8, 128, n, 128).transpose(2, 1, 0, 3)
    return np.ascontiguousarray(a).reshape(n * 128, 1024)


def host_rows(w):
    w = np.asarray(w, np.float32)
    kc = w.shape[0] // 128
    a = w.reshape(kc, 128, w.shape[1]).transpose(1, 0, 2)
    return np.ascontiguousarray(a).reshape(128, kc * w.shape[1])


def consts_host():
    bf = ml_dtypes.bfloat16
    c = {}
    c["ident"] = np.eye(128, dtype=np.float32).astype(bf)
    c["ident32"] = np.eye(128, dtype=np.float32)
    p = np.arange(128)[:, None]
    f = np.arange(512)[None, :]
    nm = np.zeros((128, 4, 512), np.float32)
    le = np.zeros((128, 4, 512), np.float32)
    wn = np.zeros((128, 4, 512), np.float32)
    for j in range(4):
        nm[:, j, :] = np.where(f <= 128 * j + p, NEGM, 0.0)
        le[:, j, :] = np.where(128 * j + p > f, NEGM, 0.0)
        wn[:, j, :] = np.where(f >= 128 * j + p, NEGM, 0.0)
    c["negmask"] = nm.astype(bf)
    c["negmask_le"] = le.astype(bf)
    c["negmask_win"] = wn.astype(bf)
    cm = np.zeros((128, 5, 512), np.float32)
    for u in range(5):
        cm[:, u, :] = np.where(16 * p + 31 > 512 * u + f, NEGM, 0.0)
    c["negmask_cmp"] = cm.astype(bf)
    x = np.arange(512)[None, :]
    c["cmask_tm"] = np.where(16 * (x - 248) + 31 > p, NEGM, 0.0).astype(np.float32).astype(bf)
    jj = np.arange(128)[:, None]
    ss = np.arange(128)[None, :]
    c["uincneg"] = np.where(jj >= ss, -1.0, 0.0).astype(np.float32).astype(bf)
    os_ = np.zeros((128, 2, 128), np.float32)
    sl = np.zeros((128, 2, 128), np.float32)
    for hh in range(2):
        os_[:, hh, hh] = 1.0
        os_[:, hh, 32 + hh] = 1.0
        sl[hh, hh, :] = 1.0
        sl[32 + hh, hh, :] = 1.0
    c["onesel"] = os_.astype(bf)
    c["sel"] = sl.astype(bf)
    cc = np.arange(256)[:, None] * 16
    s0 = np.arange(64)[None, :] * 64
    ov = np.clip(np.minimum(cc + 32, s0 + 64) - np.maximum(cc, s0), 0, None) / 32.0
    ov[255, :] = 0.0
    c["overlap"] = np.ascontiguousarray(ov.reshape(2, 128, 64).transpose(1, 0, 2)).astype(np.float32)
    bon = np.zeros((128, 192), np.float32)
    y = np.arange(128)[None, :]
    npr = y - 62
    cur = p // 64
    bon[:, 0:128] = np.where(npr > cur, -1.0e9, np.where((npr == cur) | (npr == cur - 1), 1.0e4, 0.0))
    bon[:, 128] = 1.0e4
    c["bonus"] = bon
    o64 = np.zeros((128, 128), np.float32)
    o64[64, :] = 1.0
    c["onesrow64"] = o64.astype(bf)
    gs = np.zeros((128, 48, 128), np.float32)
    for r in range(48):
        gs[r, r, :] = 1.0
    c["gsel"] = gs.astype(bf)
    half = 32
    inv = (10000.0 ** (-np.arange(half, dtype=np.float32) / half)).astype(np.float32)
    rc = np.zeros((128, 2), np.float32)
    rc[:, 0] = inv[np.arange(128) % 32]
    rc[:, 1] = np.where((np.arange(128) % 64) < 32, -1.0, 1.0)
    c["ropec"] = rc
    ef = (np.arange(S)[None, :] // 64 == np.arange(64)[:, None]).astype(np.float32)
    c["efull"] = ef.astype(bf)
    return c


CONST_SHAPES = {"ident": ([128, 128], BF16), "ident32": ([128, 128], F32), "negmask": ([128, 4, 512], BF16),
                "negmask_le": ([128, 4, 512], BF16), "negmask_win": ([128, 4, 512], BF16),
                "negmask_cmp": ([128, 5, 512], BF16), "cmask_tm": ([128, 512], BF16),
                "uincneg": ([128, 128], BF16), "onesel": ([128, 2, 128], BF16), "sel": ([128, 2, 128], BF16),
                "overlap": ([128, 2, 64], F32), "bonus": ([128, 192], F32), "onesrow64": ([128, 128], BF16),
                "gsel": ([128, 48, 128], BF16), "ropec": ([128, 2], F32)}


SBA_CONSTS = ("ident", "negmask", "uincneg", "onesel", "sel")
NSA_CONSTS = ("ident", "ident32", "negmask_le", "negmask_win", "negmask_cmp", "cmask_tm", "overlap", "bonus", "onesrow64",
              "gsel", "ropec")
FFN_CONSTS = ("ident",)


def load_consts(kb, names=None, stack=None):
    nc = kb.nc
    if not hasattr(kb, "cdram"):
        kb.cdram = {}
        for n, (shp, dt) in CONST_SHAPES.items():
            kb.cdram[n] = nc.dram_tensor("c_" + n, shp, dt, kind="ExternalInput").ap()
        kb.cdram["efull_d"] = nc.dram_tensor("c_efull", [64, S], BF16, kind="ExternalInput").ap()
    stack = stack if stack is not None else kb.root
    C = {}
    sl = kb.slot(kb.name("consts"))
    for n, (shp, dt) in CONST_SHAPES.items():
        if names is not None and n not in names:
            continue
        d = kb.cdram[n]
        t = stack.enter_context(nc.sbuf_tensor(kb.name("cs_" + n), shp, dt))
        if len(shp) == 2:
            kb.dma(kb.sp, sl, t[:, :], d)
            C[n] = t[:, :]
        else:
            kb.dma(kb.sp, sl, t[:, :, :], d)
            C[n] = t[:, :, :]
    C["efull_d"] = kb.cdram["efull_d"]
    kb.barrier()
    return C


def stage_norm_all(kb, C, xin, xin_bufs, grow_d, hnT, hnT_b):
    nc = kb.nc
    with ExitStack() as st:
        sb = lambda n, shp, dt: st.enter_context(nc.sbuf_tensor(kb.name(n), shp, dt))
        ps = lambda n, shp, dt: st.enter_context(nc.psum_tensor(kb.name(n), shp, dt))
        grow = sb("grow", [128, 1024], F32)
        grow_b = Buf()
        xt = [sb("xt", [128, 1024], F32) for _ in range(8)]
        xt_b = [Buf() for _ in range(8)]
        xt_s = [kb.slot(kb.name("xt")) for _ in range(8)]
        hn = [sb("hn", [128, 1024], BF16) for _ in range(4)]
        hn_b = [Buf() for _ in range(4)]
        sq = sb("sq", [128, 1024], BF16)
        sq_b = Buf()
        st2 = [sb("st2", [128, 12], F32) for _ in range(2)]
        st2_b = [Buf() for _ in range(2)]
        pst = [ps("pst", [128, 8, 128], BF16) for _ in range(2)]
        pst_b = [Buf() for _ in range(2)]
        s_misc = kb.slot(kb.name("misc"))
        kb.dma(kb.pool, s_misc, grow[:, :], grow_d, wr=[grow_b])
        for tb in range(NT // 4):
            o = (tb % 2) * 4
            for t4 in range(4):
                tt = tb * 4 + t4
                kb.dma(kb.sp, xt_s[o + t4], xt[o + t4][:, :], xin[tt * 128:(tt + 1) * 128, :], rd=[xin_bufs[tt]],
                       wr=[xt_b[o + t4]])
            norm_block(kb, [xt[o + i][:, :] for i in range(4)], xt_b[o:o + 4], grow[:, :], grow_b,
                       [hn[i][:, :] for i in range(4)], hn_b[0:4], sq[:, :], sq_b, st2[tb % 2][:, :],
                       st2_b[tb % 2])
            for t4 in range(4):
                tt = tb * 4 + t4
                transpose_tile(kb, C, hn[t4], hn_b[t4], pst[tt % 2], pst_b[tt % 2],
                               hnT[:, :, tt * 128:(tt + 1) * 128], hnT_b[tt],
                               kb.act if tt % 2 == 0 else kb.dve)
        kb.barrier()


def stage_outproj(kb, C, oT, oT_b, wo_d, wo_buf, xin, xin_bufs, xout, xout_bufs):
    nc = kb.nc
    with ExitStack() as st:
        sb = lambda n, shp, dt: st.enter_context(nc.sbuf_tensor(kb.name(n), shp, dt))
        ps = lambda n, shp, dt: st.enter_context(nc.psum_tensor(kb.name(n), shp, dt))
        wo = sb("wo", [128, 8, 1024], BF16)
        wo_b = Buf()
        xr = [sb("xr", [128, 1024], F32) for _ in range(3)]
        xr_b = [Buf() for _ in range(3)]
        xr_s = [kb.slot(kb.name("xr")) for _ in range(3)]
        xo = [sb("xo", [128, 1024], F32) for _ in range(3)]
        xo_b = [Buf() for _ in range(3)]
        xo_s = [kb.slot(kb.name("xo")) for _ in range(3)]
        po = [ps("po", [128, 512], F32) for _ in range(4)]
        po_b = [Buf() for _ in range(4)]
        s_misc = kb.slot(kb.name("misc"))
        wo_v = wo_d.rearrange("p (c n) -> p c n", c=8)
        kb.dma(kb.pool, s_misc, wo[:, 0:4, :], wo_v[:, 0:4, :], rd=[wo_buf], wr=[wo_b])
        kb.dma(kb.pool, s_misc, wo[:, 4:8, :], wo_v[:, 4:8, :], rd=[wo_buf], wr=[wo_b])
        kb.batch_end(s_misc, [wo_b])
        for tt in range(NT):
            k = tt % 3
            kb.dma(kb.sp, xr_s[k], xr[k][:, :], xin[tt * 128:(tt + 1) * 128, :], rd=[xin_bufs[tt]], wr=[xr_b[k]])
            for nh in range(2):
                pi = (tt * 2 + nh) % 4
                terms = [(oT[:, c, tt * 128:(tt + 1) * 128], wo[:, c, nh * 512:(nh + 1) * 512], [oT_b[tt // 4], wo_b])
                         for c in range(8)]
                kb.mm_group(po[pi][:, :], po_b[pi], terms)
                kb.op(kb.dve, lambda: nc.vector.tensor_tensor(xo[k][:, nh * 512:(nh + 1) * 512], po[pi][:, :],
                                                              xr[k][:, nh * 512:(nh + 1) * 512], ALU.add),
                      rd=[po_b[pi], xr_b[k]], wr=[xo_b[k]])
            kb.dma(kb.pool, xo_s[k], xout[tt * 128:(tt + 1) * 128, :], xo[k][:, :], rd=[xo_b[k]], wr=[xout_bufs[tt]])
        kb.barrier()


def phase_sba(kb, C, xin, xin_bufs, xout, xout_bufs, grow_d, wqk_d, wqk_buf, wv_d, wv_buf, wo_d, wo_buf,
              ngroups=8, nq=8, bg=None):
    nc = kb.nc
    with ExitStack() as st0:
        sb0 = lambda n, shp, dt: st0.enter_context(nc.sbuf_tensor(kb.name(n), shp, dt))
        oT = sb0("oT", [128, 8, S], BF16)
        oT_b = [Buf() for _ in range(8)]
        if ngroups < 8 or nq < 8:
            kb.op(kb.pool, lambda: nc.gpsimd.memset(oT[:, :, :], 0.0), wr=oT_b)
        with ExitStack() as st1:
            sb1 = lambda n, shp, dt: st1.enter_context(nc.sbuf_tensor(kb.name(n), shp, dt))
            hnT = sb1("hnT", [128, 8, S], BF16)
            hnT_b = [Buf() for _ in range(NT)]
            stage_norm_all(kb, C, xin, xin_bufs, grow_d, hnT, hnT_b)
            with ExitStack() as st:
                sb = lambda n, shp, dt: st.enter_context(nc.sbuf_tensor(kb.name(n), shp, dt))
                ps = lambda n, shp, dt: st.enter_context(nc.psum_tensor(kb.name(n), shp, dt))
                wg = [sb("wg", [128, 3, 8, 128], BF16) for _ in range(1)]
                wg_b = [Buf() for _ in range(1)]
                wg_s = [kb.slot(kb.name("wg")) for _ in range(1)]
                qz = sb("qz", [128, 2, S], BF16)
                kT = sb("kT", [128, S], BF16)
                Vt = sb("Vt", [128, NT, 2, 128], BF16)
                qz_b = [Buf() for _ in range(8)]
                kT_b = [Buf() for _ in range(8)]
                Vt_b = [Buf() for _ in range(NT)]
                NB3 = 3
                E = [sb("E", [128, 2, 512], F32) for _ in range(2)]
                E_b = [Buf() for _ in range(2)]
                SP = [sb("SP", [128, 2, 512], BF16) for _ in range(2)]
                SP_b = [Buf() for _ in range(2)]
                A = [sb("A", [128, 2, 512], BF16) for _ in range(2)]
                A_b = [Buf() for _ in range(2)]
                R34 = sb("R34", [34, 512], F32)
                R34_b = Buf()
                RHL = [sb("RHL", [128, 512], BF16) for _ in range(2)]
                RHL_b = [Buf() for _ in range(2)]
                pz = [ps("pz", [128, 2, 512], F32) for _ in range(NB3)]
                pz_b = [Buf() for _ in range(NB3)]
                pr = ps("pr", [128, 512], F32)
                pr_b = Buf()
                po = ps("po", [128, 512], F32)
                po_b = Buf()
                pq = [pz[0][:, 0, :], pz[0][:, 1, :], pz[1][:, 0, :], pz[1][:, 1, :]]
                pq_b = [pz_b[0], pz_b[0], pz_b[1], pz_b[1]]

                def load_w(c):
                    k = 0
                    kb.dma(kb.sp, wg_s[k], wg[k][:, 0, :, :], wqk_d[c].rearrange("p (kc n) -> p kc n", kc=8),
                           rd=[wqk_buf], wr=[wg_b[k]])
                    kb.dma(kb.sp, wg_s[k], wg[k][:, 1, :, :], wqk_d[8 + c].rearrange("p (kc n) -> p kc n", kc=8),
                           rd=[wqk_buf], wr=[wg_b[k]])
                    wv_v = wv_d.rearrange("p (kc n) -> p kc n", kc=8)
                    for k0_ in range(0, 8, 2):
                        kb.dma(kb.sp, wg_s[k], wg[k][:, 2, k0_:k0_ + 2, :], wv_v[:, k0_:k0_ + 2, c * 128:(c + 1) * 128],
                               rd=[wv_buf], wr=[wg_b[k]])

                kb.op(kb.pool, lambda: nc.gpsimd.memset(Vt[:, :, :, :], 0.0), wr=Vt_b)
                kb.op(kb.pool, lambda: nc.gpsimd.memset(qz[:, :, :], 0.0), wr=qz_b)
                for i_ in range(2):
                    kb.op(kb.pool, lambda: nc.gpsimd.memset(RHL[i_][:, :], 0.0), wr=[RHL_b[i_]])
                load_w(0)
                qi = 0
                for c in range(ngroups):
                    k = 0
                    for which in (0, 1):
                        for tg in range(8):
                            p = qi % 4
                            qi += 1
                            ts_ = slice(tg * 512, (tg + 1) * 512)
                            terms = [(wg[k][:, which, kc, :], hnT[:, kc, ts_],
                                      [wg_b[k]] + hnT_b[tg * 4:(tg + 1) * 4]) for kc in range(8)]
                            kb.mm_group(pq[p], pq_b[p], terms)
                            if which == 0:
                                kb.op(kb.act, lambda: nc.scalar.mul(qz[0:64, 0, ts_], pq[p][0:64, :], 0.125),
                                      rd=[pq_b[p]], wr=[qz_b[tg]])
                                kb.op(kb.act, lambda: nc.scalar.mul(qz[64:128, 1, ts_], pq[p][64:128, :], 0.125),
                                      rd=[pq_b[p]], wr=[qz_b[tg]])
                            else:
                                kb.op(kb.dve, lambda: nc.vector.tensor_copy(kT[:, ts_], pq[p]),
                                      rd=[pq_b[p]], wr=[kT_b[tg]])
                    for tt4 in range(NT // 4):
                        p = qi % 4
                        qi += 1
                        pe = kb.pe
                        pe.wait(pq_b[p].wr_deps(), wg_b[k].w, [hnT_b[tt4 * 4 + i].w for i in range(4)])
                        ins = None
                        for i in range(4):
                            tt = tt4 * 4 + i
                            for kc in range(8):
                                ins = nc.tensor.matmul(pq[p][:, i * 128:(i + 1) * 128], hnT[:, kc, tt * 128:(tt + 1) * 128],
                                                       wg[k][:, 2, kc, :], start=(kc == 0), stop=(kc == 7))
                        tok = pe.done(ins)
                        wg_b[k].note_read(tok)
                        pq_b[p].note_write(tok)
                        src = pq[p].rearrange("p (i n) -> p i n", i=4)
                        kb.op(kb.act, lambda: nc.scalar.copy(Vt[:, tt4 * 4:(tt4 + 1) * 4, 0, 0:64], src[:, :, 0:64]),
                              rd=[pq_b[p]], wr=Vt_b[tt4 * 4:(tt4 + 1) * 4])
                        kb.op(kb.act, lambda: nc.scalar.copy(Vt[:, tt4 * 4:(tt4 + 1) * 4, 1, 64:128], src[:, :, 64:128]),
                              rd=[pq_b[p]], wr=Vt_b[tt4 * 4:(tt4 + 1) * 4])
                    if c + 1 < ngroups:
                        load_w(c + 1)
                    for g in range(nq):
                        qs = slice(g * 512, (g + 1) * 512)
                        nsteps = 4 * g + 4
                        kts = [4 * g + 3 - i for i in range(nsteps)]
                        kb.op(kb.dve, lambda: nc.vector.memset(R34[:, :], 0.0), wr=[R34_b])
                        kb.op(kb.dve, lambda: nc.vector.memset(RHL[0][0:34, :], 0.0), wr=[RHL_b[0]])

                        def emit_Z(i):
                            kt = kts[i]
                            ks = slice(kt * 128, (kt + 1) * 128)
                            b3 = i % NB3
                            pe = kb.pe
                            pe.wait(pz_b[b3].wr_deps(), kT_b[kt // 4].w, qz_b[g].w)
                            ins = None
                            diag = kt >= 4 * g
                            for hh in range(2):
                                ins = nc.tensor.matmul(pz[b3][:, hh, :], kT[:, ks], qz[:, hh, qs], start=True,
                                                       stop=not diag)
                                if diag:
                                    ins = nc.tensor.matmul(pz[b3][:, hh, :], C["ident"], C["negmask"][:, kt - 4 * g, :],
                                                           start=False, stop=True)
                            tok = pe.done(ins)
                            kT_b[kt // 4].note_read(tok)
                            qz_b[g].note_read(tok)
                            pz_b[b3].note_write(tok)
                            e = i % 2
                            kb.op(kb.act, lambda: nc.scalar.activation(out=E[e][:, :, :], in_=pz[b3][:, :, :], func=AF.Exp),
                                  rd=[pz_b[b3]], wr=[E_b[e]])
                            kb.op(kb.act, lambda: nc.scalar.activation(out=SP[e][:, :, :], in_=E[e][:, :, :], func=AF.Ln,
                                                                       bias=1.0),
                                  rd=[E_b[e]], wr=[SP_b[e]])

                        def emit_R(i):
                            b3 = i % 2
                            terms = [(C["onesel"][:, hh, :], SP[b3][:, hh, :], [SP_b[b3]]) for hh in range(2)]
                            kb.mm_group(pr[:, :], pr_b, terms)
                            kb.op(kb.dve, lambda: nc.vector.tensor_tensor(R34[:, :], R34[:, :], pr[0:34, :], ALU.subtract),
                                  rd=[pr_b], wr=[R34_b])
                            nx = RHL[(i + 1) % 2]
                            nx_b = RHL_b[(i + 1) % 2]
                            kb.op(kb.dve, lambda: nc.vector.tensor_copy(nx[0:34, :], R34[:, :]), rd=[R34_b], wr=[nx_b])
                            kb.op(kb.dve, lambda: nc.vector.tensor_tensor(nx[32:34, :], R34[32:34, :], nx[32:34, :],
                                                                          ALU.subtract),
                                  rd=[R34_b], wr=[nx_b])

                        def emit_C(i):
                            b3 = i % NB3
                            b2 = i % 2
                            pe = kb.pe
                            pe.wait(pz_b[b3].wr_deps(), SP_b[b2].w, RHL_b[i % 2].w)
                            ins = None
                            for hh in range(2):
                                nc.tensor.matmul(pz[b3][:, hh, :], C["uincneg"], SP[b2][:, hh, :], start=False, stop=False,
                                                 skip_group_check=True)
                                ins = nc.tensor.matmul(pz[b3][:, hh, :], C["sel"][:, hh, :], RHL[i % 2][:, :],
                                                       start=False, stop=True, skip_group_check=True)
                            tok = pe.done(ins)
                            SP_b[b2].note_read(tok)
                            RHL_b[i % 2].note_read(tok)
                            pz_b[b3].note_write(tok)
                            kb.op(kb.act, lambda: nc.scalar.activation(out=A[b2][:, :, :], in_=pz[b3][:, :, :], func=AF.Exp),
                                  rd=[pz_b[b3]], wr=[A_b[b2]])

                        def emit_AV(i):
                            kt = kts[i]
                            b3 = i % 2
                            pe = kb.pe
                            deps = [A_b[b3].w, Vt_b[kt].w]
                            if i == 0:
                                deps += po_b.wr_deps()
                            pe.wait(deps)
                            ins = None
                            for hh in range(2):
                                ins = nc.tensor.matmul(po[:, :], Vt[:, kt, hh, :], A[b3][:, hh, :],
                                                       start=(i == 0 and hh == 0), stop=(i == nsteps - 1 and hh == 1))
                            tok = pe.done(ins)
                            A_b[b3].note_read(tok)
                            Vt_b[kt].note_read(tok)
                            if i == nsteps - 1:
                                po_b.note_write(tok)

                        lvl = DBG.get("lvl", 9)
                        emit_Z(0)
                        for i in range(nsteps):
                            if i + 1 < nsteps:
                                emit_Z(i + 1)
                                emit_R(i)
                            emit_C(i)
                            if i >= 1:
                                emit_AV(i - 1)
                        emit_AV(nsteps - 1)
                        kb.op(kb.dve, lambda: nc.vector.tensor_copy(oT[:, c, qs], po[:, :]), rd=[po_b], wr=[oT_b[g]])
                        if bg is not None:
                            bg.emit(6)
                if bg is not None:
                    bg.flush()
                kb.barrier()
        stage_outproj(kb, C, oT, oT_b, wo_d, wo_buf, xin, xin_bufs, xout, xout_bufs)


TWO_PI = 6.283185307179586
NCMP = 255


def nsa_stage_proj(kb, C, hnT, hnT_b, posb_d, wfm_d, wfm_buf, wgt_d, wgt_buf, wtm_d, wtm_buf,
                   q_s, q_sb, kv_s, kv_sb, Vtm, Vtm_b, sigT, sigT_b, cmpw, kcmp, kcmp_b, vcmp, vcmp_b):
    nc = kb.nc
    with ExitStack() as st:
        sb = lambda n, shp, dt: st.enter_context(nc.sbuf_tensor(kb.name(n), shp, dt))
        ps = lambda n, shp, dt: st.enter_context(nc.psum_tensor(kb.name(n), shp, dt))
        cosF = sb("cosF", [128, S], F32)
        sinS = sb("sinS", [128, S], F32)
        cs_b = Buf()
        t_b = Buf()
        s_misc = kb.slot(kb.name("misc"))
        s_miscp = kb.slot(kb.name("miscp"))
        st_tmp = ExitStack()
        HS = S // 2
        posi = st_tmp.enter_context(nc.sbuf_tensor(kb.name("posi"), [128, HS], I32))
        ang = st_tmp.enter_context(nc.sbuf_tensor(kb.name("ang"), [128, HS], F32))
        tmp = st_tmp.enter_context(nc.sbuf_tensor(kb.name("tmpang"), [128, HS], F32))
        kf = st_tmp.enter_context(nc.sbuf_tensor(kb.name("kfang"), [128, HS], F32))
        C1 = 6.28125
        C2 = TWO_PI - 6.28125

        def reduce_sin(dst, shift, post_scale):
            if shift != 0.0:
                kb.op(kb.dve, lambda: nc.vector.tensor_scalar(tmp[:, :], ang[:, :], shift, None, ALU.add),
                      rd=[t_b], wr=[t_b])
                src = tmp
            else:
                src = ang
            kb.op(kb.dve, lambda: nc.vector.tensor_scalar(kf[:, :], src[:, :], 1.0 / TWO_PI, None, ALU.mult),
                  rd=[t_b], wr=[t_b])
            kb.op(kb.dve, lambda: nc.vector.tensor_copy(posi[:, :], kf[:, :]), rd=[t_b], wr=[t_b])
            kb.op(kb.dve, lambda: nc.vector.tensor_copy(kf[:, :], posi[:, :]), rd=[t_b], wr=[t_b])
            kb.op(kb.dve, lambda: nc.vector.scalar_tensor_tensor(out=tmp[:, :], in0=kf[:, :], scalar=-C1, in1=src[:, :],
                                                                 op0=ALU.mult, op1=ALU.add), rd=[t_b], wr=[t_b])
            kb.op(kb.dve, lambda: nc.vector.scalar_tensor_tensor(out=tmp[:, :], in0=kf[:, :], scalar=-C2, in1=tmp[:, :],
                                                                 op0=ALU.mult, op1=ALU.add), rd=[t_b], wr=[t_b])
            kb.op(kb.dve, lambda: nc.vector.tensor_scalar(kf[:, :], tmp[:, :], float(np.pi), TWO_PI, ALU.is_gt, ALU.mult),
                  rd=[t_b], wr=[t_b])
            kb.op(kb.dve, lambda: nc.vector.tensor_tensor(tmp[:, :], tmp[:, :], kf[:, :], ALU.subtract),
                  rd=[t_b], wr=[t_b])
            kb.op(kb.dve, lambda: nc.vector.tensor_scalar(kf[:, :], tmp[:, :], -float(np.pi), TWO_PI, ALU.is_lt, ALU.mult),
                  rd=[t_b], wr=[t_b])
            kb.op(kb.dve, lambda: nc.vector.tensor_tensor(tmp[:, :], tmp[:, :], kf[:, :], ALU.add),
                  rd=[t_b], wr=[t_b])
            kb.op(kb.act, lambda: nc.scalar.activation(out=tmp[:, :], in_=tmp[:, :], func=AF.Sin), rd=[t_b], wr=[t_b])
            if post_scale is None:
                kb.op(kb.dve, lambda: nc.vector.tensor_copy(dst, tmp[:, :]), rd=[t_b], wr=[cs_b])
            else:
                kb.op(kb.dve, lambda: nc.vector.tensor_scalar(dst, tmp[:, :], post_scale, None, ALU.mult),
                      rd=[t_b], wr=[cs_b])

        for hf in range(2):
            cols = slice(hf * HS, (hf + 1) * HS)
            kb.dma(kb.sp, s_misc, posi[:, :], posb_d[:, cols], wr=[t_b])
            kb.op(kb.dve, lambda: nc.vector.tensor_copy(ang[:, :], posi[:, :]), rd=[t_b], wr=[t_b])
            kb.op(kb.dve, lambda: nc.vector.tensor_scalar(ang[:, :], ang[:, :], C["ropec"][:, 0:1], None, ALU.mult),
                  rd=[t_b], wr=[t_b])
            reduce_sin(sinS[:, cols], 0.0, C["ropec"][:, 1:2])
            reduce_sin(cosF[:, cols], float(np.pi / 2), None)
        kb.barrier()
        st_tmp.close()

        st_p = ExitStack()
        sbp = lambda n, shp, dt: st_p.enter_context(nc.sbuf_tensor(kb.name(n), shp, dt))
        wch = [sbp("wch", [128, 2, 8, 128], BF16) for _ in range(2)]
        wch_b = [Buf() for _ in range(2)]
        wch_s = [kb.slot(kb.name("wch")) for _ in range(2)]
        wgt = sbp("wgt", [128, 8, 48], BF16)
        wtm = sbp("wtm", [128, 8, 256], BF16)
        wgt_b = Buf()
        wtm_b = Buf()
        kb.dma(kb.pool, s_miscp, wgt[:, :, :], wgt_d.rearrange("p (kc n) -> p kc n", kc=8), rd=[wgt_buf], wr=[wgt_b])
        kb.dma(kb.pool, s_miscp, wtm[:, :, :], wtm_d.rearrange("p (kc n) -> p kc n", kc=8), rd=[wtm_buf], wr=[wtm_b])
        kb.batch_end(s_miscp, [wgt_b, wtm_b])
        pa = [ps("pa", [128, 512], F32) for _ in range(2)]
        pa_b = [Buf() for _ in range(2)]
        pb = [ps("pb", [128, 512], F32) for _ in range(2)]
        pb_b = [Buf() for _ in range(2)]
        t1 = [sbp("t1", [128, 512], F32) for _ in range(2)]
        t1_b = [Buf() for _ in range(2)]
        t2 = [sbp("t2", [128, 512], F32) for _ in range(2)]
        t2_b = [Buf() for _ in range(2)]
        ro = [sbp("ro", [128, 512], BF16) for _ in range(3)]
        ro_b = [Buf() for _ in range(3)]
        ro_s = [kb.slot(kb.name("ro")) for _ in range(3)]
        jobs = [(c, 8 + c, ("q", c)) for c in range(8)]
        jobs += [(16, None, ("kv", 0)), (17, None, ("kv", 1)), (18, 19, ("kv", 2)), (20, 21, ("kv", 3))]
        ci = 0
        ri = 0
        PL = DBG.get("proj", 9)
        for (cp, cq, dest) in (jobs if PL >= 2 else []):
            k = ci % 2
            ci += 1
            kb.dma(kb.sp, wch_s[k], wch[k][:, 0, :, :], wfm_d[cp].rearrange("p (kc n) -> p kc n", kc=8),
                   rd=[wfm_buf], wr=[wch_b[k]])
            if cq is not None:
                kb.dma(kb.sp, wch_s[k], wch[k][:, 1, :, :], wfm_d[cq].rearrange("p (kc n) -> p kc n", kc=8),
                       rd=[wfm_buf], wr=[wch_b[k]])
            for tg in range(8):
                ts_ = slice(tg * 512, (tg + 1) * 512)
                p = tg % 2
                r = ri % 3
                ri += 1
                terms = [(wch[k][:, 0, kc, :], hnT[:, kc, ts_], [wch_b[k]] + hnT_b[tg * 4:(tg + 1) * 4]) for kc in range(8)]
                kb.mm_group(pa[p][:, :], pa_b[p], terms)
                if cq is not None:
                    terms = [(wch[k][:, 1, kc, :], hnT[:, kc, ts_], [wch_b[k]] + hnT_b[tg * 4:(tg + 1) * 4])
                             for kc in range(8)]
                    kb.mm_group(pb[p][:, :], pb_b[p], terms)
                    kb.op(kb.dve, lambda: nc.vector.tensor_tensor(t1[p][:, :], pa[p][:, :], cosF[:, ts_], ALU.mult),
                          rd=[pa_b[p], cs_b], wr=[t1_b[p]])
                    kb.op(kb.dve, lambda: nc.vector.tensor_tensor(t2[p][:, :], pb[p][:, :], sinS[:, ts_], ALU.mult),
                          rd=[pb_b[p], cs_b], wr=[t2_b[p]])
                    kb.op(kb.pool, lambda: nc.gpsimd.tensor_tensor(ro[r][:, :], t1[p][:, :], t2[p][:, :], ALU.add),
                          rd=[t1_b[p], t2_b[p]], wr=[ro_b[r]])
                else:
                    kb.op(kb.act, lambda: nc.scalar.copy(ro[r][:, :], pa[p][:, :]), rd=[pa_b[p]], wr=[ro_b[r]])
                if dest[0] == "q":
                    c = dest[1]
                    kb.dma(kb.pool, ro_s[r], q_s[2 * c:2 * c + 2, :, ts_].rearrange("h d t -> (h d) t"), ro[r][:, :],
                           rd=[ro_b[r]], wr=[q_sb])
                else:
                    kb.dma(kb.pool, ro_s[r], kv_s[dest[1], :, ts_], ro[r][:, :], rd=[ro_b[r]], wr=[kv_sb])
        kb.op(kb.pool, lambda: nc.gpsimd.memset(sigT[:, :], 0.0), wr=[sigT_b])
        for tg in range(8 if PL >= 3 else 0):
            ts_ = slice(tg * 512, (tg + 1) * 512)
            p = tg % 2
            terms = [(wgt[:, kc, :], hnT[:, kc, ts_], [wgt_b] + hnT_b[tg * 4:(tg + 1) * 4]) for kc in range(8)]
            kb.mm_group(pa[p][0:48, :], pa_b[p], terms)
            kb.op(kb.act, lambda: nc.scalar.activation(out=sigT[0:48, ts_], in_=pa[p][0:48, :], func=AF.Sigmoid),
                  rd=[pa_b[p]], wr=[sigT_b])
        kb.op(kb.pool, lambda: nc.gpsimd.memset(Vtm[:, :, :, :], 1.0), wr=Vtm_b)
        for tt2 in range(NT // 2 if PL >= 4 else 0):
            p = tt2 % 2
            pe = kb.pe
            pe.wait(pb_b[p].wr_deps(), wtm_b.w, [hnT_b[tt2 * 2 + i].w for i in range(2)])
            ins = None
            for i in range(2):
                tt = tt2 * 2 + i
                for kc in range(8):
                    ins = nc.tensor.matmul(pb[p][:, i * 256:(i + 1) * 256], hnT[:, kc, tt * 128:(tt + 1) * 128],
                                           wtm[:, kc, :], start=(kc == 0), stop=(kc == 7))
            tok = pe.done(ins)
            wtm_b.note_read(tok)
            pb_b[p].note_write(tok)
            src = pb[p][:, :].rearrange("p (i g d) -> p i g d", i=2, g=4)
            eng = kb.act if tt2 % 2 == 0 else kb.dve
            for i in range(2):
                if True:
                    kb.op(kb.act, lambda: nc.scalar.copy(Vtm[:, tt2 * 2 + i, :, 0:64], src[:, i, :, :]),
                          rd=[pb_b[p]], wr=[Vtm_b[tt2 * 2 + i]])
                else:
                    kb.op(kb.dve, lambda: nc.vector.tensor_copy(Vtm[:, tt2 * 2 + i, :, 0:64], src[:, i, :, :]),
                          rd=[pb_b[p]], wr=[Vtm_b[tt2 * 2 + i]])
        kb.barrier()
        st_p.close()
        t1 = [sb("t1c", [64, 256], F32)]
        t2 = [sb("t2c", [64, 256], F32)]
        t1_b = [Buf()]
        t2_b = [Buf()]
        (w1_d, w1_buf, w2_d, w2_buf, pe_d, pe_buf) = cmpw
        W2 = sb("W2", [128, 192], BF16)
        peT = sb("peT", [64, 2, 32], BF16)
        w_b = Buf()
        kb.dma(kb.sp, s_misc, W2[:, :], w2_d, rd=[w2_buf], wr=[w_b])
        kb.dma(kb.sp, s_misc, peT[:, :, :], pe_d.rearrange("p (k l) -> p k l", k=2), rd=[pe_buf], wr=[w_b])
        Xs = [sb("Xc", [64, S], BF16) for _ in range(2)]
        X_bs = [Buf() for _ in range(2)]
        X_ss = [kb.slot(kb.name("Xc")) for _ in range(2)]
        W1s = [sb("W1", [64, 32, 128], BF16) for _ in range(2)]
        hb = sb("hb", [128, 1], F32)
        u = sb("cu", [128, NCMP], F32)
        u2 = sb("cu2", [128, NCMP], F32)
        sg = sb("csg", [128, NCMP], F32)
        hd = sb("chd", [128, 256], BF16)
        c_b = Buf()
        ph = ps("ph", [128, 256], F32)
        ph_b = Buf()
        phb = ps("phb", [128, 2], F32)
        phb_b = Buf()
        kb.op(kb.pool, lambda: nc.gpsimd.memset(kcmp[:, :, :], 0.0), wr=[kcmp_b])
        kb.op(kb.pool, lambda: nc.gpsimd.memset(vcmp[:, :, :, :], 1.0), wr=[vcmp_b])
        for kv in range(2 if PL >= 5 else 0):
            W1 = W1s[kv]
            kb.dma(kb.sp, s_misc, W1[:, :, :], w1_d[kv].rearrange("p (l n) -> p l n", l=32), rd=[w1_buf], wr=[w_b])
            for g in range(2):
                X = Xs[g]
                X_b = X_bs[g]
                kb.dma(kb.sp, X_ss[g], X[:, :], kv_s[kv, g * 64:(g + 1) * 64, :], rd=[kv_sb], wr=[X_b])
                terms = [(W1[:, l, :], X[:, l:l + 16 * (NCMP - 1) + 1:16], [w_b, X_b]) for l in range(32)]
                kb.mm_group(ph[:, 0:NCMP], ph_b, terms)
                terms = [(W1[:, l, :], peT[:, kv, l:l + 1], [w_b]) for l in range(32)]
                kb.mm_group(phb[:, 0:1], phb_b, terms)
                kb.op(kb.dve, lambda: nc.vector.tensor_copy(hb[:, :], phb[:, 0:1]), rd=[phb_b], wr=[c_b])
                kb.op(kb.act, lambda: nc.scalar.activation(out=u[:, :], in_=ph[:, 0:NCMP], func=AF.Identity,
                                                           bias=hb[:, 0:1]), rd=[ph_b, c_b], wr=[c_b])
                kb.op(kb.dve, lambda: nc.vector.tensor_tensor(u2[:, :], u[:, :], u[:, :], ALU.mult), rd=[c_b], wr=[c_b])
                kb.op(kb.dve, lambda: nc.vector.tensor_scalar(u2[:, :], u2[:, :], 0.044715, 1.0, ALU.mult, ALU.add),
                      rd=[c_b], wr=[c_b])
                kb.op(kb.dve, lambda: nc.vector.tensor_tensor(u2[:, :], u2[:, :], u[:, :], ALU.mult), rd=[c_b], wr=[c_b])
                kb.op(kb.act, lambda: nc.scalar.activation(out=sg[:, :], in_=u2[:, :], func=AF.Sigmoid,
                                                           scale=1.5957691216057308), rd=[c_b], wr=[c_b])
                kb.op(kb.dve, lambda: nc.vector.tensor_tensor(hd[:, 0:NCMP], u[:, :], sg[:, :], ALU.mult),
                      rd=[c_b], wr=[c_b])
                if kv == 0:
                    kb.mm_group(pa[0][0:64, 0:NCMP], pa_b[0], [(W2[:, 0:64], hd[:, 0:NCMP], [w_b, c_b])])
                    kb.mm_group(pb[0][0:64, 0:NCMP], pb_b[0], [(W2[:, 64:128], hd[:, 0:NCMP], [w_b, c_b])])
                    cs_sl = slice(31, 31 + 16 * (NCMP - 1) + 1, 16)
                    kb.op(kb.dve, lambda: nc.vector.tensor_tensor(t1[0][0:64, 0:NCMP], pa[0][0:64, 0:NCMP],
                                                                  cosF[0:64, cs_sl], ALU.mult),
                          rd=[pa_b[0], cs_b], wr=[t1_b[0]])
                    kb.op(kb.dve, lambda: nc.vector.tensor_tensor(t2[0][0:64, 0:NCMP], pb[0][0:64, 0:NCMP],
                                                                  sinS[0:64, cs_sl], ALU.mult),
                          rd=[pb_b[0], cs_b], wr=[t2_b[0]])
                    kb.op(kb.dve, lambda: nc.vector.tensor_tensor(kcmp[0:64, g, 0:NCMP], t1[0][0:64, 0:NCMP],
                                                                  t2[0][0:64, 0:NCMP], ALU.add),
                          rd=[t1_b[0], t2_b[0]], wr=[kcmp_b])
                else:
                    for ct in range(2):
                        rows = 128 if ct == 0 else NCMP - 128
                        kb.mm_group(pa[1][0:rows, 0:64], pa_b[1],
                                    [(hd[:, ct * 128:ct * 128 + rows], W2[:, 128:192], [w_b, c_b])])
                        kb.op(kb.dve, lambda: nc.vector.tensor_copy(vcmp[0:rows, ct, g, 0:64], pa[1][0:rows, 0:64]),
                              rd=[pa_b[1]], wr=[vcmp_b])
        kb.barrier()


def nsa_stage_select(kb, C, q_s, q_sb, kcmp, kcmp_b, mb_s, mb_sb, nqt=NT):
    nc = kb.nc
    with ExitStack() as st:
        sb = lambda n, shp, dt: st.enter_context(nc.sbuf_tensor(kb.name(n), shp, dt))
        ps = lambda n, shp, dt: st.enter_context(nc.psum_tensor(kb.name(n), shp, dt))
        Q8 = sb("Q8", [64, 8, S], BF16)
        Q8_b = Buf()
        MBT = sb("MBT", [64, S], BF16)
        MBT_b = Buf()
        P = [[sb("P1", [128, NCMP], F32) for _ in range(8)] for _ in range(3)]
        P_b = [[Buf() for _ in range(8)] for _ in range(3)]
        den = [sb("den1", [128, 16], F32) for _ in range(3)]
        den_b = [Buf() for _ in range(3)]
        accs = [sb("acc1", [128, 256], F32) for _ in range(2)]
        accs_b = [Buf() for _ in range(2)]
        accTs = [sb("accT", [128, 2, 128], F32) for _ in range(2)]
        accTs_b = [Buf() for _ in range(2)]
        score = sb("score", [128, 64], F32)
        work = sb("work", [128, 64], F32)
        m8 = sb("m8", [128, 16], F32)
        thr = sb("thr", [128, 1], F32)
        mbq = sb("mbq", [128, 64], BF16)
        sc_b = Buf()
        NPS = 3
        pss = [ps("pss", [128, 256], F32) for _ in range(NPS)]
        pss_b = [Buf() for _ in range(NPS)]
        pts = [ps("pt1", [128, 2, 128], F32) for _ in range(2)]
        pts_b = [Buf() for _ in range(2)]
        pimps = [ps("pimp", [128, 64], F32) for _ in range(2)]
        pimps_b = [Buf() for _ in range(2)]
        pmt = ps("pmt", [64, 128], BF16)
        pmt_b = Buf()
        s_misc = kb.slot(kb.name("misc"))
        s_out = kb.slot(kb.name("mbout"))
        hi = 0
        for g in range(2):
            for j in range(8):
                kb.dma(kb.sp, s_misc, Q8[:, j, :], q_s[g * 8 + j], rd=[q_sb], wr=[Q8_b])
            for i_ in range(2):
                kb.op(kb.dve, lambda: nc.vector.memset(accs[i_][:, :], 0.0), wr=[accs_b[i_]])

            def stage_a(qt):
                nonlocal hi
                qs = slice(qt * 128, (qt + 1) * 128)
                b2 = qt % 3
                for j in range(8):
                    p4 = hi % NPS
                    hi += 1
                    terms = [(Q8[:, j, qs], kcmp[0:64, g, 0:NCMP], [Q8_b, kcmp_b]),
                             (C["ident"], C["cmask_tm"][:, 248 - 8 * qt:248 - 8 * qt + NCMP], [])]
                    kb.mm_group(pss[p4][:, 0:NCMP], pss_b[p4], terms)
                    kb.op(kb.act, lambda: nc.scalar.activation(out=P[b2][j][:, :], in_=pss[p4][:, 0:NCMP], func=AF.Exp,
                                                               scale=0.125, accum_out=den[b2][:, j:j + 1]),
                          rd=[pss_b[p4]], wr=[P_b[b2][j], den_b[b2]])

            def stage_b1(qt):
                qs = slice(qt * 128, (qt + 1) * 128)
                b2 = qt % 2
                acc, acc_b, accT, accT_b = accs[b2], accs_b[b2], accTs[b2], accTs_b[b2]
                pt, pt_b, pimp, pimp_b = pts[b2], pts_b[b2], pimps[b2], pimps_b[b2]
                b2 = qt % 3
                kb.op(kb.dve, lambda: nc.vector.tensor_scalar(den[b2][:, 8:16], den[b2][:, 0:8], 1e-30, None, ALU.add),
                      rd=[den_b[b2]], wr=[den_b[b2]])
                kb.op(kb.dve, lambda: nc.vector.reciprocal(den[b2][:, 8:16], den[b2][:, 8:16]),
                      rd=[den_b[b2]], wr=[den_b[b2]])
                for j in range(8):
                    if j == 0:
                        kb.op(kb.dve, lambda: nc.vector.tensor_scalar(acc[:, 0:NCMP], P[b2][j][:, :], den[b2][:, 8:9], None,
                                                                      ALU.mult),
                              rd=[P_b[b2][j], den_b[b2]], wr=[acc_b])
                    else:
                        kb.op(kb.dve, lambda: nc.vector.scalar_tensor_tensor(out=acc[:, 0:NCMP], in0=P[b2][j][:, :],
                                                                             scalar=den[b2][:, 8 + j:9 + j],
                                                                             in1=acc[:, 0:NCMP],
                                                                             op0=ALU.mult, op1=ALU.add),
                              rd=[P_b[b2][j], den_b[b2]], wr=[acc_b])
                pe = kb.pe
                pe.wait(pt_b.wr_deps(), acc_b.w)
                nc.tensor.transpose(pt[:, 0, :], acc[:, 0:128], C["ident32"])
                ins = nc.tensor.transpose(pt[:, 1, :], acc[:, 128:256], C["ident32"])
                tok = pe.done(ins)
                acc_b.note_read(tok)
                pt_b.note_write(tok)
                kb.op(kb.act, lambda: nc.scalar.copy(accT[:, :, :], pt[:, :, :]), rd=[pt_b], wr=[accT_b])
                terms = [(accT[:, 0, :], C["overlap"][:, 0, :], [accT_b]),
                         (accT[0:127, 1, :], C["overlap"][0:127, 1, :], [accT_b])]
                kb.mm_group(pimp[:, :], pimp_b, terms)

            def stage_b2(qt):
                qs = slice(qt * 128, (qt + 1) * 128)
                b2 = qt % 2
                pimp, pimp_b = pimps[b2], pimps_b[b2]
                pe = kb.pe
                kb.op(kb.dve, lambda: nc.vector.tensor_tensor(score[:, :], pimp[:, :],
                                                              C["bonus"][:, 62 - 2 * qt:62 - 2 * qt + 64], ALU.add),
                      rd=[pimp_b], wr=[sc_b])
                kb.op(kb.dve, lambda: nc.vector.tensor_tensor(score[:, :], score[:, :], C["bonus"][:, 128:192], ALU.add),
                      rd=[sc_b], wr=[sc_b])
                kb.op(kb.dve, lambda: nc.vector.max(out=m8[:, 0:8], in_=score[:, :]), rd=[sc_b], wr=[sc_b])
                kb.op(kb.dve, lambda: nc.vector.match_replace(out=work[:, :], in_to_replace=m8[:, 0:8],
                                                              in_values=score[:, :], imm_value=-3.0e38),
                      rd=[sc_b], wr=[sc_b])
                kb.op(kb.dve, lambda: nc.vector.max(out=m8[:, 8:16], in_=work[:, :]), rd=[sc_b], wr=[sc_b])
                kb.op(kb.dve, lambda: nc.vector.tensor_reduce(out=thr[:, :], in_=m8[:, 8:16], axis=AX.X, op=ALU.min),
                      rd=[sc_b], wr=[sc_b])
                kb.op(kb.dve, lambda: nc.vector.tensor_scalar(mbq[:, :], score[:, :], thr[:, 0:1], NEGM, ALU.is_lt,
                                                              ALU.mult),
                      rd=[sc_b], wr=[sc_b])
                pe.wait(pmt_b.wr_deps(), sc_b.w)
                ins = nc.tensor.transpose(pmt[:, :], mbq[:, :], C["ident"])
                tok = pe.done(ins)
                sc_b.note_read(tok)
                pmt_b.note_write(tok)
                kb.op(kb.act, lambda: nc.scalar.copy(MBT[:, qs], pmt[:, :]), rd=[pmt_b], wr=[MBT_b])

            stage_a(0)
            if nqt > 1:
                stage_a(1)
            stage_b1(0)
            for qt in range(nqt):
                if qt + 2 < nqt:
                    stage_a(qt + 2)
                if qt + 1 < nqt:
                    stage_b1(qt + 1)
                stage_b2(qt)
            kb.dma(kb.pool, s_out, mb_s[g], MBT[:, :], rd=[MBT_b], wr=[mb_sb])
        kb.barrier()


def nsa_stage_attn(kb, C, q_s, q_sb, kv_s, kv_sb, mb_s, mb_sb, efull_d, kcmp, kcmp_b, vcmp, vcmp_b, Vtm, Vtm_b,
                   sigT, sigT_b, o_s, o_sb, heads=range(16), nqg=8):
    nc = kb.nc
    with ExitStack() as st:
        sb = lambda n, shp, dt: st.enter_context(nc.sbuf_tensor(kb.name(n), shp, dt))
        ps = lambda n, shp, dt: st.enter_context(nc.psum_tensor(kb.name(n), shp, dt))
        QM = [sb("QM", [128, S], BF16) for _ in range(2)]
        QM_b = [Buf() for _ in range(2)]
        QM_s = [kb.slot(kb.name("QM")) for _ in range(2)]
        ksE = sb("ksE", [128, S], BF16)
        kwT = sb("kwT", [128, S], BF16)
        kk_b = Buf()
        s_kk = kb.slot(kb.name("kk"))
        NPT = 6
        PT = [sb("PT", [128, 512], BF16) for _ in range(NPT)]
        PT_b = [Buf() for _ in range(NPT)]
        dhl = [sb("dhl", [128, 2, 512], BF16) for _ in range(2)]
        dhl_b = [Buf() for _ in range(2)]
        rd = [sb("rd", [64, 512], F32) for _ in range(2)]
        rd_b = [Buf() for _ in range(2)]
        wv = sb("wv", [64, 512], F32)
        tm = sb("tm", [64, 512], F32)
        e_b = Buf()
        oacc = [sb("oacc", [64, 512], F32) for _ in range(2)]
        oacc_b = [Buf() for _ in range(2)]
        obf = [sb("obf", [64, 512], BF16) for _ in range(2)]
        obf_b = [Buf() for _ in range(2)]
        obf_s = [kb.slot(kb.name("obf")) for _ in range(2)]
        NZ = 3
        pz = [ps("pz", [128, 512], F32) for _ in range(NZ)]
        pz_b = [Buf() for _ in range(NZ)]
        po = [ps("po2", [128, 512], F32) for _ in range(3)]
        po_b = [Buf() for _ in range(3)]
        pg = [ps("pg", [128, 512], F32) for _ in range(2)]
        pg_b = [Buf() for _ in range(2)]
        kb.op(kb.pool, lambda: nc.gpsimd.memset(kwT[:, :], 0.0), wr=[kk_b])
        for i_ in range(2):
            kb.op(kb.pool, lambda: nc.gpsimd.memset(dhl[i_][:, :, :], 0.0), wr=[dhl_b[i_]])
        cur_g = -1
        zi = 0
        pi = 0
        oi = 0
        ei = 0
        defer = []

        def run_deferred(force=False):
            keep = []
            for item in defer:
                item[0] -= 1
                if item[0] <= 0 or force:
                    item[1]()
                else:
                    keep.append(item)
            defer[:] = keep

        for hn_, h in enumerate(heads):
            g = h // 8
            qm = QM[hn_ % 2]
            qm_b = QM_b[hn_ % 2]
            kb.dma(kb.sp, QM_s[hn_ % 2], qm[0:64, :], q_s[h], rd=[q_sb], wr=[qm_b])
            kb.dma(kb.sp, QM_s[hn_ % 2], qm[64:128, :], mb_s[g], rd=[mb_sb], wr=[qm_b])
            if g != cur_g:
                cur_g = g
                kb.dma(kb.sp, s_kk, ksE[0:64, :], kv_s[2, g * 64:(g + 1) * 64, :], rd=[kv_sb], wr=[kk_b])
                kb.dma(kb.sp, s_kk, ksE[64:128, :], efull_d, wr=[kk_b])
                kb.dma(kb.sp, s_kk, kwT[0:64, :], kv_s[3, g * 64:(g + 1) * 64, :], rd=[kv_sb], wr=[kk_b])
            for qg in range(nqg):
                qs = slice(qg * 512, (qg + 1) * 512)
                tiles = []
                m0 = C["negmask_cmp"][:, qg, :] if qg <= 4 else None
                tiles.append((0, kcmp[:, g, 0:128], m0, 128, vcmp[:, 0, g, 0:65], [kcmp_b, vcmp_b], 0, 512))
                if qg >= 4:
                    tiles.append((0, kcmp[:, g, 128:NCMP], C["negmask_cmp"][0:127, qg - 4, :], 127,
                                  vcmp[0:127, 1, g, 0:65], [kcmp_b, vcmp_b], 0, 512))
                for kt in range(0, 4 * qg + 4):
                    j = kt - 4 * qg
                    m = C["negmask_le"][:, j, :] if j >= 0 else None
                    tiles.append((1, ksE[:, kt * 128:(kt + 1) * 128], m, 128, Vtm[:, kt, g, 0:65], [kk_b, Vtm_b[kt]],
                                  128 * j if j >= 0 else 0, 512))
                wl = []
                for kt in range(max(0, 4 * qg - 4), 4 * qg):
                    jp = kt - (4 * qg - 4)
                    wl.append((2, kwT[:, kt * 128:(kt + 1) * 128], C["negmask_win"][:, jp, :], 128,
                               Vtm[:, kt, 2 + g, 0:65], [kk_b, Vtm_b[kt]], 0, 128 * (jp + 1)))
                wl.reverse()
                for kt in range(4 * qg, 4 * qg + 4):
                    j = kt - 4 * qg
                    wl.append((2, kwT[:, kt * 128:(kt + 1) * 128], C["negmask_le"][:, j, :], 128,
                               Vtm[:, kt, 2 + g, 0:65], [kk_b, Vtm_b[kt]], 128 * j, 512))
                tiles += wl
                nt_ = len(tiles)
                first = {}
                last = {}
                for i, t in enumerate(tiles):
                    first.setdefault(t[0], i)
                    last[t[0]] = i
                slots = {}
                oa = oacc[oi % 2]
                oa_b = oacc_b[oi % 2]
                ob = obf[oi % 2]
                ob_b = obf_b[oi % 2]
                ob_s = obf_s[oi % 2]
                oi += 1
                done_br = []

                def emit_qk(i, tiles=tiles, slots=slots, qm=qm, qm_b=qm_b, qs=qs, qg=qg):
                    nonlocal zi, pi
                    br, l, m, rows, va, bufs, c0, c1 = tiles[i]
                    z = zi % NZ
                    zi += 1
                    p = pi % NPT
                    pi += 1
                    slots[i] = p
                    terms = [(l, qm[:, qg * 512 + c0:qg * 512 + c1], [bufs[0], qm_b])]
                    if m is not None:
                        terms.append((C["ident"][0:rows, 0:rows], m[:, c0:c1], []))
                    kb.mm_group(pz[z][0:rows, c0:c1], pz_b[z], terms)
                    kb.op(kb.act, lambda: nc.scalar.activation(out=PT[p][0:rows, c0:c1], in_=pz[z][0:rows, c0:c1],
                                                               func=AF.Exp, scale=0.125), rd=[pz_b[z]], wr=[PT_b[p]])

                def make_epilogue(br, h=h, qs=qs, oa=oa, oa_b=oa_b, ob=ob, ob_b=ob_b, ob_s=ob_s, done_br=done_br):
                    nonlocal ei
                    e = ei % 2
                    ei += 1
                    kb.op(kb.dve, lambda: nc.vector.tensor_copy(dhl[e][64:65, 0, :], po[br][64:65, :]), rd=[po_b[br]],
                          wr=[dhl_b[e]])
                    kb.op(kb.dve, lambda: nc.vector.scalar_tensor_tensor(out=dhl[e][64:65, 1, :], in0=po[br][64:65, :],
                                                                         scalar=1e-30, in1=dhl[e][64:65, 0, :],
                                                                         op0=ALU.add, op1=ALU.subtract),
                          rd=[po_b[br]], wr=[dhl_b[e]])

                    def part_b():
                        kb.mm_group(pg[0][:, :], pg_b[0], [(C["onesrow64"], dhl[e][:, 0, :], [dhl_b[e]]),
                                                           (C["onesrow64"], dhl[e][:, 1, :], [dhl_b[e]])])
                        kb.mm_group(pg[1][:, :], pg_b[1], [(C["gsel"][:, h * 3 + br, :], sigT[:, qs], [sigT_b])])
                        kb.op(kb.act, lambda: nc.scalar.activation(out=rd[e][:, :], in_=pg[0][0:64, :], func=AF.Ln),
                              rd=[pg_b[0]], wr=[rd_b[e]])
                        kb.op(kb.act, lambda: nc.scalar.activation(out=rd[e][:, :], in_=rd[e][:, :], func=AF.Exp,
                                                                   scale=-1.0), rd=[rd_b[e]], wr=[rd_b[e]])
                        kb.op(kb.dve, lambda: nc.vector.tensor_tensor(wv[:, :], pg[1][0:64, :], rd[e][:, :], ALU.mult),
                              rd=[pg_b[1], rd_b[e]], wr=[e_b])
                        if not done_br:
                            kb.op(kb.dve, lambda: nc.vector.tensor_tensor(oa[:, :], po[br][0:64, :], wv[:, :], ALU.mult),
                                  rd=[po_b[br], e_b], wr=[oa_b])
                        else:
                            kb.op(kb.dve, lambda: nc.vector.tensor_tensor(tm[:, :], po[br][0:64, :], wv[:, :], ALU.mult),
                                  rd=[po_b[br], e_b], wr=[e_b])
                            kb.op(kb.dve, lambda: nc.vector.tensor_tensor(oa[:, :], oa[:, :], tm[:, :], ALU.add),
                                  rd=[e_b], wr=[oa_b])
                        done_br.append(br)
                        if len(done_br) == 3:
                            kb.op(kb.pool, lambda: nc.gpsimd.tensor_copy(ob[:, :], oa[:, :]), rd=[oa_b], wr=[ob_b])
                            kb.dma(kb.pool, ob_s, o_s[h, :, qs], ob[:, :], rd=[ob_b], wr=[o_sb])
                    defer.append([2, part_b])

                def emit_av(i, tiles=tiles, slots=slots):
                    br, l, m, rows, va, bufs, c0, c1 = tiles[i]
                    p = slots[i]
                    pe = kb.pe
                    deps = [PT_b[p].w, bufs[1].w]
                    if i == first[br]:
                        deps += po_b[br].wr_deps()
                        assert c0 == 0 and c1 == 512
                    pe.wait(deps)
                    ins = nc.tensor.matmul(po[br][0:65, c0:c1], va, PT[p][0:rows, c0:c1], start=(i == first[br]),
                                           stop=(i == last[br]), skip_group_check=True)
                    tok = pe.done(ins)
                    PT_b[p].note_read(tok)
                    bufs[1].note_read(tok)
                    if i == last[br]:
                        po_b[br].note_write(tok)
                        make_epilogue(br)

                emit_qk(0)
                if nt_ > 1:
                    emit_qk(1)
                for i in range(nt_):
                    if i + 2 < nt_:
                        emit_qk(i + 2)
                    emit_av(i)
                    run_deferred()
        run_deferred(force=True)
        run_deferred(force=True)
        kb.barrier()


def phase_nsa(kb, C, xin, xin_bufs, xout, xout_bufs, grow_d, posb_d, W, scr, heads=range(16), nqg=8, nqt=NT):
    nc = kb.nc
    q_s, kv_s, mb_s, o_s = scr["q_s"], scr["kv_s"], scr["mb_s"], scr["o_s"]
    q_sb, kv_sb, mb_sb, o_sb = Buf(), Buf(), Buf(), Buf()
    with ExitStack() as st0:
        sb0 = lambda n, shp, dt: st0.enter_context(nc.sbuf_tensor(kb.name(n), shp, dt))
        Vtm = sb0("Vtm", [128, NT, 4, 80], BF16)
        Vtm_b = [Buf() for _ in range(NT)]
        sigT = sb0("sigT", [128, S], BF16)
        sigT_b = Buf()
        kcmp = sb0("kcmp", [128, 2, 256], BF16)
        kcmp_b = Buf()
        vcmp = sb0("vcmp", [128, 2, 2, 80], BF16)
        vcmp_b = Buf()
        with ExitStack() as st1:
            hnT = st1.enter_context(nc.sbuf_tensor(kb.name("hnT"), [128, 8, S], BF16))
            hnT_b = [Buf() for _ in range(NT)]
            stage_norm_all(kb, C, xin, xin_bufs, grow_d, hnT, hnT_b)
            nsa_stage_proj(kb, C, hnT, hnT_b, posb_d, W["wfm"][0], W["wfm"][1], W["wgt"][0], W["wgt"][1],
                           W["wtm"][0], W["wtm"][1], q_s, q_sb, kv_s, kv_sb, Vtm, Vtm_b, sigT, sigT_b,
                           (W["w1"][0], W["w1"][1], W["w2"][0], W["w2"][1], W["pe"][0], W["pe"][1]),
                           kcmp, kcmp_b, vcmp, vcmp_b)
        if DBG.get("nsa", 9) >= 2:
            nsa_stage_select(kb, C, q_s, q_sb, kcmp, kcmp_b, mb_s, mb_sb, nqt=nqt)
        if DBG.get("nsa", 9) >= 3:
          nsa_stage_attn(kb, C, q_s, q_sb, kv_s, kv_sb, mb_s, mb_sb, C["efull_d"], kcmp, kcmp_b, vcmp, vcmp_b,
                       Vtm, Vtm_b, sigT, sigT_b, o_s, o_sb, heads=heads, nqg=nqg)
    with ExitStack() as st2:
        oT = st2.enter_context(nc.sbuf_tensor(kb.name("oT"), [128, 8, S], BF16))
        oT_b = [Buf() for _ in range(8)]
        sl = kb.slot(kb.name("oTl"))
        for c in range(8):
            kb.dma(kb.sp, sl, oT[:, c, :], o_s[2 * c:2 * c + 2].rearrange("h d t -> (h d) t"), rd=[o_sb], wr=oT_b)
        stage_outproj(kb, C, oT, oT_b, W["wo"][0], W["wo"][1], xin, xin_bufs, xout, xout_bufs)


def host_nsa_weights(w_in, pe_k, pe_v, k_w1, k_w2, v_w1, v_w2, w_out):
    w_in = np.asarray(w_in, np.float32)
    perm64 = (np.arange(64) + 32) % 64
    qperm = (np.arange(1024) // 64) * 64 + perm64[np.arange(1024) % 64]
    kperm = (np.arange(128) // 64) * 64 + perm64[np.arange(128) % 64]
    q = w_in[:, 0:1024]
    blk = lambda i: w_in[:, 1024 + 128 * i:1024 + 128 * (i + 1)]
    kc, vc, ks, vs, kw, vw = [blk(i) for i in range(6)]
    fm = np.concatenate([q, q[:, qperm], kc, vc, ks, ks[:, kperm], kw, kw[:, kperm]], axis=1)
    out = {}
    out["wfm_h"] = host_chunks(fm)
    out["wgt_h"] = host_rows(w_in[:, 1792:1840])
    out["wtm_h"] = host_rows(np.concatenate([vs, vw], axis=1))
    w1 = lambda w: np.ascontiguousarray(np.asarray(w, np.float32).reshape(32, 64, 128).transpose(1, 0, 2)).reshape(64, 4096)
    out["w1_h"] = np.stack([w1(k_w1), w1(v_w1)], axis=0)
    k_w2 = np.asarray(k_w2, np.float32)
    out["w2_h"] = np.ascontiguousarray(np.concatenate([k_w2, k_w2[:, perm64], np.asarray(v_w2, np.float32)], axis=1))
    out["pe_h"] = np.ascontiguousarray(np.concatenate([np.asarray(pe_k, np.float32).T, np.asarray(pe_v, np.float32).T],
                                                      axis=1))
    out["wo_h"] = host_rows(w_out)
    return out


W_SHAPES = {
    "sba_wqk": ([16 * 128, 1024], 128), "sba_wv": ([128, 8192], 128), "sba_wo": ([128, 8192], 128),
    "ffn0_wup": ([NPAIR * 128, 2048], 128), "ffn0_wdn": ([DFF, D], 128),
    "ffn1_wup": ([NPAIR * 128, 2048], 128), "ffn1_wdn": ([DFF, D], 128),
    "nsa_wfm": ([22 * 128, 1024], 128), "nsa_wgt": ([128, 384], 128), "nsa_wtm": ([128, 2048], 128),
    "nsa_w1": ([128, 4096], 64), "nsa_w2": ([128, 192], 128), "nsa_pe": ([64, 64], 64), "nsa_wo": ([128, 8192], 128),
}


def build_full():
    kb = KB()
    nc = kb.nc
    x_d = nc.dram_tensor("x", [S, D], F32, kind="ExternalInput").ap()
    posb_d = nc.dram_tensor("posb", [128, S], I32, kind="ExternalInput").ap()
    g_d = {n: nc.dram_tensor(n, [128, D], F32, kind="ExternalInput").ap()
           for n in ("g_mix0", "g_ffn0", "g_mix1", "g_ffn1", "g_fin")}
    cwb_d = [nc.dram_tensor(f"cwb{l}", [128, 4 * 44], F32, kind="ExternalInput").ap() for l in range(2)]
    y_d = nc.dram_tensor("y", [S, D], F32, kind="ExternalOutput").ap()
    H, Sx, Wb = {}, {}, {}
    for n, (shp, rc) in W_SHAPES.items():
        H[n] = nc.dram_tensor(n + "_h", shp, F32, kind="ExternalInput").ap()
        Sx[n] = nc.dram_tensor(n + "_s", shp, BF16).ap()
        Wb[n] = Buf()
    xa = nc.dram_tensor("xa", [S, D], F32).ap()
    xb = nc.dram_tensor("xb", [S, D], F32).ap()
    xc = nc.dram_tensor("xc", [S, D], F32).ap()
    scr = {"q_s": nc.dram_tensor("q_s", [16, 64, S], BF16).ap(), "kv_s": nc.dram_tensor("kv_s", [4, 128, S], BF16).ap(),
           "mb_s": nc.dram_tensor("mb_s", [2, 64, S], BF16).ap(), "o_s": nc.dram_tensor("o_s", [16, 64, S], BF16).ap()}
    jobs = []
    jobs_bg = []
    for n, (shp, rc) in W_SHAPES.items():
        for r0 in range(0, shp[0], rc):
            (jobs if n.startswith("sba_") else jobs_bg).append((H[n][r0:r0 + rc, :], Sx[n][r0:r0 + rc, :], Wb[n]))
    phase_convert(kb, jobs)
    chunk = lambda ap, p: ap.rearrange("(c p) n -> c p n", p=p)
    x_b = [Buf() for _ in range(NT)]
    xa_b = [Buf() for _ in range(NT)]
    xb_b = [Buf() for _ in range(NT)]
    xc_b = [Buf() for _ in range(NT)]
    y_b = [Buf() for _ in range(NT)]
    with ExitStack() as cst:
        C = load_consts(kb, SBA_CONSTS, cst)
        bg = BgConv(kb, cst, jobs_bg)
        phase_sba(kb, C, x_d, x_b, xa, xa_b, g_d["g_mix0"], chunk(Sx["sba_wqk"], 128), Wb["sba_wqk"],
                  Sx["sba_wv"], Wb["sba_wv"], Sx["sba_wo"], Wb["sba_wo"], bg=bg)
    with ExitStack() as cst:
        C = load_consts(kb, FFN_CONSTS, cst)
        phase_ffn(kb, C, 0, xa, xa_b, xb, xb_b, chunk(Sx["ffn0_wup"], 128), Wb["ffn0_wup"],
                  chunk(Sx["ffn0_wdn"], 128), Wb["ffn0_wdn"], cwb_d[0], g_d["g_ffn0"])
    W = {"wfm": (chunk(Sx["nsa_wfm"], 128), Wb["nsa_wfm"]), "wgt": (Sx["nsa_wgt"], Wb["nsa_wgt"]),
         "wtm": (Sx["nsa_wtm"], Wb["nsa_wtm"]), "w1": (chunk(Sx["nsa_w1"], 64), Wb["nsa_w1"]),
         "w2": (Sx["nsa_w2"], Wb["nsa_w2"]), "pe": (Sx["nsa_pe"], Wb["nsa_pe"]), "wo": (Sx["nsa_wo"], Wb["nsa_wo"])}
    with ExitStack() as cst:
        C = load_consts(kb, NSA_CONSTS, cst)
        phase_nsa(kb, C, xb, xb_b, xc, xc_b, g_d["g_mix1"], posb_d, W, scr)
    with ExitStack() as cst:
        C = load_consts(kb, FFN_CONSTS, cst)
        phase_ffn(kb, C, 1, xc, xc_b, y_d, y_b, chunk(Sx["ffn1_wup"], 128), Wb["ffn1_wup"],
                  chunk(Sx["ffn1_wdn"], 128), Wb["ffn1_wdn"], cwb_d[1], g_d["g_ffn1"], final_grow_d=g_d["g_fin"])
    return kb


def kernel(x, positions, norm_mix, sba_w_in, sba_w_out, nsa_w_in, nsa_cmp_pos_k, nsa_cmp_pos_v, nsa_cmp_k_w1,
           nsa_cmp_k_w2, nsa_cmp_v_w1, nsa_cmp_v_w2, nsa_w_out, norm_ffn, ffn_w_up, ffn_conv_w, ffn_conv_b,
           ffn_w_down, norm_final):
    x = np.asarray(x, np.float32)
    positions = np.asarray(positions)
    B = x.shape[0]
    shared = {}
    sw = np.asarray(sba_w_in[0], np.float32)
    shared["sba_wqk_h"] = host_chunks(sw[:, :2048])
    shared["sba_wv_h"] = host_rows(sw[:, 2048:])
    shared["sba_wo_h"] = host_rows(sba_w_out[0])
    for l in range(2):
        shared[f"ffn{l}_wup_h"] = host_wup(ffn_w_up[l])
        shared[f"ffn{l}_wdn_h"] = np.ascontiguousarray(np.asarray(ffn_w_down[l], np.float32))
        shared[f"cwb{l}"] = host_cwb(ffn_conv_w[l], ffn_conv_b[l])
    hw = host_nsa_weights(nsa_w_in[0], nsa_cmp_pos_k[0], nsa_cmp_pos_v[0], nsa_cmp_k_w1[0], nsa_cmp_k_w2[0],
                          nsa_cmp_v_w1[0], nsa_cmp_v_w2[0], nsa_w_out[0])
    for k_ in ("wfm", "wgt", "wtm", "w1", "w2", "pe", "wo"):
        shared["nsa_" + k_ + "_h"] = np.ascontiguousarray(hw[k_ + "_h"].reshape(W_SHAPES["nsa_" + k_][0]))
    shared["g_mix0"] = host_row(norm_mix[0])
    shared["g_mix1"] = host_row(norm_mix[1])
    shared["g_ffn0"] = host_row(norm_ffn[0])
    shared["g_ffn1"] = host_row(norm_ffn[1])
    shared["g_fin"] = host_row(norm_final)
    for k_, v_ in consts_host().items():
        shared["c_" + k_] = v_
    in_maps = []
    for b in range(B):
        m = dict(shared)
        m["x"] = np.ascontiguousarray(x[b])
        m["posb"] = np.ascontiguousarray(np.broadcast_to(positions[b].astype(np.int32)[None, :], (128, S)))
        in_maps.append(m)
    kb = build_full()
    res = run_bass_kernel_spmd(kb.nc, in_maps, core_ids=list(range(B)))
    return np.stack([np.asarray(r["y"], np.float32) for r in res.results], axis=0)
```

```python
from contextlib import ExitStack
import numpy as np
import ml_dtypes
import concourse.bass as bass
import concourse.mybir as mybir
from concourse.bass_utils import run_bass_kernel_spmd

F32 = mybir.dt.float32
BF16 = mybir.dt.bfloat16
I32 = mybir.dt.int32
AF = mybir.ActivationFunctionType
ALU = mybir.AluOpType
AX = mybir.AxisListType

DBG = {}
S = 4096
D = 1024
NT = S // 128
DFF = 2816
NPAIR = DFF // 128
EPS = 1e-6
NEGM = -30000.0


def _flat(ts):
    for t in ts:
        if t is None:
            continue
        if isinstance(t, tuple) and len(t) == 3 and isinstance(t[1], int):
            yield t
        else:
            yield from _flat(t)


class Buf:
    def __init__(self, name=""):
        self.name = name
        self.w = None
        self.r = {}

    def rd_deps(self):
        return [self.w]

    def wr_deps(self):
        return [self.w] + list(self.r.values())

    def note_read(self, tok):
        if tok is None:
            return
        k = tok[2]
        if k not in self.r or self.r[k][1] < tok[1]:
            self.r[k] = tok

    def note_write(self, tok):
        self.w = tok
        self.r = {}


class Eng:
    def __init__(self, kb, name, eng):
        self.kb = kb
        self.name = name
        self.eng = eng
        self.sem = kb.newsem("e_" + name)
        self.n = 0
        self.seen = {}

    def wait(self, *toks):
        for t in _flat(toks):
            sem, val, key = t
            if self.seen.get(key, 0) >= val:
                continue
            self.seen[key] = val
            self.eng.wait_ge(sem, val)

    def done(self, ins):
        self.n += 1
        ins.then_inc(self.sem, 1)
        return (self.sem, self.n, self.name)


class Slot:
    def __init__(self, kb, name):
        self.sem = kb.newsem("d_" + name)
        self.val = 0
        self.key = "d_" + name

    def done(self, ins):
        self.val += 16
        ins.then_inc(self.sem, 16)
        return (self.sem, self.val, self.key)


class KB:
    def __init__(self):
        self.nc = bass.Bass("TRN2", target_bir_lowering=False)
        self.root = ExitStack()
        self.nsem = 0
        nc = self.nc
        self.pe = Eng(self, "pe", nc.tensor)
        self.act = Eng(self, "act", nc.scalar)
        self.dve = Eng(self, "dve", nc.vector)
        self.pool = Eng(self, "pool", nc.gpsimd)
        self.sp = Eng(self, "sp", nc.sync)
        self.engs = [self.pe, self.act, self.dve, self.pool, self.sp]
        self.slots = []
        self.uid = 0

    def newsem(self, name):
        self.nsem += 1
        return self.root.enter_context(self.nc.semaphore(name))

    def slot(self, name):
        s = Slot(self, name)
        self.slots.append(s)
        return s

    def name(self, p):
        self.uid += 1
        return f"{p}_{self.uid}"

    def op(self, E, make, rd=(), wr=(), sig=True, extra=()):
        deps = list(extra)
        for b in rd:
            deps.append(b.w)
        for b in wr:
            deps.extend(b.wr_deps())
        E.wait(deps)
        ins = make()
        tok = E.done(ins) if sig else None
        if tok is not None:
            for b in rd:
                b.note_read(tok)
            for b in wr:
                b.note_write(tok)
        return tok

    def dma(self, Q, slot, out, in_, rd=(), wr=(), extra=()):
        deps = list(extra)
        for b in rd:
            deps.append(b.w)
        for b in wr:
            deps.extend(b.wr_deps())
        Q.wait(deps)
        ins = Q.eng.dma_start(out=out, in_=in_)
        tok = slot.done(ins)
        for b in rd:
            b.note_read(tok)
        for b in wr:
            b.note_write(tok)
        return tok

    def batch_end(self, slot, bufs):
        tok = (slot.sem, slot.val, slot.key)
        for b in bufs:
            b.w = tok

    def mm_group(self, out_ap, obuf, terms, sig=True):
        pe = self.pe
        deps = list(obuf.wr_deps())
        for (_, _, bufs) in terms:
            for b in bufs:
                deps.append(b.w)
        pe.wait(deps)
        n = len(terms)
        ins = None
        for i, (l, r, _) in enumerate(terms):
            ins = self.nc.tensor.matmul(out_ap, l, r, start=(i == 0), stop=(i == n - 1))
        tok = pe.done(ins)
        for (_, _, bufs) in terms:
            for b in bufs:
                b.note_read(tok)
        obuf.note_write(tok)
        return tok

    def barrier(self):
        toks = []
        for e in self.engs:
            if e.n > 0:
                toks.append((e.sem, e.n, e.name))
        for s in self.slots:
            if s.val > 0:
                toks.append((s.sem, s.val, s.key))
        for e in self.engs:
            e.wait(toks)


def phase_convert(kb, jobs):
    nc = kb.nc
    CH = 2048
    with ExitStack() as st:
        NB = 3
        tin = [st.enter_context(nc.sbuf_tensor(kb.name("cvi"), [128, CH], F32)) for _ in range(NB)]
        tout = [st.enter_context(nc.sbuf_tensor(kb.name("cvo"), [128, CH], BF16)) for _ in range(NB)]
        bin_ = [Buf() for _ in range(NB)]
        bout = [Buf() for _ in range(NB)]
        sin = [kb.slot(kb.name("cvin")) for _ in range(NB)]
        sout = [kb.slot(kb.name("cvout")) for _ in range(NB)]
        i = 0
        for (src, dst, dbuf) in jobs:
            R, Fd = src.shape[0], src.shape[1]
            for c0 in range(0, Fd, CH):
                w = min(CH, Fd - c0)
                k = i % NB
                kb.dma(kb.sp, sin[k], tin[k][0:R, 0:w], src[:, c0:c0 + w], wr=[bin_[k]])
                sel = i % 3
                if sel == 0:
                    kb.op(kb.dve, lambda: nc.vector.tensor_copy(tout[k][0:R, 0:w], tin[k][0:R, 0:w]),
                          rd=[bin_[k]], wr=[bout[k]])
                elif sel == 1:
                    kb.op(kb.pool, lambda: nc.gpsimd.tensor_copy(tout[k][0:R, 0:w], tin[k][0:R, 0:w]),
                          rd=[bin_[k]], wr=[bout[k]])
                else:
                    kb.op(kb.act, lambda: nc.scalar.copy(tout[k][0:R, 0:w], tin[k][0:R, 0:w]),
                          rd=[bin_[k]], wr=[bout[k]])
                kb.dma(kb.pool, sout[k], dst[:, c0:c0 + w], tout[k][0:R, 0:w], rd=[bout[k]], wr=[dbuf])
                i += 1
        kb.barrier()


class BgConv:
    def __init__(self, kb, stack, jobs, CH=512, NB=2):
        nc = kb.nc
        self.kb = kb
        self.NB = NB
        self.tin = [stack.enter_context(nc.sbuf_tensor(kb.name("bgi"), [128, CH], F32)) for _ in range(NB)]
        self.tout = [stack.enter_context(nc.sbuf_tensor(kb.name("bgo"), [128, CH], BF16)) for _ in range(NB)]
        self.bin = [Buf() for _ in range(NB)]
        self.bout = [Buf() for _ in range(NB)]
        self.sin = [kb.slot(kb.name("bgin")) for _ in range(NB)]
        self.sout = [kb.slot(kb.name("bgout")) for _ in range(NB)]
        self.tiles = []
        for (src, dst, dbuf) in jobs:
            R, Fd = src.shape[0], src.shape[1]
            for c0 in range(0, Fd, CH):
                w = min(CH, Fd - c0)
                self.tiles.append((src[:, c0:c0 + w], dst[:, c0:c0 + w], R, w, dbuf))
        self.pos = 0

    def emit(self, n):
        kb = self.kb
        nc = kb.nc
        for _ in range(n):
            if self.pos >= len(self.tiles):
                return
            src, dst, R, w, dbuf = self.tiles[self.pos]
            k = self.pos % self.NB
            self.pos += 1
            kb.dma(kb.sp, self.sin[k], self.tin[k][0:R, 0:w], src, wr=[self.bin[k]])
            kb.op(kb.pool, lambda: nc.gpsimd.tensor_copy(self.tout[k][0:R, 0:w], self.tin[k][0:R, 0:w]),
                  rd=[self.bin[k]], wr=[self.bout[k]])
            kb.dma(kb.pool, self.sout[k], dst, self.tout[k][0:R, 0:w], rd=[self.bout[k]], wr=[dbuf])

    def flush(self):
        self.emit(len(self.tiles))


def norm_block(kb, xts, xbufs, grow, gbuf, hns, hnbufs, sq, sqbuf, st, stbuf):
    nc = kb.nc
    n = len(xts)
    for i in range(n):
        kb.op(kb.act, lambda: nc.scalar.activation(out=sq, in_=xts[i], func=AF.Square, accum_out=st[:, i:i + 1]),
              rd=[xbufs[i]], wr=[sqbuf, stbuf])
    kb.op(kb.dve, lambda: nc.vector.tensor_scalar(st[:, 4:4 + n], st[:, 0:n], 1.0 / D, EPS, ALU.mult, ALU.add),
          rd=[stbuf], wr=[stbuf])
    kb.op(kb.act, lambda: nc.scalar.activation(out=st[:, 8:8 + n], in_=st[:, 4:4 + n], func=AF.Sqrt),
          rd=[stbuf], wr=[stbuf])
    kb.op(kb.dve, lambda: nc.vector.reciprocal(st[:, 4:4 + n], st[:, 8:8 + n]), rd=[stbuf], wr=[stbuf])
    for i in range(n):
        kb.op(kb.dve, lambda: nc.vector.scalar_tensor_tensor(out=hns[i], in0=xts[i], scalar=st[:, 4 + i:5 + i],
                                                             in1=grow, op0=ALU.mult, op1=ALU.mult),
              rd=[xbufs[i], stbuf, gbuf], wr=[hnbufs[i]])


def transpose_tile(kb, C, hn, hnbuf, pst, pstbuf, dst_ap, dstbuf, evac_eng):
    nc = kb.nc
    pe = kb.pe
    pe.wait(pstbuf.wr_deps(), hnbuf.w)
    ins = None
    for kc in range(8):
        ins = nc.tensor.transpose(pst[:, kc, :], hn[:, kc * 128:(kc + 1) * 128], C["ident"])
    tok = pe.done(ins)
    hnbuf.note_read(tok)
    pstbuf.note_write(tok)
    if evac_eng is kb.act:
        kb.op(kb.act, lambda: nc.scalar.copy(dst_ap, pst[:, :, :]), rd=[pstbuf], wr=[dstbuf])
    else:
        kb.op(kb.dve, lambda: nc.vector.tensor_copy(dst_ap, pst[:, :, :]), rd=[pstbuf], wr=[dstbuf])


def phase_ffn(kb, C, layer, xin, xin_bufs, xout, xout_bufs, wup_d, wup_buf, wdn_d, wdn_buf,
              cwb_d, grow_d, final_grow_d=None):
    nc = kb.nc
    with ExitStack() as st:
        sb = lambda n, shp, dt: st.enter_context(nc.sbuf_tensor(kb.name(n), shp, dt))
        ps = lambda n, shp, dt: st.enter_context(nc.psum_tensor(kb.name(n), shp, dt))
        wdn = sb("wdn", [128, NPAIR, 1024], BF16)
        wdn_b = Buf()
        cw = sb("cw", [128, 4, 44], F32)
        cw_b = Buf()
        grow = sb("grow", [128, 1024], F32)
        grow_b = Buf()
        NW = 4
        wup = [sb("wup", [128, 8, 256], BF16) for _ in range(NW)]
        wup_b = [Buf() for _ in range(NW)]
        wup_s = [kb.slot(kb.name("wup")) for _ in range(NW)]
        hnT = [sb("hnT", [128, 8, 512], BF16) for _ in range(2)]
        hnT_b = [[Buf() for _ in range(4)] for _ in range(2)]
        G2 = [sb("G", [128, NPAIR, 512], BF16) for _ in range(2)]
        G2_b = [[Buf() for _ in range(NPAIR)] for _ in range(2)]
        T0 = [sb("T0", [128, 512], F32) for _ in range(4)]
        T0_b = [Buf() for _ in range(4)]
        xt = [sb("xt", [128, 1024], F32) for _ in range(4)]
        xt_b = [Buf() for _ in range(4)]
        xt_s = [kb.slot(kb.name("xt")) for _ in range(4)]
        hn = [sb("hn", [128, 1024], BF16) for _ in range(4)]
        hn_b = [Buf() for _ in range(4)]
        sq = sb("sq", [128, 1024], BF16)
        sq_b = Buf()
        st2 = sb("st2", [128, 12], F32)
        st2_b = Buf()
        U = [sb("U", [128, 514], F32) for _ in range(4)]
        U_b = [Buf() for _ in range(4)]
        Cc = [sb("Cc", [128, 512], F32) for _ in range(4)]
        Cc_b = [Buf() for _ in range(4)]
        Sg = [sb("Sg", [128, 512], F32) for _ in range(2)]
        Sg_b = [Buf() for _ in range(2)]
        halo = sb("halo", [128, 44, 2], F32)
        halo_b = [Buf() for _ in range(44)]
        xr = [sb("xr", [128, 1024], F32) for _ in range(2)]
        xr_b = [Buf() for _ in range(2)]
        xr_s = [kb.slot(kb.name("xr")) for _ in range(2)]
        NXO = 2 if final_grow_d is None else 4
        xo = [sb("xo", [128, 1024], F32) for _ in range(NXO)]
        xo_b = [Buf() for _ in range(NXO)]
        xo_s = [kb.slot(kb.name("xo")) for _ in range(4)]
        if final_grow_d is not None:
            fgrow = sb("fgrow", [128, 1024], F32)
            fgrow_b = Buf()
            fst = sb("fst", [128, 12], F32)
            fst_b = Buf()
        pst = ps("pst", [128, 8, 128], BF16)
        pst_b = Buf()
        pu = [ps("pu", [128, 512], F32) for _ in range(4)]
        pu_b = [Buf() for _ in range(4)]
        po = [ps("po", [128, 512], F32) for _ in range(2)]
        po_b = [Buf() for _ in range(2)]

        s_misc = kb.slot(kb.name("misc"))
        for c0_, c1_ in ((0, 6), (6, 11), (11, 17), (17, 22)):
            kb.dma(kb.pool, s_misc, wdn[:, c0_:c1_, :], wdn_d[c0_:c1_].rearrange("c p n -> p c n"), rd=[wdn_buf],
                   wr=[wdn_b])
        kb.dma(kb.pool, s_misc, cw[:, :, :], cwb_d.rearrange("p (j c) -> p j c", j=4), wr=[cw_b])
        kb.dma(kb.pool, s_misc, grow[:, :], grow_d, wr=[grow_b])
        if final_grow_d is not None:
            kb.dma(kb.pool, s_misc, fgrow[:, :], final_grow_d, wr=[fgrow_b])
            kb.batch_end(s_misc, [fgrow_b])
        kb.batch_end(s_misc, [wdn_b, cw_b, grow_b])
        kb.op(kb.dve, lambda: nc.vector.memset(halo[:, :, :], 0.0), wr=halo_b)

        wi = 0
        ui = 0
        oi = 0
        NBLK = S // 512

        def norm_stage(tb):
            hb = tb % 2
            for t4 in range(4):
                tt = tb * 4 + t4
                kb.dma(kb.sp, xt_s[t4], xt[t4][:, :], xin[tt * 128:(tt + 1) * 128, :], rd=[xin_bufs[tt]],
                       wr=[xt_b[t4]])
            norm_block(kb, [xt[i][:, :] for i in range(4)], xt_b, grow[:, :], grow_b,
                       [hn[i][:, :] for i in range(4)], hn_b, sq[:, :], sq_b, st2[:, :], st2_b)
            for t4 in range(4):
                transpose_tile(kb, C, hn[t4], hn_b[t4], pst, pst_b, hnT[hb][:, :, t4 * 128:(t4 + 1) * 128],
                               hnT_b[hb][t4], kb.act)

        def down_group(tb, gi):
            nonlocal oi
            t4, nh = divmod(gi, 2)
            tt = tb * 4 + t4
            Gd = G2[tb % 2]
            Gd_b = G2_b[tb % 2]
            k = t4 % 2
            ko = t4 % NXO
            if nh == 0:
                kb.dma(kb.sp, xr_s[k], xr[k][:, :], xin[tt * 128:(tt + 1) * 128, :], rd=[xin_bufs[tt]], wr=[xr_b[k]])
            terms = [(Gd[:, fc, t4 * 128:(t4 + 1) * 128], wdn[:, fc, nh * 512:(nh + 1) * 512], [Gd_b[fc], wdn_b])
                     for fc in range(NPAIR)]
            kb.mm_group(po[nh][:, :], po_b[nh], terms)
            kb.op(kb.dve, lambda: nc.vector.tensor_tensor(xo[ko][:, nh * 512:(nh + 1) * 512], po[nh][:, :],
                                                          xr[k][:, nh * 512:(nh + 1) * 512], ALU.add),
                  rd=[po_b[nh], xr_b[k]], wr=[xo_b[ko]])
            if nh == 1 and final_grow_d is None:
                kb.dma(kb.pool, xo_s[ko], xout[tt * 128:(tt + 1) * 128, :], xo[ko][:, :], rd=[xo_b[ko]],
                       wr=[xout_bufs[tt]])
            if gi == 7 and final_grow_d is not None:
                norm_block(kb, [xo[i][:, :] for i in range(4)], xo_b, fgrow[:, :], fgrow_b,
                           [xo[i][:, :] for i in range(4)], xo_b, sq[:, :], sq_b, fst[:, :], fst_b)
                for t4_ in range(4):
                    tt_ = tb * 4 + t4_
                    kb.dma(kb.pool, xo_s[t4_], xout[tt_ * 128:(tt_ + 1) * 128, :], xo[t4_][:, :], rd=[xo_b[t4_]],
                           wr=[xout_bufs[tt_]])

        norm_stage(0)
        for tb in range(NBLK):
            hb = tb % 2
            Gw = G2[tb % 2]
            Gw_b = G2_b[tb % 2]
            for j in range(NPAIR):
                k = wi % NW
                wi += 1
                kb.dma(kb.sp, wup_s[k], wup[k][:, :, :], wup_d[j].rearrange("p (kc n) -> p kc n", kc=8),
                       rd=[wup_buf], wr=[wup_b[k]])
                cs = []
                for gv in range(2):
                    u = ui % 4
                    ui += 1
                    ch = gv * NPAIR + j
                    terms = [(wup[k][:, kc, gv * 128:(gv + 1) * 128], hnT[hb][:, kc, :], [wup_b[k]] + hnT_b[hb])
                             for kc in range(8)]
                    kb.mm_group(pu[u][:, :], pu_b[u], terms)
                    kb.op(kb.act, lambda: nc.scalar.activation(out=Cc[u][:, :], in_=pu[u][:, :], func=AF.Identity,
                                                               bias=cw[:, 3, ch:ch + 1], scale=cw[:, 2, ch:ch + 1]),
                          rd=[pu_b[u], cw_b], wr=[Cc_b[u]])
                    kb.op(kb.act, lambda: nc.scalar.copy(U[u][:, 2:514], pu[u][:, :]), rd=[pu_b[u]], wr=[U_b[u]])
                    kb.op(kb.pool, lambda: nc.gpsimd.tensor_copy(U[u][:, 0:2], halo[:, ch, :]),
                          rd=[halo_b[ch]], wr=[U_b[u]])
                    kb.op(kb.pool, lambda: nc.gpsimd.tensor_copy(halo[:, ch, :], U[u][:, 512:514]),
                          rd=[U_b[u]], wr=[halo_b[ch]])
                    kb.op(kb.pool, lambda: nc.gpsimd.tensor_scalar(T0[u][:, :], U[u][:, 0:512], cw[:, 0, ch:ch + 1], 0.0,
                                                                   ALU.mult, ALU.add),
                          rd=[U_b[u], cw_b], wr=[T0_b[u]])
                    kb.op(kb.dve, lambda: nc.vector.scalar_tensor_tensor(out=Cc[u][:, :], in0=U[u][:, 1:513],
                                                                         scalar=cw[:, 1, ch:ch + 1], in1=Cc[u][:, :],
                                                                         op0=ALU.mult, op1=ALU.add),
                          rd=[U_b[u], cw_b], wr=[Cc_b[u]])
                    kb.op(kb.dve, lambda: nc.vector.tensor_tensor(Cc[u][:, :], Cc[u][:, :], T0[u][:, :], ALU.add),
                          rd=[T0_b[u]], wr=[Cc_b[u]])
                    cs.append(u)
                ug, uv = cs
                sgi = j % 2
                kb.op(kb.act, lambda: nc.scalar.activation(out=Sg[sgi][:, :], in_=Cc[ug][:, :], func=AF.Silu),
                      rd=[Cc_b[ug]], wr=[Sg_b[sgi]])
                kb.op(kb.dve, lambda: nc.vector.tensor_tensor(Gw[:, j, :], Sg[sgi][:, :], Cc[uv][:, :], ALU.mult),
                      rd=[Sg_b[sgi], Cc_b[uv]], wr=[Gw_b[j]])
                if tb > 0 and j % 3 == 0 and j // 3 < 8:
                    down_group(tb - 1, j // 3)
            if tb + 1 < NBLK:
                norm_stage(tb + 1)
        for gi in range(8):
            down_group(NBLK - 1, gi)
        kb.barrier()


def host_wup(w):
    a = np.asarray(w, np.float32).reshape(8, 128, 2, NPAIR, 128)
    a = a.transpose(3, 1, 0, 2, 4)
    return np.ascontiguousarray(a).reshape(NPAIR * 128, 2048)


def host_cwb(cw, cb):
    a = np.concatenate([np.asarray(cw, np.float32), np.asarray(cb, np.float32)[None, :]], axis=0)
    a = a.reshape(4, 44, 128).transpose(2, 0, 1)
    return np.ascontiguousarray(a).reshape(128, 4 * 44)


def host_row(g):
    return np.ascontiguousarray(np.broadcast_to(np.asarray(g, np.float32)[None, :], (128, D)))


def host_chunks(w):
    w = np.asarray(w, np.float32)
    n = w.shape[1] // 128
    a = w.reshape(8, 128, n, 128).transpose(2, 1, 0, 3)
    return np.ascontiguousarray(a).reshape(n * 128, 1024)


def host_rows(w):
    w = np.asarray(w, np.float32)
    kc = w.shape[0] // 128
    a = w.reshape(kc, 128, w.shape[1]).transpose(1, 0, 2)
    return np.ascontiguousarray(a).reshape(128, kc * w.shape[1])


def consts_host():
    bf = ml_dtypes.bfloat16
    c = {}
    c["ident"] = np.eye(128, dtype=np.float32).astype(bf)
    c["ident32"] = np.eye(128, dtype=np.float32)
    p = np.arange(128)[:, None]
    f = np.arange(512)[None, :]
    nm = np.zeros((128, 4, 512), np.float32)
    le = np.zeros((128, 4, 512), np.float32)
    wn = np.zeros((128, 4, 512), np.float32)
    for j in range(4):
        nm[:, j, :] = np.where(f <= 128 * j + p, NEGM, 0.0)
        le[:, j, :] = np.where(128 * j + p > f, NEGM, 0.0)
        wn[:, j, :] = np.where(f >= 128 * j + p, NEGM, 0.0)
    c["negmask"] = nm.astype(bf)
    c["negmask_le"] = le.astype(bf)
    c["negmask_win"] = wn.astype(bf)
    cm = np.zeros((128, 5, 512), np.float32)
    for u in range(5):
        cm[:, u, :] = np.where(16 * p + 31 > 512 * u + f, NEGM, 0.0)
    c["negmask_cmp"] = cm.astype(bf)
    x = np.arange(512)[None, :]
    c["cmask_tm"] = np.where(16 * (x - 248) + 31 > p, NEGM, 0.0).astype(np.float32).astype(bf)
    jj = np.arange(128)[:, None]
    ss = np.arange(128)[None, :]
    c["uincneg"] = np.where(jj >= ss, -1.0, 0.0).astype(np.float32).astype(bf)
    os_ = np.zeros((128, 2, 128), np.float32)
    sl = np.zeros((128, 2, 128), np.float32)
    for hh in range(2):
        os_[:, hh, hh] = 1.0
        os_[:, hh, 32 + hh] = 1.0
        sl[hh, hh, :] = 1.0
        sl[32 + hh, hh, :] = 1.0
    c["onesel"] = os_.astype(bf)
    c["sel"] = sl.astype(bf)
    c["zeros512"] = np.zeros((128, 512), np.float32).astype(bf)
    cc = np.arange(256)[:, None] * 16
    s0 = np.arange(64)[None, :] * 64
    ov = np.clip(np.minimum(cc + 32, s0 + 64) - np.maximum(cc, s0), 0, None) / 32.0
    ov[255, :] = 0.0
    c["overlap"] = np.ascontiguousarray(ov.reshape(2, 128, 64).transpose(1, 0, 2)).astype(np.float32)
    bon = np.zeros((128, 192), np.float32)
    y = np.arange(128)[None, :]
    npr = y - 62
    cur = p // 64
    bon[:, 0:128] = np.where(npr > cur, -1.0e9, np.where((npr == cur) | (npr == cur - 1), 1.0e4, 0.0))
    bon[:, 128] = 1.0e4
    c["bonus"] = bon
    o64 = np.zeros((128, 128), np.float32)
    o64[64, :] = 1.0
    c["onesrow64"] = o64.astype(bf)
    gs = np.zeros((128, 48, 128), np.float32)
    for r in range(48):
        gs[r, r, :] = 1.0
    c["gsel"] = gs.astype(bf)
    half = 32
    inv = (10000.0 ** (-np.arange(half, dtype=np.float32) / half)).astype(np.float32)
    rc = np.zeros((128, 2), np.float32)
    rc[:, 0] = inv[np.arange(128) % 32]
    rc[:, 1] = np.where((np.arange(128) % 64) < 32, -1.0, 1.0)
    c["ropec"] = rc
    ef = (np.arange(S)[None, :] // 64 == np.arange(64)[:, None]).astype(np.float32)
    c["efull"] = ef.astype(bf)
    return c


CONST_SHAPES = {"ident": ([128, 128], BF16), "ident32": ([128, 128], F32), "negmask": ([128, 4, 512], BF16),
                "negmask_le": ([128, 4, 512], BF16), "negmask_win": ([128, 4, 512], BF16),
                "negmask_cmp": ([128, 5, 512], BF16), "cmask_tm": ([128, 512], BF16),
                "uincneg": ([128, 128], BF16), "onesel": ([128, 2, 128], BF16), "sel": ([128, 2, 128], BF16),
                "zeros512": ([128, 512], BF16),
                "overlap": ([128, 2, 64], F32), "bonus": ([128, 192], F32), "onesrow64": ([128, 128], BF16),
                "gsel": ([128, 48, 128], BF16), "ropec": ([128, 2], F32)}


SBA_CONSTS = ("ident", "negmask", "uincneg", "onesel", "sel", "zeros512")
NSA_CONSTS = ("ident", "ident32", "negmask_le", "negmask_win", "negmask_cmp", "cmask_tm", "overlap", "bonus", "onesrow64",
              "gsel", "ropec")
FFN_CONSTS = ("ident",)


def load_consts(kb, names=None, stack=None):
    nc = kb.nc
    if not hasattr(kb, "cdram"):
        kb.cdram = {}
        for n, (shp, dt) in CONST_SHAPES.items():
            kb.cdram[n] = nc.dram_tensor("c_" + n, shp, dt, kind="ExternalInput").ap()
        kb.cdram["efull_d"] = nc.dram_tensor("c_efull", [64, S], BF16, kind="ExternalInput").ap()
    stack = stack if stack is not None else kb.root
    C = {}
    sl = kb.slot(kb.name("consts"))
    for n, (shp, dt) in CONST_SHAPES.items():
        if names is not None and n not in names:
            continue
        d = kb.cdram[n]
        t = stack.enter_context(nc.sbuf_tensor(kb.name("cs_" + n), shp, dt))
        if len(shp) == 2:
            kb.dma(kb.sp, sl, t[:, :], d)
            C[n] = t[:, :]
        else:
            kb.dma(kb.sp, sl, t[:, :, :], d)
            C[n] = t[:, :, :]
    C["efull_d"] = kb.cdram["efull_d"]
    kb.barrier()
    return C


def stage_norm_all(kb, C, xin, xin_bufs, grow_d, hnT, hnT_b):
    nc = kb.nc
    with ExitStack() as st:
        sb = lambda n, shp, dt: st.enter_context(nc.sbuf_tensor(kb.name(n), shp, dt))
        ps = lambda n, shp, dt: st.enter_context(nc.psum_tensor(kb.name(n), shp, dt))
        grow = sb("grow", [128, 1024], F32)
        grow_b = Buf()
        xt = [sb("xt", [128, 1024], F32) for _ in range(8)]
        xt_b = [Buf() for _ in range(8)]
        xt_s = [kb.slot(kb.name("xt")) for _ in range(8)]
        hn = [sb("hn", [128, 1024], BF16) for _ in range(4)]
        hn_b = [Buf() for _ in range(4)]
        sq = sb("sq", [128, 1024], BF16)
        sq_b = Buf()
        st2 = [sb("st2", [128, 12], F32) for _ in range(2)]
        st2_b = [Buf() for _ in range(2)]
        pst = [ps("pst", [128, 8, 128], BF16) for _ in range(2)]
        pst_b = [Buf() for _ in range(2)]
        s_misc = kb.slot(kb.name("misc"))
        kb.dma(kb.pool, s_misc, grow[:, :], grow_d, wr=[grow_b])
        for tb in range(NT // 4):
            o = (tb % 2) * 4
            for t4 in range(4):
                tt = tb * 4 + t4
                kb.dma(kb.sp, xt_s[o + t4], xt[o + t4][:, :], xin[tt * 128:(tt + 1) * 128, :], rd=[xin_bufs[tt]],
                       wr=[xt_b[o + t4]])
            norm_block(kb, [xt[o + i][:, :] for i in range(4)], xt_b[o:o + 4], grow[:, :], grow_b,
                       [hn[i][:, :] for i in range(4)], hn_b[0:4], sq[:, :], sq_b, st2[tb % 2][:, :],
                       st2_b[tb % 2])
            for t4 in range(4):
                tt = tb * 4 + t4
                transpose_tile(kb, C, hn[t4], hn_b[t4], pst[tt % 2], pst_b[tt % 2],
                               hnT[:, :, tt * 128:(tt + 1) * 128], hnT_b[tt],
                               kb.act if tt % 2 == 0 else kb.dve)
        kb.barrier()


def stage_outproj(kb, C, oT, oT_b, wo_d, wo_buf, xin, xin_bufs, xout, xout_bufs):
    nc = kb.nc
    with ExitStack() as st:
        sb = lambda n, shp, dt: st.enter_context(nc.sbuf_tensor(kb.name(n), shp, dt))
        ps = lambda n, shp, dt: st.enter_context(nc.psum_tensor(kb.name(n), shp, dt))
        wo = sb("wo", [128, 8, 1024], BF16)
        wo_b = Buf()
        xr = [sb("xr", [128, 1024], F32) for _ in range(3)]
        xr_b = [Buf() for _ in range(3)]
        xr_s = [kb.slot(kb.name("xr")) for _ in range(3)]
        xo = [sb("xo", [128, 1024], F32) for _ in range(3)]
        xo_b = [Buf() for _ in range(3)]
        xo_s = [kb.slot(kb.name("xo")) for _ in range(3)]
        po = [ps("po", [128, 512], F32) for _ in range(4)]
        po_b = [Buf() for _ in range(4)]
        s_misc = kb.slot(kb.name("misc"))
        wo_v = wo_d.rearrange("p (c n) -> p c n", c=8)
        kb.dma(kb.pool, s_misc, wo[:, 0:4, :], wo_v[:, 0:4, :], rd=[wo_buf], wr=[wo_b])
        kb.dma(kb.pool, s_misc, wo[:, 4:8, :], wo_v[:, 4:8, :], rd=[wo_buf], wr=[wo_b])
        kb.batch_end(s_misc, [wo_b])
        for tt in range(NT):
            k = tt % 3
            kb.dma(kb.sp, xr_s[k], xr[k][:, :], xin[tt * 128:(tt + 1) * 128, :], rd=[xin_bufs[tt]], wr=[xr_b[k]])
            for nh in range(2):
                pi = (tt * 2 + nh) % 4
                terms = [(oT[:, c, tt * 128:(tt + 1) * 128], wo[:, c, nh * 512:(nh + 1) * 512], [oT_b[tt // 4], wo_b])
                         for c in range(8)]
                kb.mm_group(po[pi][:, :], po_b[pi], terms)
                kb.op(kb.dve, lambda: nc.vector.tensor_tensor(xo[k][:, nh * 512:(nh + 1) * 512], po[pi][:, :],
                                                              xr[k][:, nh * 512:(nh + 1) * 512], ALU.add),
                      rd=[po_b[pi], xr_b[k]], wr=[xo_b[k]])
            kb.dma(kb.pool, xo_s[k], xout[tt * 128:(tt + 1) * 128, :], xo[k][:, :], rd=[xo_b[k]], wr=[xout_bufs[tt]])
        kb.barrier()


def phase_sba(kb, C, xin, xin_bufs, xout, xout_bufs, grow_d, wqk_d, wqk_buf, wv_d, wv_buf, wo_d, wo_buf,
              ngroups=8, nq=8, bg=None):
    nc = kb.nc
    with ExitStack() as st0:
        sb0 = lambda n, shp, dt: st0.enter_context(nc.sbuf_tensor(kb.name(n), shp, dt))
        oT = sb0("oT", [128, 8, S], BF16)
        oT_b = [Buf() for _ in range(8)]
        if ngroups < 8 or nq < 8:
            kb.op(kb.pool, lambda: nc.gpsimd.memset(oT[:, :, :], 0.0), wr=oT_b)
        with ExitStack() as st1:
            sb1 = lambda n, shp, dt: st1.enter_context(nc.sbuf_tensor(kb.name(n), shp, dt))
            hnT = sb1("hnT", [128, 8, S], BF16)
            hnT_b = [Buf() for _ in range(NT)]
            stage_norm_all(kb, C, xin, xin_bufs, grow_d, hnT, hnT_b)
            with ExitStack() as st:
                sb = lambda n, shp, dt: st.enter_context(nc.sbuf_tensor(kb.name(n), shp, dt))
                ps = lambda n, shp, dt: st.enter_context(nc.psum_tensor(kb.name(n), shp, dt))
                wg = [sb("wg", [128, 3, 8, 128], BF16) for _ in range(1)]
                wg_b = [Buf() for _ in range(1)]
                wg_s = [kb.slot(kb.name("wg")) for _ in range(1)]
                qz = sb("qz", [128, 2, S], BF16)
                kT = sb("kT", [128, S], BF16)
                Vt = sb("Vt", [128, NT, 2, 128], BF16)
                qz_b = [Buf() for _ in range(8)]
                kT_b = [Buf() for _ in range(8)]
                Vt_b = [Buf() for _ in range(NT)]
                NB3 = 3
                E = [sb("E", [128, 2, 512], F32) for _ in range(2)]
                E_b = [Buf() for _ in range(2)]
                SP = [sb("SP", [128, 2, 512], BF16) for _ in range(2)]
                SP_b = [Buf() for _ in range(2)]
                A = [sb("A", [128, 2, 512], BF16) for _ in range(2)]
                A_b = [Buf() for _ in range(2)]
                R34 = sb("R34", [34, 512], F32)
                R34_b = Buf()
                RHL = [sb("RHL", [128, 512], BF16) for _ in range(2)]
                RHL_b = [Buf() for _ in range(2)]
                pz = [ps("pz", [128, 2, 512], F32) for _ in range(NB3)]
                pz_b = [Buf() for _ in range(NB3)]
                pr = ps("pr", [128, 512], F32)
                pr_b = Buf()
                po = ps("po", [128, 512], F32)
                po_b = Buf()
                pq = [pz[0][:, 0, :], pz[0][:, 1, :], pz[1][:, 0, :], pz[1][:, 1, :]]
                pq_b = [pz_b[0], pz_b[0], pz_b[1], pz_b[1]]

                def load_w(c):
                    k = 0
                    kb.dma(kb.sp, wg_s[k], wg[k][:, 0, :, :], wqk_d[c].rearrange("p (kc n) -> p kc n", kc=8),
                           rd=[wqk_buf], wr=[wg_b[k]])
                    kb.dma(kb.sp, wg_s[k], wg[k][:, 1, :, :], wqk_d[8 + c].rearrange("p (kc n) -> p kc n", kc=8),
                           rd=[wqk_buf], wr=[wg_b[k]])
                    wv_v = wv_d.rearrange("p (kc n) -> p kc n", kc=8)
                    for k0_ in range(0, 8, 2):
                        kb.dma(kb.sp, wg_s[k], wg[k][:, 2, k0_:k0_ + 2, :], wv_v[:, k0_:k0_ + 2, c * 128:(c + 1) * 128],
                               rd=[wv_buf], wr=[wg_b[k]])

                kb.op(kb.pool, lambda: nc.gpsimd.memset(Vt[:, :, :, :], 0.0), wr=Vt_b)
                kb.op(kb.pool, lambda: nc.gpsimd.memset(qz[:, :, :], 0.0), wr=qz_b)
                for i_ in range(2):
                    kb.op(kb.pool, lambda: nc.gpsimd.memset(RHL[i_][:, :], 0.0), wr=[RHL_b[i_]])
                load_w(0)
                qi = 0
                for c in range(ngroups):
                    k = 0
                    for which in (0, 1):
                        for tg in range(8):
                            p = qi % 4
                            qi += 1
                            ts_ = slice(tg * 512, (tg + 1) * 512)
                            terms = [(wg[k][:, which, kc, :], hnT[:, kc, ts_],
                                      [wg_b[k]] + hnT_b[tg * 4:(tg + 1) * 4]) for kc in range(8)]
                            kb.mm_group(pq[p], pq_b[p], terms)
                            if which == 0:
                                kb.op(kb.act, lambda: nc.scalar.mul(qz[0:64, 0, ts_], pq[p][0:64, :], 0.125),
                                      rd=[pq_b[p]], wr=[qz_b[tg]])
                                kb.op(kb.act, lambda: nc.scalar.mul(qz[64:128, 1, ts_], pq[p][64:128, :], 0.125),
                                      rd=[pq_b[p]], wr=[qz_b[tg]])
                            else:
                                kb.op(kb.dve, lambda: nc.vector.tensor_copy(kT[:, ts_], pq[p]),
                                      rd=[pq_b[p]], wr=[kT_b[tg]])
                    for tt4 in range(NT // 4):
                        p = qi % 4
                        qi += 1
                        pe = kb.pe
                        pe.wait(pq_b[p].wr_deps(), wg_b[k].w, [hnT_b[tt4 * 4 + i].w for i in range(4)])
                        ins = None
                        for i in range(4):
                            tt = tt4 * 4 + i
                            for kc in range(8):
                                ins = nc.tensor.matmul(pq[p][:, i * 128:(i + 1) * 128], hnT[:, kc, tt * 128:(tt + 1) * 128],
                                                       wg[k][:, 2, kc, :], start=(kc == 0), stop=(kc == 7))
                        tok = pe.done(ins)
                        wg_b[k].note_read(tok)
                        pq_b[p].note_write(tok)
                        src = pq[p].rearrange("p (i n) -> p i n", i=4)
                        kb.op(kb.act, lambda: nc.scalar.copy(Vt[:, tt4 * 4:(tt4 + 1) * 4, 0, 0:64], src[:, :, 0:64]),
                              rd=[pq_b[p]], wr=Vt_b[tt4 * 4:(tt4 + 1) * 4])
                        kb.op(kb.act, lambda: nc.scalar.copy(Vt[:, tt4 * 4:(tt4 + 1) * 4, 1, 64:128], src[:, :, 64:128]),
                              rd=[pq_b[p]], wr=Vt_b[tt4 * 4:(tt4 + 1) * 4])
                    if c + 1 < ngroups:
                        load_w(c + 1)
                    for g in range(nq):
                        qs = slice(g * 512, (g + 1) * 512)
                        nsteps = 4 * g + 4
                        kts = [4 * g + 3 - i for i in range(nsteps)]
                        kb.op(kb.dve, lambda: nc.vector.memset(R34[:, :], 0.0), wr=[R34_b])
                        kb.op(kb.dve, lambda: nc.vector.memset(RHL[0][0:34, :], 0.0), wr=[RHL_b[0]])

                        def c0_of(i):
                            return 128 * (3 - i) if i < 4 else 0

                        def emit_Z(i):
                            kt = kts[i]
                            ks = slice(kt * 128, (kt + 1) * 128)
                            b3 = i % NB3
                            c0 = c0_of(i)
                            pe = kb.pe
                            pe.wait(pz_b[b3].wr_deps(), kT_b[kt // 4].w, qz_b[g].w)
                            ins = None
                            diag = kt >= 4 * g
                            for hh in range(2):
                                ins = nc.tensor.matmul(pz[b3][:, hh, c0:512], kT[:, ks], qz[:, hh, g * 512 + c0:(g + 1) * 512],
                                                       start=True, stop=not diag)
                                if diag:
                                    ins = nc.tensor.matmul(pz[b3][:, hh, c0:512], C["ident"],
                                                           C["negmask"][:, kt - 4 * g, c0:512], start=False, stop=True)
                            tok = pe.done(ins)
                            kT_b[kt // 4].note_read(tok)
                            qz_b[g].note_read(tok)
                            pz_b[b3].note_write(tok)
                            e = i % 2
                            kb.op(kb.act, lambda: nc.scalar.activation(out=E[e][:, :, c0:512], in_=pz[b3][:, :, c0:512],
                                                                       func=AF.Exp),
                                  rd=[pz_b[b3]], wr=[E_b[e]])
                            kb.op(kb.act, lambda: nc.scalar.activation(out=SP[e][:, :, c0:512], in_=E[e][:, :, c0:512],
                                                                       func=AF.Ln, bias=1.0),
                                  rd=[E_b[e]], wr=[SP_b[e]])

                        def emit_R(i):
                            b3 = i % 2
                            c0 = c0_of(i)
                            terms = [(C["onesel"][:, hh, :], SP[b3][:, hh, c0:512], [SP_b[b3]]) for hh in range(2)]
                            kb.mm_group(pr[:, c0:512], pr_b, terms)
                            kb.op(kb.dve, lambda: nc.vector.tensor_tensor(R34[:, c0:512], R34[:, c0:512], pr[0:34, c0:512],
                                                                          ALU.subtract),
                                  rd=[pr_b], wr=[R34_b])
                            nx = RHL[(i + 1) % 2]
                            nx_b = RHL_b[(i + 1) % 2]
                            kb.op(kb.dve, lambda: nc.vector.tensor_copy(nx[0:34, :], R34[:, :]), rd=[R34_b], wr=[nx_b])
                            kb.op(kb.dve, lambda: nc.vector.tensor_tensor(nx[32:34, :], R34[32:34, :], nx[32:34, :],
                                                                          ALU.subtract),
                                  rd=[R34_b], wr=[nx_b])

                        def emit_C(i):
                            b3 = i % NB3
                            b2 = i % 2
                            c0 = c0_of(i)
                            pe = kb.pe
                            pe.wait(pz_b[b3].wr_deps(), SP_b[b2].w, RHL_b[i % 2].w)
                            ins = None
                            for hh in range(2):
                                nc.tensor.matmul(pz[b3][:, hh, c0:512], C["uincneg"], SP[b2][:, hh, c0:512], start=False,
                                                 stop=False, skip_group_check=True)
                                ins = nc.tensor.matmul(pz[b3][:, hh, c0:512], C["sel"][:, hh, :], RHL[i % 2][:, c0:512],
                                                       start=False, stop=True, skip_group_check=True)
                            tok = pe.done(ins)
                            SP_b[b2].note_read(tok)
                            RHL_b[i % 2].note_read(tok)
                            pz_b[b3].note_write(tok)
                            kb.op(kb.act, lambda: nc.scalar.activation(out=A[b2][:, :, c0:512], in_=pz[b3][:, :, c0:512],
                                                                       func=AF.Exp),
                                  rd=[pz_b[b3]], wr=[A_b[b2]])

                        def emit_AV(i):
                            kt = kts[i]
                            b3 = i % 2
                            c0 = c0_of(i)
                            pe = kb.pe
                            deps = [A_b[b3].w, Vt_b[kt].w]
                            if i == 0:
                                deps += po_b.wr_deps()
                            pe.wait(deps)
                            ins = None
                            if i == 0:
                                nc.tensor.matmul(po[:, :], C["ident"], C["zeros512"], start=True, stop=False,
                                                 skip_group_check=True)
                            for hh in range(2):
                                ins = nc.tensor.matmul(po[:, c0:512], Vt[:, kt, hh, :], A[b3][:, hh, c0:512],
                                                       start=False, stop=(i == nsteps - 1 and hh == 1),
                                                       skip_group_check=True)
                            tok = pe.done(ins)
                            A_b[b3].note_read(tok)
                            Vt_b[kt].note_read(tok)
                            if i == nsteps - 1:
                                po_b.note_write(tok)

                        lvl = DBG.get("lvl", 9)
                        emit_Z(0)
                        for i in range(nsteps):
                            if i + 1 < nsteps:
                                emit_Z(i + 1)
                                emit_R(i)
                            emit_C(i)
                            if i >= 1:
                                emit_AV(i - 1)
                        emit_AV(nsteps - 1)
                        kb.op(kb.dve, lambda: nc.vector.tensor_copy(oT[:, c, qs], po[:, :]), rd=[po_b], wr=[oT_b[g]])
                        if bg is not None:
                            bg.emit(6)
                if bg is not None:
                    bg.flush()
                kb.barrier()
        stage_outproj(kb, C, oT, oT_b, wo_d, wo_buf, xin, xin_bufs, xout, xout_bufs)


TWO_PI = 6.283185307179586
NCMP = 255


def nsa_stage_proj(kb, C, hnT, hnT_b, posb_d, wfm_d, wfm_buf, wgt_d, wgt_buf, wtm_d, wtm_buf,
                   q_s, q_sb, kv_s, kv_sb, Vtm, Vtm_b, sigT, sigT_b, cmpw, kcmp, kcmp_b, vcmp, vcmp_b):
    nc = kb.nc
    with ExitStack() as st:
        sb = lambda n, shp, dt: st.enter_context(nc.sbuf_tensor(kb.name(n), shp, dt))
        ps = lambda n, shp, dt: st.enter_context(nc.psum_tensor(kb.name(n), shp, dt))
        cosF = sb("cosF", [128, S], F32)
        sinS = sb("sinS", [128, S], F32)
        cs_b = Buf()
        t_b = Buf()
        s_misc = kb.slot(kb.name("misc"))
        s_miscp = kb.slot(kb.name("miscp"))
        st_tmp = ExitStack()
        HS = S // 2
        posi = st_tmp.enter_context(nc.sbuf_tensor(kb.name("posi"), [128, HS], I32))
        ang = st_tmp.enter_context(nc.sbuf_tensor(kb.name("ang"), [128, HS], F32))
        tmp = st_tmp.enter_context(nc.sbuf_tensor(kb.name("tmpang"), [128, HS], F32))
        kf = st_tmp.enter_context(nc.sbuf_tensor(kb.name("kfang"), [128, HS], F32))
        C1 = 6.28125
        C2 = TWO_PI - 6.28125

        def reduce_sin(dst, shift, post_scale):
            if shift != 0.0:
                kb.op(kb.dve, lambda: nc.vector.tensor_scalar(tmp[:, :], ang[:, :], shift, None, ALU.add),
                      rd=[t_b], wr=[t_b])
                src = tmp
            else:
                src = ang
            kb.op(kb.dve, lambda: nc.vector.tensor_scalar(kf[:, :], src[:, :], 1.0 / TWO_PI, None, ALU.mult),
                  rd=[t_b], wr=[t_b])
            kb.op(kb.dve, lambda: nc.vector.tensor_copy(posi[:, :], kf[:, :]), rd=[t_b], wr=[t_b])
            kb.op(kb.dve, lambda: nc.vector.tensor_copy(kf[:, :], posi[:, :]), rd=[t_b], wr=[t_b])
            kb.op(kb.dve, lambda: nc.vector.scalar_tensor_tensor(out=tmp[:, :], in0=kf[:, :], scalar=-C1, in1=src[:, :],
                                                                 op0=ALU.mult, op1=ALU.add), rd=[t_b], wr=[t_b])
            kb.op(kb.dve, lambda: nc.vector.scalar_tensor_tensor(out=tmp[:, :], in0=kf[:, :], scalar=-C2, in1=tmp[:, :],
                                                                 op0=ALU.mult, op1=ALU.add), rd=[t_b], wr=[t_b])
            kb.op(kb.dve, lambda: nc.vector.tensor_scalar(kf[:, :], tmp[:, :], float(np.pi), TWO_PI, ALU.is_gt, ALU.mult),
                  rd=[t_b], wr=[t_b])
            kb.op(kb.dve, lambda: nc.vector.tensor_tensor(tmp[:, :], tmp[:, :], kf[:, :], ALU.subtract),
                  rd=[t_b], wr=[t_b])
            kb.op(kb.dve, lambda: nc.vector.tensor_scalar(kf[:, :], tmp[:, :], -float(np.pi), TWO_PI, ALU.is_lt, ALU.mult),
                  rd=[t_b], wr=[t_b])
            kb.op(kb.dve, lambda: nc.vector.tensor_tensor(tmp[:, :], tmp[:, :], kf[:, :], ALU.add),
                  rd=[t_b], wr=[t_b])
            kb.op(kb.act, lambda: nc.scalar.activation(out=tmp[:, :], in_=tmp[:, :], func=AF.Sin), rd=[t_b], wr=[t_b])
            if post_scale is None:
                kb.op(kb.dve, lambda: nc.vector.tensor_copy(dst, tmp[:, :]), rd=[t_b], wr=[cs_b])
            else:
                kb.op(kb.dve, lambda: nc.vector.tensor_scalar(dst, tmp[:, :], post_scale, None, ALU.mult),
                      rd=[t_b], wr=[cs_b])

        for hf in range(2):
            cols = slice(hf * HS, (hf + 1) * HS)
            kb.dma(kb.sp, s_misc, posi[:, :], posb_d[:, cols], wr=[t_b])
            kb.op(kb.dve, lambda: nc.vector.tensor_copy(ang[:, :], posi[:, :]), rd=[t_b], wr=[t_b])
            kb.op(kb.dve, lambda: nc.vector.tensor_scalar(ang[:, :], ang[:, :], C["ropec"][:, 0:1], None, ALU.mult),
                  rd=[t_b], wr=[t_b])
            reduce_sin(sinS[:, cols], 0.0, C["ropec"][:, 1:2])
            reduce_sin(cosF[:, cols], float(np.pi / 2), None)
        kb.barrier()
        st_tmp.close()

        st_p = ExitStack()
        sbp = lambda n, shp, dt: st_p.enter_context(nc.sbuf_tensor(kb.name(n), shp, dt))
        wch = [sbp("wch", [128, 2, 8, 128], BF16) for _ in range(2)]
        wch_b = [Buf() for _ in range(2)]
        wch_s = [kb.slot(kb.name("wch")) for _ in range(2)]
        wgt = sbp("wgt", [128, 8, 48], BF16)
        wtm = sbp("wtm", [128, 8, 256], BF16)
        wgt_b = Buf()
        wtm_b = Buf()
        kb.dma(kb.pool, s_miscp, wgt[:, :, :], wgt_d.rearrange("p (kc n) -> p kc n", kc=8), rd=[wgt_buf], wr=[wgt_b])
        kb.dma(kb.pool, s_miscp, wtm[:, :, :], wtm_d.rearrange("p (kc n) -> p kc n", kc=8), rd=[wtm_buf], wr=[wtm_b])
        kb.batch_end(s_miscp, [wgt_b, wtm_b])
        pa = [ps("pa", [128, 512], F32) for _ in range(2)]
        pa_b = [Buf() for _ in range(2)]
        pb = [ps("pb", [128, 512], F32) for _ in range(2)]
        pb_b = [Buf() for _ in range(2)]
        t1 = [sbp("t1", [128, 512], F32) for _ in range(2)]
        t1_b = [Buf() for _ in range(2)]
        t2 = [sbp("t2", [128, 512], F32) for _ in range(2)]
        t2_b = [Buf() for _ in range(2)]
        ro = [sbp("ro", [128, 512], BF16) for _ in range(3)]
        ro_b = [Buf() for _ in range(3)]
        ro_s = [kb.slot(kb.name("ro")) for _ in range(3)]
        jobs = [(c, 8 + c, ("q", c)) for c in range(8)]
        jobs += [(16, None, ("kv", 0)), (17, None, ("kv", 1)), (18, 19, ("kv", 2)), (20, 21, ("kv", 3))]
        ci = 0
        ri = 0
        PL = DBG.get("proj", 9)
        for (cp, cq, dest) in (jobs if PL >= 2 else []):
            k = ci % 2
            ci += 1
            kb.dma(kb.sp, wch_s[k], wch[k][:, 0, :, :], wfm_d[cp].rearrange("p (kc n) -> p kc n", kc=8),
                   rd=[wfm_buf], wr=[wch_b[k]])
            if cq is not None:
                kb.dma(kb.sp, wch_s[k], wch[k][:, 1, :, :], wfm_d[cq].rearrange("p (kc n) -> p kc n", kc=8),
                       rd=[wfm_buf], wr=[wch_b[k]])
            for tg in range(8):
                ts_ = slice(tg * 512, (tg + 1) * 512)
                p = tg % 2
                r = ri % 3
                ri += 1
                terms = [(wch[k][:, 0, kc, :], hnT[:, kc, ts_], [wch_b[k]] + hnT_b[tg * 4:(tg + 1) * 4]) for kc in range(8)]
                kb.mm_group(pa[p][:, :], pa_b[p], terms)
                if cq is not None:
                    terms = [(wch[k][:, 1, kc, :], hnT[:, kc, ts_], [wch_b[k]] + hnT_b[tg * 4:(tg + 1) * 4])
                             for kc in range(8)]
                    kb.mm_group(pb[p][:, :], pb_b[p], terms)
                    kb.op(kb.dve, lambda: nc.vector.tensor_tensor(t1[p][:, :], pa[p][:, :], cosF[:, ts_], ALU.mult),
                          rd=[pa_b[p], cs_b], wr=[t1_b[p]])
                    kb.op(kb.dve, lambda: nc.vector.tensor_tensor(t2[p][:, :], pb[p][:, :], sinS[:, ts_], ALU.mult),
                          rd=[pb_b[p], cs_b], wr=[t2_b[p]])
                    kb.op(kb.pool, lambda: nc.gpsimd.tensor_tensor(ro[r][:, :], t1[p][:, :], t2[p][:, :], ALU.add),
                          rd=[t1_b[p], t2_b[p]], wr=[ro_b[r]])
                else:
                    kb.op(kb.act, lambda: nc.scalar.copy(ro[r][:, :], pa[p][:, :]), rd=[pa_b[p]], wr=[ro_b[r]])
                if dest[0] == "q":
                    c = dest[1]
                    kb.dma(kb.pool, ro_s[r], q_s[2 * c:2 * c + 2, :, ts_].rearrange("h d t -> (h d) t"), ro[r][:, :],
                           rd=[ro_b[r]], wr=[q_sb])
                else:
                    kb.dma(kb.pool, ro_s[r], kv_s[dest[1], :, ts_], ro[r][:, :], rd=[ro_b[r]], wr=[kv_sb])
        kb.op(kb.pool, lambda: nc.gpsimd.memset(sigT[:, :], 0.0), wr=[sigT_b])
        for tg in range(8 if PL >= 3 else 0):
            ts_ = slice(tg * 512, (tg + 1) * 512)
            p = tg % 2
            terms = [(wgt[:, kc, :], hnT[:, kc, ts_], [wgt_b] + hnT_b[tg * 4:(tg + 1) * 4]) for kc in range(8)]
            kb.mm_group(pa[p][0:48, :], pa_b[p], terms)
            kb.op(kb.act, lambda: nc.scalar.activation(out=sigT[0:48, ts_], in_=pa[p][0:48, :], func=AF.Sigmoid),
                  rd=[pa_b[p]], wr=[sigT_b])
        kb.op(kb.pool, lambda: nc.gpsimd.memset(Vtm[:, :, :, :], 1.0), wr=Vtm_b)
        for tt2 in range(NT // 2 if PL >= 4 else 0):
            p = tt2 % 2
            pe = kb.pe
            pe.wait(pb_b[p].wr_deps(), wtm_b.w, [hnT_b[tt2 * 2 + i].w for i in range(2)])
            ins = None
            for i in range(2):
                tt = tt2 * 2 + i
                for kc in range(8):
                    ins = nc.tensor.matmul(pb[p][:, i * 256:(i + 1) * 256], hnT[:, kc, tt * 128:(tt + 1) * 128],
                                           wtm[:, kc, :], start=(kc == 0), stop=(kc == 7))
            tok = pe.done(ins)
            wtm_b.note_read(tok)
            pb_b[p].note_write(tok)
            src = pb[p][:, :].rearrange("p (i g d) -> p i g d", i=2, g=4)
            eng = kb.act if tt2 % 2 == 0 else kb.dve
            for i in range(2):
                if True:
                    kb.op(kb.act, lambda: nc.scalar.copy(Vtm[:, tt2 * 2 + i, :, 0:64], src[:, i, :, :]),
                          rd=[pb_b[p]], wr=[Vtm_b[tt2 * 2 + i]])
                else:
                    kb.op(kb.dve, lambda: nc.vector.tensor_copy(Vtm[:, tt2 * 2 + i, :, 0:64], src[:, i, :, :]),
                          rd=[pb_b[p]], wr=[Vtm_b[tt2 * 2 + i]])
        kb.barrier()
        st_p.close()
        t1 = [sb("t1c", [64, 256], F32)]
        t2 = [sb("t2c", [64, 256], F32)]
        t1_b = [Buf()]
        t2_b = [Buf()]
        (w1_d, w1_buf, w2_d, w2_buf, pe_d, pe_buf) = cmpw
        W2 = sb("W2", [128, 192], BF16)
        peT = sb("peT", [64, 2, 32], BF16)
        w_b = Buf()
        kb.dma(kb.sp, s_misc, W2[:, :], w2_d, rd=[w2_buf], wr=[w_b])
        kb.dma(kb.sp, s_misc, peT[:, :, :], pe_d.rearrange("p (k l) -> p k l", k=2), rd=[pe_buf], wr=[w_b])
        Xs = [sb("Xc", [64, S], BF16) for _ in range(2)]
        X_bs = [Buf() for _ in range(2)]
        X_ss = [kb.slot(kb.name("Xc")) for _ in range(2)]
        W1s = [sb("W1", [64, 32, 128], BF16) for _ in range(2)]
        hb = sb("hb", [128, 1], F32)
        u = sb("cu", [128, NCMP], F32)
        u2 = sb("cu2", [128, NCMP], F32)
        sg = sb("csg", [128, NCMP], F32)
        hd = sb("chd", [128, 256], BF16)
        c_b = Buf()
        ph = ps("ph", [128, 256], F32)
        ph_b = Buf()
        phb = ps("phb", [128, 2], F32)
        phb_b = Buf()
        kb.op(kb.pool, lambda: nc.gpsimd.memset(kcmp[:, :, :], 0.0), wr=[kcmp_b])
        kb.op(kb.pool, lambda: nc.gpsimd.memset(vcmp[:, :, :, :], 1.0), wr=[vcmp_b])
        for kv in range(2 if PL >= 5 else 0):
            W1 = W1s[kv]
            kb.dma(kb.sp, s_misc, W1[:, :, :], w1_d[kv].rearrange("p (l n) -> p l n", l=32), rd=[w1_buf], wr=[w_b])
            for g in range(2):
                X = Xs[g]
                X_b = X_bs[g]
                kb.dma(kb.sp, X_ss[g], X[:, :], kv_s[kv, g * 64:(g + 1) * 64, :], rd=[kv_sb], wr=[X_b])
                terms = [(W1[:, l, :], X[:, l:l + 16 * (NCMP - 1) + 1:16], [w_b, X_b]) for l in range(32)]
                kb.mm_group(ph[:, 0:NCMP], ph_b, terms)
                terms = [(W1[:, l, :], peT[:, kv, l:l + 1], [w_b]) for l in range(32)]
                kb.mm_group(phb[:, 0:1], phb_b, terms)
                kb.op(kb.dve, lambda: nc.vector.tensor_copy(hb[:, :], phb[:, 0:1]), rd=[phb_b], wr=[c_b])
                kb.op(kb.act, lambda: nc.scalar.activation(out=u[:, :], in_=ph[:, 0:NCMP], func=AF.Identity,
                                                           bias=hb[:, 0:1]), rd=[ph_b, c_b], wr=[c_b])
                kb.op(kb.dve, lambda: nc.vector.tensor_tensor(u2[:, :], u[:, :], u[:, :], ALU.mult), rd=[c_b], wr=[c_b])
                kb.op(kb.dve, lambda: nc.vector.tensor_scalar(u2[:, :], u2[:, :], 0.044715, 1.0, ALU.mult, ALU.add),
                      rd=[c_b], wr=[c_b])
                kb.op(kb.dve, lambda: nc.vector.tensor_tensor(u2[:, :], u2[:, :], u[:, :], ALU.mult), rd=[c_b], wr=[c_b])
                kb.op(kb.act, lambda: nc.scalar.activation(out=sg[:, :], in_=u2[:, :], func=AF.Sigmoid,
                                                           scale=1.5957691216057308), rd=[c_b], wr=[c_b])
                kb.op(kb.dve, lambda: nc.vector.tensor_tensor(hd[:, 0:NCMP], u[:, :], sg[:, :], ALU.mult),
                      rd=[c_b], wr=[c_b])
                if kv == 0:
                    kb.mm_group(pa[0][0:64, 0:NCMP], pa_b[0], [(W2[:, 0:64], hd[:, 0:NCMP], [w_b, c_b])])
                    kb.mm_group(pb[0][0:64, 0:NCMP], pb_b[0], [(W2[:, 64:128], hd[:, 0:NCMP], [w_b, c_b])])
                    cs_sl = slice(31, 31 + 16 * (NCMP - 1) + 1, 16)
                    kb.op(kb.dve, lambda: nc.vector.tensor_tensor(t1[0][0:64, 0:NCMP], pa[0][0:64, 0:NCMP],
                                                                  cosF[0:64, cs_sl], ALU.mult),
                          rd=[pa_b[0], cs_b], wr=[t1_b[0]])
                    kb.op(kb.dve, lambda: nc.vector.tensor_tensor(t2[0][0:64, 0:NCMP], pb[0][0:64, 0:NCMP],
                                                                  sinS[0:64, cs_sl], ALU.mult),
                          rd=[pb_b[0], cs_b], wr=[t2_b[0]])
                    kb.op(kb.dve, lambda: nc.vector.tensor_tensor(kcmp[0:64, g, 0:NCMP], t1[0][0:64, 0:NCMP],
                                                                  t2[0][0:64, 0:NCMP], ALU.add),
                          rd=[t1_b[0], t2_b[0]], wr=[kcmp_b])
                else:
                    for ct in range(2):
                        rows = 128 if ct == 0 else NCMP - 128
                        kb.mm_group(pa[1][0:rows, 0:64], pa_b[1],
                                    [(hd[:, ct * 128:ct * 128 + rows], W2[:, 128:192], [w_b, c_b])])
                        kb.op(kb.dve, lambda: nc.vector.tensor_copy(vcmp[0:rows, ct, g, 0:64], pa[1][0:rows, 0:64]),
                              rd=[pa_b[1]], wr=[vcmp_b])
        kb.barrier()


def nsa_stage_select(kb, C, q_s, q_sb, kcmp, kcmp_b, mb_s, mb_sb, nqt=NT):
    nc = kb.nc
    with ExitStack() as st:
        sb = lambda n, shp, dt: st.enter_context(nc.sbuf_tensor(kb.name(n), shp, dt))
        ps = lambda n, shp, dt: st.enter_context(nc.psum_tensor(kb.name(n), shp, dt))
        Q8 = sb("Q8", [64, 8, S], BF16)
        Q8_b = Buf()
        MBT = sb("MBT", [64, S], BF16)
        MBT_b = Buf()
        P = [[sb("P1", [128, NCMP], F32) for _ in range(8)] for _ in range(3)]
        P_b = [[Buf() for _ in range(8)] for _ in range(3)]
        den = [sb("den1", [128, 16], F32) for _ in range(3)]
        den_b = [Buf() for _ in range(3)]
        accs = [sb("acc1", [128, 256], F32) for _ in range(2)]
        accs_b = [Buf() for _ in range(2)]
        accTs = [sb("accT", [128, 2, 128], F32) for _ in range(2)]
        accTs_b = [Buf() for _ in range(2)]
        score = sb("score", [128, 64], F32)
        work = sb("work", [128, 64], F32)
        m8 = sb("m8", [128, 16], F32)
        thr = sb("thr", [128, 1], F32)
        mbq = sb("mbq", [128, 64], BF16)
        sc_b = Buf()
        NPS = 3
        pss = [ps("pss", [128, 256], F32) for _ in range(NPS)]
        pss_b = [Buf() for _ in range(NPS)]
        pts = [ps("pt1", [128, 2, 128], F32) for _ in range(2)]
        pts_b = [Buf() for _ in range(2)]
        pimps = [ps("pimp", [128, 64], F32) for _ in range(2)]
        pimps_b = [Buf() for _ in range(2)]
        pmt = ps("pmt", [64, 128], BF16)
        pmt_b = Buf()
        s_misc = kb.slot(kb.name("misc"))
        s_out = kb.slot(kb.name("mbout"))
        hi = 0
        for g in range(2):
            for j in range(8):
                kb.dma(kb.sp, s_misc, Q8[:, j, :], q_s[g * 8 + j], rd=[q_sb], wr=[Q8_b])
            for i_ in range(2):
                kb.op(kb.dve, lambda: nc.vector.memset(accs[i_][:, :], 0.0), wr=[accs_b[i_]])

            def stage_a(qt):
                nonlocal hi
                qs = slice(qt * 128, (qt + 1) * 128)
                b2 = qt % 3
                for j in range(8):
                    p4 = hi % NPS
                    hi += 1
                    terms = [(Q8[:, j, qs], kcmp[0:64, g, 0:NCMP], [Q8_b, kcmp_b]),
                             (C["ident"], C["cmask_tm"][:, 248 - 8 * qt:248 - 8 * qt + NCMP], [])]
                    kb.mm_group(pss[p4][:, 0:NCMP], pss_b[p4], terms)
                    kb.op(kb.act, lambda: nc.scalar.activation(out=P[b2][j][:, :], in_=pss[p4][:, 0:NCMP], func=AF.Exp,
                                                               scale=0.125, accum_out=den[b2][:, j:j + 1]),
                          rd=[pss_b[p4]], wr=[P_b[b2][j], den_b[b2]])

            def stage_b1(qt):
                qs = slice(qt * 128, (qt + 1) * 128)
                b2 = qt % 2
                acc, acc_b, accT, accT_b = accs[b2], accs_b[b2], accTs[b2], accTs_b[b2]
                pt, pt_b, pimp, pimp_b = pts[b2], pts_b[b2], pimps[b2], pimps_b[b2]
                b2 = qt % 3
                kb.op(kb.dve, lambda: nc.vector.tensor_scalar(den[b2][:, 8:16], den[b2][:, 0:8], 1e-30, None, ALU.add),
                      rd=[den_b[b2]], wr=[den_b[b2]])
                kb.op(kb.dve, lambda: nc.vector.reciprocal(den[b2][:, 8:16], den[b2][:, 8:16]),
                      rd=[den_b[b2]], wr=[den_b[b2]])
                for j in range(8):
                    if j == 0:
                        kb.op(kb.dve, lambda: nc.vector.tensor_scalar(acc[:, 0:NCMP], P[b2][j][:, :], den[b2][:, 8:9], None,
                                                                      ALU.mult),
                              rd=[P_b[b2][j], den_b[b2]], wr=[acc_b])
                    else:
                        kb.op(kb.dve, lambda: nc.vector.scalar_tensor_tensor(out=acc[:, 0:NCMP], in0=P[b2][j][:, :],
                                                                             scalar=den[b2][:, 8 + j:9 + j],
                                                                             in1=acc[:, 0:NCMP],
                                                                             op0=ALU.mult, op1=ALU.add),
                              rd=[P_b[b2][j], den_b[b2]], wr=[acc_b])
                pe = kb.pe
                pe.wait(pt_b.wr_deps(), acc_b.w)
                nc.tensor.transpose(pt[:, 0, :], acc[:, 0:128], C["ident32"])
                ins = nc.tensor.transpose(pt[:, 1, :], acc[:, 128:256], C["ident32"])
                tok = pe.done(ins)
                acc_b.note_read(tok)
                pt_b.note_write(tok)
                kb.op(kb.act, lambda: nc.scalar.copy(accT[:, :, :], pt[:, :, :]), rd=[pt_b], wr=[accT_b])
                terms = [(accT[:, 0, :], C["overlap"][:, 0, :], [accT_b]),
                         (accT[0:127, 1, :], C["overlap"][0:127, 1, :], [accT_b])]
                kb.mm_group(pimp[:, :], pimp_b, terms)

            def stage_b2(qt):
                qs = slice(qt * 128, (qt + 1) * 128)
                b2 = qt % 2
                pimp, pimp_b = pimps[b2], pimps_b[b2]
                pe = kb.pe
                kb.op(kb.dve, lambda: nc.vector.tensor_tensor(score[:, :], pimp[:, :],
                                                              C["bonus"][:, 62 - 2 * qt:62 - 2 * qt + 64], ALU.add),
                      rd=[pimp_b], wr=[sc_b])
                kb.op(kb.dve, lambda: nc.vector.tensor_tensor(score[:, :], score[:, :], C["bonus"][:, 128:192], ALU.add),
                      rd=[sc_b], wr=[sc_b])
                kb.op(kb.dve, lambda: nc.vector.max(out=m8[:, 0:8], in_=score[:, :]), rd=[sc_b], wr=[sc_b])
                kb.op(kb.dve, lambda: nc.vector.match_replace(out=work[:, :], in_to_replace=m8[:, 0:8],
                                                              in_values=score[:, :], imm_value=-3.0e38),
                      rd=[sc_b], wr=[sc_b])
                kb.op(kb.dve, lambda: nc.vector.max(out=m8[:, 8:16], in_=work[:, :]), rd=[sc_b], wr=[sc_b])
                kb.op(kb.dve, lambda: nc.vector.tensor_reduce(out=thr[:, :], in_=m8[:, 8:16], axis=AX.X, op=ALU.min),
                      rd=[sc_b], wr=[sc_b])
                kb.op(kb.dve, lambda: nc.vector.tensor_scalar(mbq[:, :], score[:, :], thr[:, 0:1], NEGM, ALU.is_lt,
                                                              ALU.mult),
                      rd=[sc_b], wr=[sc_b])
                pe.wait(pmt_b.wr_deps(), sc_b.w)
                ins = nc.tensor.transpose(pmt[:, :], mbq[:, :], C["ident"])
                tok = pe.done(ins)
                sc_b.note_read(tok)
                pmt_b.note_write(tok)
                kb.op(kb.act, lambda: nc.scalar.copy(MBT[:, qs], pmt[:, :]), rd=[pmt_b], wr=[MBT_b])

            stage_a(0)
            if nqt > 1:
                stage_a(1)
            stage_b1(0)
            for qt in range(nqt):
                if qt + 2 < nqt:
                    stage_a(qt + 2)
                if qt + 1 < nqt:
                    stage_b1(qt + 1)
                stage_b2(qt)
            kb.dma(kb.pool, s_out, mb_s[g], MBT[:, :], rd=[MBT_b], wr=[mb_sb])
        kb.barrier()


def nsa_stage_attn(kb, C, q_s, q_sb, kv_s, kv_sb, mb_s, mb_sb, efull_d, kcmp, kcmp_b, vcmp, vcmp_b, Vtm, Vtm_b,
                   sigT, sigT_b, o_s, o_sb, heads=range(16), nqg=8):
    nc = kb.nc
    with ExitStack() as st:
        sb = lambda n, shp, dt: st.enter_context(nc.sbuf_tensor(kb.name(n), shp, dt))
        ps = lambda n, shp, dt: st.enter_context(nc.psum_tensor(kb.name(n), shp, dt))
        QM = [sb("QM", [128, S], BF16) for _ in range(2)]
        QM_b = [Buf() for _ in range(2)]
        QM_s = [kb.slot(kb.name("QM")) for _ in range(2)]
        ksE = sb("ksE", [128, S], BF16)
        kwT = sb("kwT", [128, S], BF16)
        kk_b = Buf()
        s_kk = kb.slot(kb.name("kk"))
        NPT = 6
        PT = [sb("PT", [128, 512], BF16) for _ in range(NPT)]
        PT_b = [Buf() for _ in range(NPT)]
        dhl = [sb("dhl", [128, 2, 512], BF16) for _ in range(2)]
        dhl_b = [Buf() for _ in range(2)]
        rd = [sb("rd", [64, 512], F32) for _ in range(2)]
        rd_b = [Buf() for _ in range(2)]
        wv = sb("wv", [64, 512], F32)
        tm = sb("tm", [64, 512], F32)
        e_b = Buf()
        oacc = [sb("oacc", [64, 512], F32) for _ in range(2)]
        oacc_b = [Buf() for _ in range(2)]
        obf = [sb("obf", [64, 512], BF16) for _ in range(2)]
        obf_b = [Buf() for _ in range(2)]
        obf_s = [kb.slot(kb.name("obf")) for _ in range(2)]
        NZ = 3
        pz = [ps("pz", [128, 512], F32) for _ in range(NZ)]
        pz_b = [Buf() for _ in range(NZ)]
        po = [ps("po2", [128, 512], F32) for _ in range(3)]
        po_b = [Buf() for _ in range(3)]
        pg = [ps("pg", [128, 512], F32) for _ in range(2)]
        pg_b = [Buf() for _ in range(2)]
        kb.op(kb.pool, lambda: nc.gpsimd.memset(kwT[:, :], 0.0), wr=[kk_b])
        for i_ in range(2):
            kb.op(kb.pool, lambda: nc.gpsimd.memset(dhl[i_][:, :, :], 0.0), wr=[dhl_b[i_]])
        cur_g = -1
        zi = 0
        pi = 0
        oi = 0
        ei = 0
        defer = []

        def run_deferred(force=False):
            keep = []
            for item in defer:
                item[0] -= 1
                if item[0] <= 0 or force:
                    item[1]()
                else:
                    keep.append(item)
            defer[:] = keep

        for hn_, h in enumerate(heads):
            g = h // 8
            qm = QM[hn_ % 2]
            qm_b = QM_b[hn_ % 2]
            kb.dma(kb.sp, QM_s[hn_ % 2], qm[0:64, :], q_s[h], rd=[q_sb], wr=[qm_b])
            kb.dma(kb.sp, QM_s[hn_ % 2], qm[64:128, :], mb_s[g], rd=[mb_sb], wr=[qm_b])
            if g != cur_g:
                cur_g = g
                kb.dma(kb.sp, s_kk, ksE[0:64, :], kv_s[2, g * 64:(g + 1) * 64, :], rd=[kv_sb], wr=[kk_b])
                kb.dma(kb.sp, s_kk, ksE[64:128, :], efull_d, wr=[kk_b])
                kb.dma(kb.sp, s_kk, kwT[0:64, :], kv_s[3, g * 64:(g + 1) * 64, :], rd=[kv_sb], wr=[kk_b])
            for qg in range(nqg):
                qs = slice(qg * 512, (qg + 1) * 512)
                tiles = []
                m0 = C["negmask_cmp"][:, qg, :] if qg <= 4 else None
                tiles.append((0, kcmp[:, g, 0:128], m0, 128, vcmp[:, 0, g, 0:65], [kcmp_b, vcmp_b], 0, 512))
                if qg >= 4:
                    tiles.append((0, kcmp[:, g, 128:NCMP], C["negmask_cmp"][0:127, qg - 4, :], 127,
                                  vcmp[0:127, 1, g, 0:65], [kcmp_b, vcmp_b], 0, 512))
                for kt in range(0, 4 * qg + 4):
                    j = kt - 4 * qg
                    m = C["negmask_le"][:, j, :] if j >= 0 else None
                    tiles.append((1, ksE[:, kt * 128:(kt + 1) * 128], m, 128, Vtm[:, kt, g, 0:65], [kk_b, Vtm_b[kt]],
                                  128 * j if j >= 0 else 0, 512))
                wl = []
                for kt in range(max(0, 4 * qg - 4), 4 * qg):
                    jp = kt - (4 * qg - 4)
                    wl.append((2, kwT[:, kt * 128:(kt + 1) * 128], C["negmask_win"][:, jp, :], 128,
                               Vtm[:, kt, 2 + g, 0:65], [kk_b, Vtm_b[kt]], 0, 128 * (jp + 1)))
                wl.reverse()
                for kt in range(4 * qg, 4 * qg + 4):
                    j = kt - 4 * qg
                    wl.append((2, kwT[:, kt * 128:(kt + 1) * 128], C["negmask_le"][:, j, :], 128,
                               Vtm[:, kt, 2 + g, 0:65], [kk_b, Vtm_b[kt]], 128 * j, 512))
                tiles += wl
                nt_ = len(tiles)
                first = {}
                last = {}
                for i, t in enumerate(tiles):
                    first.setdefault(t[0], i)
                    last[t[0]] = i
                slots = {}
                oa = oacc[oi % 2]
                oa_b = oacc_b[oi % 2]
                ob = obf[oi % 2]
                ob_b = obf_b[oi % 2]
                ob_s = obf_s[oi % 2]
                oi += 1
                done_br = []

                def emit_qk(i, tiles=tiles, slots=slots, qm=qm, qm_b=qm_b, qs=qs, qg=qg):
                    nonlocal zi, pi
                    br, l, m, rows, va, bufs, c0, c1 = tiles[i]
                    z = zi % NZ
                    zi += 1
                    p = pi % NPT
                    pi += 1
                    slots[i] = p
                    terms = [(l, qm[:, qg * 512 + c0:qg * 512 + c1], [bufs[0], qm_b])]
                    if m is not None:
                        terms.append((C["ident"][0:rows, 0:rows], m[:, c0:c1], []))
                    kb.mm_group(pz[z][0:rows, c0:c1], pz_b[z], terms)
                    kb.op(kb.act, lambda: nc.scalar.activation(out=PT[p][0:rows, c0:c1], in_=pz[z][0:rows, c0:c1],
                                                               func=AF.Exp, scale=0.125), rd=[pz_b[z]], wr=[PT_b[p]])

                def make_epilogue(br, h=h, qs=qs, oa=oa, oa_b=oa_b, ob=ob, ob_b=ob_b, ob_s=ob_s, done_br=done_br):
                    nonlocal ei
                    e = ei % 2
                    ei += 1
                    kb.op(kb.dve, lambda: nc.vector.tensor_copy(dhl[e][64:65, 0, :], po[br][64:65, :]), rd=[po_b[br]],
                          wr=[dhl_b[e]])
                    kb.op(kb.dve, lambda: nc.vector.scalar_tensor_tensor(out=dhl[e][64:65, 1, :], in0=po[br][64:65, :],
                                                                         scalar=1e-30, in1=dhl[e][64:65, 0, :],
                                                                         op0=ALU.add, op1=ALU.subtract),
                          rd=[po_b[br]], wr=[dhl_b[e]])

                    def part_b():
                        kb.mm_group(pg[0][:, :], pg_b[0], [(C["onesrow64"], dhl[e][:, 0, :], [dhl_b[e]]),
                                                           (C["onesrow64"], dhl[e][:, 1, :], [dhl_b[e]])])
                        kb.mm_group(pg[1][:, :], pg_b[1], [(C["gsel"][:, h * 3 + br, :], sigT[:, qs], [sigT_b])])
                        kb.op(kb.act, lambda: nc.scalar.activation(out=rd[e][:, :], in_=pg[0][0:64, :], func=AF.Ln),
                              rd=[pg_b[0]], wr=[rd_b[e]])
                        kb.op(kb.act, lambda: nc.scalar.activation(out=rd[e][:, :], in_=rd[e][:, :], func=AF.Exp,
                                                                   scale=-1.0), rd=[rd_b[e]], wr=[rd_b[e]])
                        kb.op(kb.dve, lambda: nc.vector.tensor_tensor(wv[:, :], pg[1][0:64, :], rd[e][:, :], ALU.mult),
                              rd=[pg_b[1], rd_b[e]], wr=[e_b])
                        if not done_br:
                            kb.op(kb.dve, lambda: nc.vector.tensor_tensor(oa[:, :], po[br][0:64, :], wv[:, :], ALU.mult),
                                  rd=[po_b[br], e_b], wr=[oa_b])
                        else:
                            kb.op(kb.dve, lambda: nc.vector.tensor_tensor(tm[:, :], po[br][0:64, :], wv[:, :], ALU.mult),
                                  rd=[po_b[br], e_b], wr=[e_b])
                            kb.op(kb.dve, lambda: nc.vector.tensor_tensor(oa[:, :], oa[:, :], tm[:, :], ALU.add),
                                  rd=[e_b], wr=[oa_b])
                        done_br.append(br)
                        if len(done_br) == 3:
                            kb.op(kb.pool, lambda: nc.gpsimd.tensor_copy(ob[:, :], oa[:, :]), rd=[oa_b], wr=[ob_b])
                            kb.dma(kb.pool, ob_s, o_s[h, :, qs], ob[:, :], rd=[ob_b], wr=[o_sb])
                    defer.append([2, part_b])

                def emit_av(i, tiles=tiles, slots=slots):
                    br, l, m, rows, va, bufs, c0, c1 = tiles[i]
                    p = slots[i]
                    pe = kb.pe
                    deps = [PT_b[p].w, bufs[1].w]
                    if i == first[br]:
                        deps += po_b[br].wr_deps()
                        assert c0 == 0 and c1 == 512
                    pe.wait(deps)
                    ins = nc.tensor.matmul(po[br][0:65, c0:c1], va, PT[p][0:rows, c0:c1], start=(i == first[br]),
                                           stop=(i == last[br]), skip_group_check=True)
                    tok = pe.done(ins)
                    PT_b[p].note_read(tok)
                    bufs[1].note_read(tok)
                    if i == last[br]:
                        po_b[br].note_write(tok)
                        make_epilogue(br)

                emit_qk(0)
                if nt_ > 1:
                    emit_qk(1)
                for i in range(nt_):
                    if i + 2 < nt_:
                        emit_qk(i + 2)
                    emit_av(i)
                    run_deferred()
        run_deferred(force=True)
        run_deferred(force=True)
        kb.barrier()


def phase_nsa(kb, C, xin, xin_bufs, xout, xout_bufs, grow_d, posb_d, W, scr, heads=range(16), nqg=8, nqt=NT):
    nc = kb.nc
    q_s, kv_s, mb_s, o_s = scr["q_s"], scr["kv_s"], scr["mb_s"], scr["o_s"]
    q_sb, kv_sb, mb_sb, o_sb = Buf(), Buf(), Buf(), Buf()
    with ExitStack() as st0:
        sb0 = lambda n, shp, dt: st0.enter_context(nc.sbuf_tensor(kb.name(n), shp, dt))
        Vtm = sb0("Vtm", [128, NT, 4, 80], BF16)
        Vtm_b = [Buf() for _ in range(NT)]
        sigT = sb0("sigT", [128, S], BF16)
        sigT_b = Buf()
        kcmp = sb0("kcmp", [128, 2, 256], BF16)
        kcmp_b = Buf()
        vcmp = sb0("vcmp", [128, 2, 2, 80], BF16)
        vcmp_b = Buf()
        with ExitStack() as st1:
            hnT = st1.enter_context(nc.sbuf_tensor(kb.name("hnT"), [128, 8, S], BF16))
            hnT_b = [Buf() for _ in range(NT)]
            stage_norm_all(kb, C, xin, xin_bufs, grow_d, hnT, hnT_b)
            nsa_stage_proj(kb, C, hnT, hnT_b, posb_d, W["wfm"][0], W["wfm"][1], W["wgt"][0], W["wgt"][1],
                           W["wtm"][0], W["wtm"][1], q_s, q_sb, kv_s, kv_sb, Vtm, Vtm_b, sigT, sigT_b,
                           (W["w1"][0], W["w1"][1], W["w2"][0], W["w2"][1], W["pe"][0], W["pe"][1]),
                           kcmp, kcmp_b, vcmp, vcmp_b)
        if DBG.get("nsa", 9) >= 2:
            nsa_stage_select(kb, C, q_s, q_sb, kcmp, kcmp_b, mb_s, mb_sb, nqt=nqt)
        if DBG.get("nsa", 9) >= 3:
          nsa_stage_attn(kb, C, q_s, q_sb, kv_s, kv_sb, mb_s, mb_sb, C["efull_d"], kcmp, kcmp_b, vcmp, vcmp_b,
                       Vtm, Vtm_b, sigT, sigT_b, o_s, o_sb, heads=heads, nqg=nqg)
    with ExitStack() as st2:
        oT = st2.enter_context(nc.sbuf_tensor(kb.name("oT"), [128, 8, S], BF16))
        oT_b = [Buf() for _ in range(8)]
        sl = kb.slot(kb.name("oTl"))
        for c in range(8):
            kb.dma(kb.sp, sl, oT[:, c, :], o_s[2 * c:2 * c + 2].rearrange("h d t -> (h d) t"), rd=[o_sb], wr=oT_b)
        stage_outproj(kb, C, oT, oT_b, W["wo"][0], W["wo"][1], xin, xin_bufs, xout, xout_bufs)


def host_nsa_weights(w_in, pe_k, pe_v, k_w1, k_w2, v_w1, v_w2, w_out):
    w_in = np.asarray(w_in, np.float32)
    perm64 = (np.arange(64) + 32) % 64
    qperm = (np.arange(1024) // 64) * 64 + perm64[np.arange(1024) % 64]
    kperm = (np.arange(128) // 64) * 64 + perm64[np.arange(128) % 64]
    q = w_in[:, 0:1024]
    blk = lambda i: w_in[:, 1024 + 128 * i:1024 + 128 * (i + 1)]
    kc, vc, ks, vs, kw, vw = [blk(i) for i in range(6)]
    fm = np.concatenate([q, q[:, qperm], kc, vc, ks, ks[:, kperm], kw, kw[:, kperm]], axis=1)
    out = {}
    out["wfm_h"] = host_chunks(fm)
    out["wgt_h"] = host_rows(w_in[:, 1792:1840])
    out["wtm_h"] = host_rows(np.concatenate([vs, vw], axis=1))
    w1 = lambda w: np.ascontiguousarray(np.asarray(w, np.float32).reshape(32, 64, 128).transpose(1, 0, 2)).reshape(64, 4096)
    out["w1_h"] = np.stack([w1(k_w1), w1(v_w1)], axis=0)
    k_w2 = np.asarray(k_w2, np.float32)
    out["w2_h"] = np.ascontiguousarray(np.concatenate([k_w2, k_w2[:, perm64], np.asarray(v_w2, np.float32)], axis=1))
    out["pe_h"] = np.ascontiguousarray(np.concatenate([np.asarray(pe_k, np.float32).T, np.asarray(pe_v, np.float32).T],
                                                      axis=1))
    out["wo_h"] = host_rows(w_out)
    return out


W_SHAPES = {
    "sba_wqk": ([16 * 128, 1024], 128), "sba_wv": ([128, 8192], 128), "sba_wo": ([128, 8192], 128),
    "ffn0_wup": ([NPAIR * 128, 2048], 128), "ffn0_wdn": ([DFF, D], 128),
    "ffn1_wup": ([NPAIR * 128, 2048], 128), "ffn1_wdn": ([DFF, D], 128),
    "nsa_wfm": ([22 * 128, 1024], 128), "nsa_wgt": ([128, 384], 128), "nsa_wtm": ([128, 2048], 128),
    "nsa_w1": ([128, 4096], 64), "nsa_w2": ([128, 192], 128), "nsa_pe": ([64, 64], 64), "nsa_wo": ([128, 8192], 128),
}


def build_full():
    kb = KB()
    nc = kb.nc
    x_d = nc.dram_tensor("x", [S, D], F32, kind="ExternalInput").ap()
    posb_d = nc.dram_tensor("posb", [128, S], I32, kind="ExternalInput").ap()
    g_d = {n: nc.dram_tensor(n, [128, D], F32, kind="ExternalInput").ap()
           for n in ("g_mix0", "g_ffn0", "g_mix1", "g_ffn1", "g_fin")}
    cwb_d = [nc.dram_tensor(f"cwb{l}", [128, 4 * 44], F32, kind="ExternalInput").ap() for l in range(2)]
    y_d = nc.dram_tensor("y", [S, D], F32, kind="ExternalOutput").ap()
    H, Sx, Wb = {}, {}, {}
    for n, (shp, rc) in W_SHAPES.items():
        H[n] = nc.dram_tensor(n + "_h", shp, F32, kind="ExternalInput").ap()
        Sx[n] = nc.dram_tensor(n + "_s", shp, BF16).ap()
        Wb[n] = Buf()
    xa = nc.dram_tensor("xa", [S, D], F32).ap()
    xb = nc.dram_tensor("xb", [S, D], F32).ap()
    xc = nc.dram_tensor("xc", [S, D], F32).ap()
    scr = {"q_s": nc.dram_tensor("q_s", [16, 64, S], BF16).ap(), "kv_s": nc.dram_tensor("kv_s", [4, 128, S], BF16).ap(),
           "mb_s": nc.dram_tensor("mb_s", [2, 64, S], BF16).ap(), "o_s": nc.dram_tensor("o_s", [16, 64, S], BF16).ap()}
    jobs = []
    jobs_bg = []
    for n, (shp, rc) in W_SHAPES.items():
        for r0 in range(0, shp[0], rc):
            (jobs if n.startswith("sba_") else jobs_bg).append((H[n][r0:r0 + rc, :], Sx[n][r0:r0 + rc, :], Wb[n]))
    phase_convert(kb, jobs)
    chunk = lambda ap, p: ap.rearrange("(c p) n -> c p n", p=p)
    x_b = [Buf() for _ in range(NT)]
    xa_b = [Buf() for _ in range(NT)]
    xb_b = [Buf() for _ in range(NT)]
    xc_b = [Buf() for _ in range(NT)]
    y_b = [Buf() for _ in range(NT)]
    with ExitStack() as cst:
        C = load_consts(kb, SBA_CONSTS, cst)
        bg = BgConv(kb, cst, jobs_bg)
        phase_sba(kb, C, x_d, x_b, xa, xa_b, g_d["g_mix0"], chunk(Sx["sba_wqk"], 128), Wb["sba_wqk"],
                  Sx["sba_wv"], Wb["sba_wv"], Sx["sba_wo"], Wb["sba_wo"], bg=bg)
    with ExitStack() as cst:
        C = load_consts(kb, FFN_CONSTS, cst)
        phase_ffn(kb, C, 0, xa, xa_b, xb, xb_b, chunk(Sx["ffn0_wup"], 128), Wb["ffn0_wup"],
                  chunk(Sx["ffn0_wdn"], 128), Wb["ffn0_wdn"], cwb_d[0], g_d["g_ffn0"])
    W = {"wfm": (chunk(Sx["nsa_wfm"], 128), Wb["nsa_wfm"]), "wgt": (Sx["nsa_wgt"], Wb["nsa_wgt"]),
         "wtm": (Sx["nsa_wtm"], Wb["nsa_wtm"]), "w1": (chunk(Sx["nsa_w1"], 64), Wb["nsa_w1"]),
         "w2": (Sx["nsa_w2"], Wb["nsa_w2"]), "pe": (Sx["nsa_pe"], Wb["nsa_pe"]), "wo": (Sx["nsa_wo"], Wb["nsa_wo"])}
    with ExitStack() as cst:
        C = load_consts(kb, NSA_CONSTS, cst)
        phase_nsa(kb, C, xb, xb_b, xc, xc_b, g_d["g_mix1"], posb_d, W, scr)
    with ExitStack() as cst:
        C = load_consts(kb, FFN_CONSTS, cst)
        phase_ffn(kb, C, 1, xc, xc_b, y_d, y_b, chunk(Sx["ffn1_wup"], 128), Wb["ffn1_wup"],
                  chunk(Sx["ffn1_wdn"], 128), Wb["ffn1_wdn"], cwb_d[1], g_d["g_ffn1"], final_grow_d=g_d["g_fin"])
    return kb


def kernel(x, positions, norm_mix, sba_w_in, sba_w_out, nsa_w_in, nsa_cmp_pos_k, nsa_cmp_pos_v, nsa_cmp_k_w1,
           nsa_cmp_k_w2, nsa_cmp_v_w1, nsa_cmp_v_w2, nsa_w_out, norm_ffn, ffn_w_up, ffn_conv_w, ffn_conv_b,
           ffn_w_down, norm_final):
    x = np.asarray(x, np.float32)
    positions = np.asarray(positions)
    B = x.shape[0]
    shared = {}
    sw = np.asarray(sba_w_in[0], np.float32)
    shared["sba_wqk_h"] = host_chunks(sw[:, :2048])
    shared["sba_wv_h"] = host_rows(sw[:, 2048:])
    shared["sba_wo_h"] = host_rows(sba_w_out[0])
    for l in range(2):
        shared[f"ffn{l}_wup_h"] = host_wup(ffn_w_up[l])
        shared[f"ffn{l}_wdn_h"] = np.ascontiguousarray(np.asarray(ffn_w_down[l], np.float32))
        shared[f"cwb{l}"] = host_cwb(ffn_conv_w[l], ffn_conv_b[l])
    hw = host_nsa_weights(nsa_w_in[0], nsa_cmp_pos_k[0], nsa_cmp_pos_v[0], nsa_cmp_k_w1[0], nsa_cmp_k_w2[0],
                          nsa_cmp_v_w1[0], nsa_cmp_v_w2[0], nsa_w_out[0])
    for k_ in ("wfm", "wgt", "wtm", "w1", "w2", "pe", "wo"):
        shared["nsa_" + k_ + "_h"] = np.ascontiguousarray(hw[k_ + "_h"].reshape(W_SHAPES["nsa_" + k_][0]))
    shared["g_mix0"] = host_row(norm_mix[0])
    shared["g_mix1"] = host_row(norm_mix[1])
    shared["g_ffn0"] = host_row(norm_ffn[0])
    shared["g_ffn1"] = host_row(norm_ffn[1])
    shared["g_fin"] = host_row(norm_final)
    for k_, v_ in consts_host().items():
        shared["c_" + k_] = v_
    in_maps = []
    for b in range(B):
        m = dict(shared)
        m["x"] = np.ascontiguousarray(x[b])
        m["posb"] = np.ascontiguousarray(np.broadcast_to(positions[b].astype(np.int32)[None, :], (128, S)))
        in_maps.append(m)
    kb = build_full()
    res = run_bass_kernel_spmd(kb.nc, in_maps, core_ids=list(range(B)))
    return np.stack([np.asarray(r["y"], np.float32) for r in res.results], axis=0)
```

```python
from contextlib import ExitStack
import numpy as np
import ml_dtypes
import concourse.bass as bass
import concourse.mybir as mybir
from concourse.bass_utils import run_bass_kernel_spmd

F32 = mybir.dt.float32
BF16 = mybir.dt.bfloat16
I32 = mybir.dt.int32
AF = mybir.ActivationFunctionType
ALU = mybir.AluOpType
AX = mybir.AxisListType

DBG = {}
S = 4096
D = 1024
NT = S // 128
DFF = 2816
NPAIR = DFF // 128
EPS = 1e-6
NEGM = -30000.0


def _flat(ts):
    for t in ts:
        if t is None:
            continue
        if isinstance(t, tuple) and len(t) == 3 and isinstance(t[1], int):
            yield t
        else:
            yield from _flat(t)


class Buf:
    def __init__(self, name=""):
        self.name = name
        self.w = None
        self.r = {}

    def rd_deps(self):
        return [self.w]

    def wr_deps(self):
        return [self.w] + list(self.r.values())

    def note_read(self, tok):
        if tok is None:
            return
        k = tok[2]
        if k not in self.r or self.r[k][1] < tok[1]:
            self.r[k] = tok

    def note_write(self, tok):
        self.w = tok
        self.r = {}


class Eng:
    def __init__(self, kb, name, eng):
        self.kb = kb
        self.name = name
        self.eng = eng
        self.sem = kb.newsem("e_" + name)
        self.n = 0
        self.seen = {}

    def wait(self, *toks):
        for t in _flat(toks):
            sem, val, key = t
            if self.seen.get(key, 0) >= val:
                continue
            self.seen[key] = val
            self.eng.wait_ge(sem, val)

    def done(self, ins):
        self.n += 1
        ins.then_inc(self.sem, 1)
        return (self.sem, self.n, self.name)


class Slot:
    def __init__(self, kb, name):
        self.sem = kb.newsem("d_" + name)
        self.val = 0
        self.key = "d_" + name

    def done(self, ins):
        self.val += 16
        ins.then_inc(self.sem, 16)
        return (self.sem, self.val, self.key)


class KB:
    def __init__(self):
        self.nc = bass.Bass("TRN2", target_bir_lowering=False)
        self.root = ExitStack()
        self.nsem = 0
        nc = self.nc
        self.pe = Eng(self, "pe", nc.tensor)
        self.act = Eng(self, "act", nc.scalar)
        self.dve = Eng(self, "dve", nc.vector)
        self.pool = Eng(self, "pool", nc.gpsimd)
        self.sp = Eng(self, "sp", nc.sync)
        self.engs = [self.pe, self.act, self.dve, self.pool, self.sp]
        self.slots = []
        self.uid = 0

    def newsem(self, name):
        self.nsem += 1
        return self.root.enter_context(self.nc.semaphore(name))

    def slot(self, name):
        s = Slot(self, name)
        self.slots.append(s)
        return s

    def name(self, p):
        self.uid += 1
        return f"{p}_{self.uid}"

    def op(self, E, make, rd=(), wr=(), sig=True, extra=()):
        deps = list(extra)
        for b in rd:
            deps.append(b.w)
        for b in wr:
            deps.extend(b.wr_deps())
        E.wait(deps)
        ins = make()
        tok = E.done(ins) if sig else None
        if tok is not None:
            for b in rd:
                b.note_read(tok)
            for b in wr:
                b.note_write(tok)
        return tok

    def dma(self, Q, slot, out, in_, rd=(), wr=(), extra=()):
        deps = list(extra)
        for b in rd:
            deps.append(b.w)
        for b in wr:
            deps.extend(b.wr_deps())
        Q.wait(deps)
        ins = Q.eng.dma_start(out=out, in_=in_)
        tok = slot.done(ins)
        for b in rd:
            b.note_read(tok)
        for b in wr:
            b.note_write(tok)
        return tok

    def batch_end(self, slot, bufs):
        tok = (slot.sem, slot.val, slot.key)
        for b in bufs:
            b.w = tok

    def mm_group(self, out_ap, obuf, terms, sig=True):
        pe = self.pe
        deps = list(obuf.wr_deps())
        for (_, _, bufs) in terms:
            for b in bufs:
                deps.append(b.w)
        pe.wait(deps)
        n = len(terms)
        ins = None
        for i, (l, r, _) in enumerate(terms):
            ins = self.nc.tensor.matmul(out_ap, l, r, start=(i == 0), stop=(i == n - 1))
        tok = pe.done(ins)
        for (_, _, bufs) in terms:
            for b in bufs:
                b.note_read(tok)
        obuf.note_write(tok)
        return tok

    def barrier(self):
        toks = []
        for e in self.engs:
            if e.n > 0:
                toks.append((e.sem, e.n, e.name))
        for s in self.slots:
            if s.val > 0:
                toks.append((s.sem, s.val, s.key))
        for e in self.engs:
            e.wait(toks)


def phase_convert(kb, jobs):
    nc = kb.nc
    CH = 2048
    with ExitStack() as st:
        NB = 3
        tin = [st.enter_context(nc.sbuf_tensor(kb.name("cvi"), [128, CH], F32)) for _ in range(NB)]
        tout = [st.enter_context(nc.sbuf_tensor(kb.name("cvo"), [128, CH], BF16)) for _ in range(NB)]
        bin_ = [Buf() for _ in range(NB)]
        bout = [Buf() for _ in range(NB)]
        sin = [kb.slot(kb.name("cvin")) for _ in range(NB)]
        sout = [kb.slot(kb.name("cvout")) for _ in range(NB)]
        i = 0
        for (src, dst, dbuf) in jobs:
            R, Fd = src.shape[0], src.shape[1]
            for c0 in range(0, Fd, CH):
                w = min(CH, Fd - c0)
                k = i % NB
                kb.dma(kb.sp, sin[k], tin[k][0:R, 0:w], src[:, c0:c0 + w], wr=[bin_[k]])
                sel = i % 3
                if sel == 0:
                    kb.op(kb.dve, lambda: nc.vector.tensor_copy(tout[k][0:R, 0:w], tin[k][0:R, 0:w]),
                          rd=[bin_[k]], wr=[bout[k]])
                elif sel == 1:
                    kb.op(kb.pool, lambda: nc.gpsimd.tensor_copy(tout[k][0:R, 0:w], tin[k][0:R, 0:w]),
                          rd=[bin_[k]], wr=[bout[k]])
                else:
                    kb.op(kb.act, lambda: nc.scalar.copy(tout[k][0:R, 0:w], tin[k][0:R, 0:w]),
                          rd=[bin_[k]], wr=[bout[k]])
                kb.dma(kb.pool, sout[k], dst[:, c0:c0 + w], tout[k][0:R, 0:w], rd=[bout[k]], wr=[dbuf])
                i += 1
        kb.barrier()


class BgConv:
    def __init__(self, kb, stack, jobs, CH=512, NB=2):
        nc = kb.nc
        self.kb = kb
        self.NB = NB
        self.tin = [stack.enter_context(nc.sbuf_tensor(kb.name("bgi"), [128, CH], F32)) for _ in range(NB)]
        self.tout = [stack.enter_context(nc.sbuf_tensor(kb.name("bgo"), [128, CH], BF16)) for _ in range(NB)]
        self.bin = [Buf() for _ in range(NB)]
        self.bout = [Buf() for _ in range(NB)]
        self.sin = [kb.slot(kb.name("bgin")) for _ in range(NB)]
        self.sout = [kb.slot(kb.name("bgout")) for _ in range(NB)]
        self.tiles = []
        for (src, dst, dbuf) in jobs:
            R, Fd = src.shape[0], src.shape[1]
            for c0 in range(0, Fd, CH):
                w = min(CH, Fd - c0)
                self.tiles.append((src[:, c0:c0 + w], dst[:, c0:c0 + w], R, w, dbuf))
        self.pos = 0

    def emit(self, n):
        kb = self.kb
        nc = kb.nc
        for _ in range(n):
            if self.pos >= len(self.tiles):
                return
            src, dst, R, w, dbuf = self.tiles[self.pos]
            k = self.pos % self.NB
            self.pos += 1
            kb.dma(kb.sp, self.sin[k], self.tin[k][0:R, 0:w], src, wr=[self.bin[k]])
            kb.op(kb.pool, lambda: nc.gpsimd.tensor_copy(self.tout[k][0:R, 0:w], self.tin[k][0:R, 0:w]),
                  rd=[self.bin[k]], wr=[self.bout[k]])
            kb.dma(kb.pool, self.sout[k], dst, self.tout[k][0:R, 0:w], rd=[self.bout[k]], wr=[dbuf])

    def flush(self):
        self.emit(len(self.tiles))


def norm_block(kb, xts, xbufs, grow, gbuf, hns, hnbufs, sq, sqbuf, st, stbuf):
    nc = kb.nc
    n = len(xts)
    for i in range(n):
        kb.op(kb.act, lambda: nc.scalar.activation(out=sq, in_=xts[i], func=AF.Square, accum_out=st[:, i:i + 1]),
              rd=[xbufs[i]], wr=[sqbuf, stbuf])
    kb.op(kb.dve, lambda: nc.vector.tensor_scalar(st[:, 4:4 + n], st[:, 0:n], 1.0 / D, EPS, ALU.mult, ALU.add),
          rd=[stbuf], wr=[stbuf])
    kb.op(kb.act, lambda: nc.scalar.activation(out=st[:, 8:8 + n], in_=st[:, 4:4 + n], func=AF.Sqrt),
          rd=[stbuf], wr=[stbuf])
    kb.op(kb.dve, lambda: nc.vector.reciprocal(st[:, 4:4 + n], st[:, 8:8 + n]), rd=[stbuf], wr=[stbuf])
    for i in range(n):
        kb.op(kb.dve, lambda: nc.vector.scalar_tensor_tensor(out=hns[i], in0=xts[i], scalar=st[:, 4 + i:5 + i],
                                                             in1=grow, op0=ALU.mult, op1=ALU.mult),
              rd=[xbufs[i], stbuf, gbuf], wr=[hnbufs[i]])


def transpose_tile(kb, C, hn, hnbuf, pst, pstbuf, dst_ap, dstbuf, evac_eng):
    nc = kb.nc
    pe = kb.pe
    pe.wait(pstbuf.wr_deps(), hnbuf.w)
    ins = None
    for kc in range(8):
        ins = nc.tensor.transpose(pst[:, kc, :], hn[:, kc * 128:(kc + 1) * 128], C["ident"])
    tok = pe.done(ins)
    hnbuf.note_read(tok)
    pstbuf.note_write(tok)
    if evac_eng is kb.act:
        kb.op(kb.act, lambda: nc.scalar.copy(dst_ap, pst[:, :, :]), rd=[pstbuf], wr=[dstbuf])
    else:
        kb.op(kb.dve, lambda: nc.vector.tensor_copy(dst_ap, pst[:, :, :]), rd=[pstbuf], wr=[dstbuf])


def phase_ffn(kb, C, layer, xin, xin_bufs, xout, xout_bufs, wup_d, wup_buf, wdn_d, wdn_buf,
              cwb_d, grow_d, final_grow_d=None):
    nc = kb.nc
    with ExitStack() as st:
        sb = lambda n, shp, dt: st.enter_context(nc.sbuf_tensor(kb.name(n), shp, dt))
        ps = lambda n, shp, dt: st.enter_context(nc.psum_tensor(kb.name(n), shp, dt))
        wdn = sb("wdn", [128, NPAIR, 1024], BF16)
        wdn_b = Buf()
        cw = sb("cw", [128, 4, 44], F32)
        cw_b = Buf()
        grow = sb("grow", [128, 1024], F32)
        grow_b = Buf()
        NW = 4
        wup = [sb("wup", [128, 8, 256], BF16) for _ in range(NW)]
        wup_b = [Buf() for _ in range(NW)]
        wup_s = [kb.slot(kb.name("wup")) for _ in range(NW)]
        hnT = [sb("hnT", [128, 8, 512], BF16) for _ in range(2)]
        hnT_b = [[Buf() for _ in range(4)] for _ in range(2)]
        G2 = [sb("G", [128, NPAIR, 512], BF16) for _ in range(2)]
        G2_b = [[Buf() for _ in range(NPAIR)] for _ in range(2)]
        T0 = [sb("T0", [128, 512], F32) for _ in range(4)]
        T0_b = [Buf() for _ in range(4)]
        xt = [sb("xt", [128, 1024], F32) for _ in range(4)]
        xt_b = [Buf() for _ in range(4)]
        xt_s = [kb.slot(kb.name("xt")) for _ in range(4)]
        hn = [sb("hn", [128, 1024], BF16) for _ in range(4)]
        hn_b = [Buf() for _ in range(4)]
        sq = sb("sq", [128, 1024], BF16)
        sq_b = Buf()
        st2 = sb("st2", [128, 12], F32)
        st2_b = Buf()
        U = [sb("U", [128, 514], F32) for _ in range(4)]
        U_b = [Buf() for _ in range(4)]
        Cc = [sb("Cc", [128, 512], F32) for _ in range(4)]
        Cc_b = [Buf() for _ in range(4)]
        Sg = [sb("Sg", [128, 512], F32) for _ in range(2)]
        Sg_b = [Buf() for _ in range(2)]
        halo = sb("halo", [128, 44, 2], F32)
        halo_b = [Buf() for _ in range(44)]
        xr = [sb("xr", [128, 1024], F32) for _ in range(2)]
        xr_b = [Buf() for _ in range(2)]
        xr_s = [kb.slot(kb.name("xr")) for _ in range(2)]
        NXO = 2 if final_grow_d is None else 4
        xo = [sb("xo", [128, 1024], F32) for _ in range(NXO)]
        xo_b = [Buf() for _ in range(NXO)]
        xo_s = [kb.slot(kb.name("xo")) for _ in range(4)]
        if final_grow_d is not None:
            fgrow = sb("fgrow", [128, 1024], F32)
            fgrow_b = Buf()
            fst = sb("fst", [128, 12], F32)
            fst_b = Buf()
        pst = ps("pst", [128, 8, 128], BF16)
        pst_b = Buf()
        pu = [ps("pu", [128, 512], F32) for _ in range(4)]
        pu_b = [Buf() for _ in range(4)]
        po = [ps("po", [128, 512], F32) for _ in range(2)]
        po_b = [Buf() for _ in range(2)]

        s_misc = kb.slot(kb.name("misc"))
        for c0_, c1_ in ((0, 6), (6, 11), (11, 17), (17, 22)):
            kb.dma(kb.pool, s_misc, wdn[:, c0_:c1_, :], wdn_d[c0_:c1_].rearrange("c p n -> p c n"), rd=[wdn_buf],
                   wr=[wdn_b])
        kb.dma(kb.pool, s_misc, cw[:, :, :], cwb_d.rearrange("p (j c) -> p j c", j=4), wr=[cw_b])
        kb.dma(kb.pool, s_misc, grow[:, :], grow_d, wr=[grow_b])
        if final_grow_d is not None:
            kb.dma(kb.pool, s_misc, fgrow[:, :], final_grow_d, wr=[fgrow_b])
            kb.batch_end(s_misc, [fgrow_b])
        kb.batch_end(s_misc, [wdn_b, cw_b, grow_b])
        kb.op(kb.dve, lambda: nc.vector.memset(halo[:, :, :], 0.0), wr=halo_b)

        wi = 0
        ui = 0
        oi = 0
        NBLK = S // 512

        def norm_stage(tb):
            hb = tb % 2
            for t4 in range(4):
                tt = tb * 4 + t4
                kb.dma(kb.sp, xt_s[t4], xt[t4][:, :], xin[tt * 128:(tt + 1) * 128, :], rd=[xin_bufs[tt]],
                       wr=[xt_b[t4]])
            norm_block(kb, [xt[i][:, :] for i in range(4)], xt_b, grow[:, :], grow_b,
                       [hn[i][:, :] for i in range(4)], hn_b, sq[:, :], sq_b, st2[:, :], st2_b)
            for t4 in range(4):
                transpose_tile(kb, C, hn[t4], hn_b[t4], pst, pst_b, hnT[hb][:, :, t4 * 128:(t4 + 1) * 128],
                               hnT_b[hb][t4], kb.act)

        def down_group(tb, gi):
            nonlocal oi
            t4, nh = divmod(gi, 2)
            tt = tb * 4 + t4
            Gd = G2[tb % 2]
            Gd_b = G2_b[tb % 2]
            k = t4 % 2
            ko = t4 % NXO
            if nh == 0:
                kb.dma(kb.sp, xr_s[k], xr[k][:, :], xin[tt * 128:(tt + 1) * 128, :], rd=[xin_bufs[tt]], wr=[xr_b[k]])
            terms = [(Gd[:, fc, t4 * 128:(t4 + 1) * 128], wdn[:, fc, nh * 512:(nh + 1) * 512], [Gd_b[fc], wdn_b])
                     for fc in range(NPAIR)]
            kb.mm_group(po[nh][:, :], po_b[nh], terms)
            kb.op(kb.dve, lambda: nc.vector.tensor_tensor(xo[ko][:, nh * 512:(nh + 1) * 512], po[nh][:, :],
                                                          xr[k][:, nh * 512:(nh + 1) * 512], ALU.add),
                  rd=[po_b[nh], xr_b[k]], wr=[xo_b[ko]])
            if nh == 1 and final_grow_d is None:
                kb.dma(kb.pool, xo_s[ko], xout[tt * 128:(tt + 1) * 128, :], xo[ko][:, :], rd=[xo_b[ko]],
                       wr=[xout_bufs[tt]])
            if gi == 7 and final_grow_d is not None:
                norm_block(kb, [xo[i][:, :] for i in range(4)], xo_b, fgrow[:, :], fgrow_b,
                           [xo[i][:, :] for i in range(4)], xo_b, sq[:, :], sq_b, fst[:, :], fst_b)
                for t4_ in range(4):
                    tt_ = tb * 4 + t4_
                    kb.dma(kb.pool, xo_s[t4_], xout[tt_ * 128:(tt_ + 1) * 128, :], xo[t4_][:, :], rd=[xo_b[t4_]],
                           wr=[xout_bufs[tt_]])

        norm_stage(0)
        for tb in range(NBLK):
            hb = tb % 2
            Gw = G2[tb % 2]
            Gw_b = G2_b[tb % 2]
            for j in range(NPAIR):
                k = wi % NW
                wi += 1
                kb.dma(kb.sp, wup_s[k], wup[k][:, :, :], wup_d[j].rearrange("p (kc n) -> p kc n", kc=8),
                       rd=[wup_buf], wr=[wup_b[k]])
                cs = []
                for gv in range(2):
                    u = ui % 4
                    ui += 1
                    ch = gv * NPAIR + j
                    terms = [(wup[k][:, kc, gv * 128:(gv + 1) * 128], hnT[hb][:, kc, :], [wup_b[k]] + hnT_b[hb])
                             for kc in range(8)]
                    kb.mm_group(pu[u][:, :], pu_b[u], terms)
                    kb.op(kb.act, lambda: nc.scalar.activation(out=Cc[u][:, :], in_=pu[u][:, :], func=AF.Identity,
                                                               bias=cw[:, 3, ch:ch + 1], scale=cw[:, 2, ch:ch + 1]),
                          rd=[pu_b[u], cw_b], wr=[Cc_b[u]])
                    kb.op(kb.act, lambda: nc.scalar.copy(U[u][:, 2:514], pu[u][:, :]), rd=[pu_b[u]], wr=[U_b[u]])
                    kb.op(kb.pool, lambda: nc.gpsimd.tensor_copy(U[u][:, 0:2], halo[:, ch, :]),
                          rd=[halo_b[ch]], wr=[U_b[u]])
                    kb.op(kb.pool, lambda: nc.gpsimd.tensor_copy(halo[:, ch, :], U[u][:, 512:514]),
                          rd=[U_b[u]], wr=[halo_b[ch]])
                    kb.op(kb.pool, lambda: nc.gpsimd.tensor_scalar(T0[u][:, :], U[u][:, 0:512], cw[:, 0, ch:ch + 1], 0.0,
                                                                   ALU.mult, ALU.add),
                          rd=[U_b[u], cw_b], wr=[T0_b[u]])
                    kb.op(kb.dve, lambda: nc.vector.scalar_tensor_tensor(out=Cc[u][:, :], in0=U[u][:, 1:513],
                                                                         scalar=cw[:, 1, ch:ch + 1], in1=Cc[u][:, :],
                                                                         op0=ALU.mult, op1=ALU.add),
                          rd=[U_b[u], cw_b], wr=[Cc_b[u]])
                    kb.op(kb.dve, lambda: nc.vector.tensor_tensor(Cc[u][:, :], Cc[u][:, :], T0[u][:, :], ALU.add),
                          rd=[T0_b[u]], wr=[Cc_b[u]])
                    cs.append(u)
                ug, uv = cs
                sgi = j % 2
                kb.op(kb.act, lambda: nc.scalar.activation(out=Sg[sgi][:, :], in_=Cc[ug][:, :], func=AF.Silu),
                      rd=[Cc_b[ug]], wr=[Sg_b[sgi]])
                kb.op(kb.dve, lambda: nc.vector.tensor_tensor(Gw[:, j, :], Sg[sgi][:, :], Cc[uv][:, :], ALU.mult),
                      rd=[Sg_b[sgi], Cc_b[uv]], wr=[Gw_b[j]])
                if tb > 0 and j % 3 == 0 and j // 3 < 8:
                    down_group(tb - 1, j // 3)
            if tb + 1 < NBLK:
                norm_stage(tb + 1)
        for gi in range(8):
            down_group(NBLK - 1, gi)
        kb.barrier()


def host_wup(w):
    a = np.asarray(w, np.float32).reshape(8, 128, 2, NPAIR, 128)
    a = a.transpose(3, 1, 0, 2, 4)
    return np.ascontiguousarray(a).reshape(NPAIR * 128, 2048)


def host_cwb(cw, cb):
    a = np.concatenate([np.asarray(cw, np.float32), np.asarray(cb, np.float32)[None, :]], axis=0)
    a = a.reshape(4, 44, 128).transpose(2, 0, 1)
    return np.ascontiguousarray(a).reshape(128, 4 * 44)


def host_row(g):
    return np.ascontiguousarray(np.broadcast_to(np.asarray(g, np.float32)[None, :], (128, D)))


def host_chunks(w):
    w = np.asarray(w, np.float32)
    n = w.shape[1] // 128
    a = w.reshape(8, 128, n, 128).transpose(2, 1, 0, 3)
    return np.ascontiguousarray(a).reshape(n * 128, 1024)


def host_rows(w):
    w = np.asarray(w, np.float32)
    kc = w.shape[0] // 128
    a = w.reshape(kc, 128, w.shape[1]).transpose(1, 0, 2)
    return np.ascontiguousarray(a).reshape(128, kc * w.shape[1])


def consts_host():
    bf = ml_dtypes.bfloat16
    c = {}
    c["ident"] = np.eye(128, dtype=np.float32).astype(bf)
    c["ident32"] = np.eye(128, dtype=np.float32)
    p = np.arange(128)[:, None]
    f = np.arange(512)[None, :]
    nm = np.zeros((128, 4, 512), np.float32)
    le = np.zeros((128, 4, 512), np.float32)
    wn = np.zeros((128, 4, 512), np.float32)
    for j in range(4):
        nm[:, j, :] = np.where(f <= 128 * j + p, NEGM, 0.0)
        le[:, j, :] = np.where(128 * j + p > f, NEGM, 0.0)
        wn[:, j, :] = np.where(f >= 128 * j + p, NEGM, 0.0)
    c["negmask"] = nm.astype(bf)
    c["negmask_le"] = le.astype(bf)
    c["negmask_win"] = wn.astype(bf)
    cm = np.zeros((128, 5, 512), np.float32)
    for u in range(5):
        cm[:, u, :] = np.where(16 * p + 31 > 512 * u + f, NEGM, 0.0)
    c["negmask_cmp"] = cm.astype(bf)
    x = np.arange(512)[None, :]
    c["cmask_tm"] = np.where(16 * (x - 248) + 31 > p, NEGM, 0.0).astype(np.float32).astype(bf)
    jj = np.arange(128)[:, None]
    ss = np.arange(128)[None, :]
    c["uincneg"] = np.where(jj >= ss, -1.0, 0.0).astype(np.float32).astype(bf)
    os_ = np.zeros((128, 2, 128), np.float32)
    sl = np.zeros((128, 2, 128), np.float32)
    for hh in range(2):
        os_[:, hh, hh] = 1.0
        os_[:, hh, 32 + hh] = 1.0
        sl[hh, hh, :] = 1.0
        sl[32 + hh, hh, :] = 1.0
    c["onesel"] = os_.astype(bf)
    c["sel"] = sl.astype(bf)
    c["zeros512"] = np.zeros((128, 512), np.float32).astype(bf)
    cc = np.arange(256)[:, None] * 16
    s0 = np.arange(64)[None, :] * 64
    ov = np.clip(np.minimum(cc + 32, s0 + 64) - np.maximum(cc, s0), 0, None) / 32.0
    ov[255, :] = 0.0
    c["overlap"] = np.ascontiguousarray(ov.reshape(2, 128, 64).transpose(1, 0, 2)).astype(np.float32)
    bon = np.zeros((128, 192), np.float32)
    y = np.arange(128)[None, :]
    npr = y - 62
    cur = p // 64
    bon[:, 0:128] = np.where(npr > cur, -1.0e9, np.where((npr == cur) | (npr == cur - 1), 1.0e4, 0.0))
    bon[:, 128] = 1.0e4
    c["bonus"] = bon
    o64 = np.zeros((128, 128), np.float32)
    o64[64, :] = 1.0
    c["onesrow64"] = o64.astype(bf)
    gs = np.zeros((128, 48, 128), np.float32)
    for r in range(48):
        gs[r, r, :] = 1.0
    c["gsel"] = gs.astype(bf)
    half = 32
    inv = (10000.0 ** (-np.arange(half, dtype=np.float32) / half)).astype(np.float32)
    rc = np.zeros((128, 2), np.float32)
    rc[:, 0] = inv[np.arange(128) % 32]
    rc[:, 1] = np.where((np.arange(128) % 64) < 32, -1.0, 1.0)
    c["ropec"] = rc
    ef = (np.arange(S)[None, :] // 64 == np.arange(64)[:, None]).astype(np.float32)
    c["efull"] = ef.astype(bf)
    return c


CONST_SHAPES = {"ident": ([128, 128], BF16), "ident32": ([128, 128], F32), "negmask": ([128, 4, 512], BF16),
                "negmask_le": ([128, 4, 512], BF16), "negmask_win": ([128, 4, 512], BF16),
                "negmask_cmp": ([128, 5, 512], BF16), "cmask_tm": ([128, 512], BF16),
                "uincneg": ([128, 128], BF16), "onesel": ([128, 2, 128], BF16), "sel": ([128, 2, 128], BF16),
                "zeros512": ([128, 512], BF16),
                "overlap": ([128, 2, 64], F32), "bonus": ([128, 192], F32), "onesrow64": ([128, 128], BF16),
                "gsel": ([128, 48, 128], BF16), "ropec": ([128, 2], F32)}


SBA_CONSTS = ("ident", "negmask", "uincneg", "onesel", "sel", "zeros512")
NSA_CONSTS = ("ident", "ident32", "negmask_le", "negmask_win", "negmask_cmp", "cmask_tm", "overlap", "bonus", "onesrow64",
              "gsel", "ropec")
FFN_CONSTS = ("ident",)


def load_consts(kb, names=None, stack=None):
    nc = kb.nc
    if not hasattr(kb, "cdram"):
        kb.cdram = {}
        for n, (shp, dt) in CONST_SHAPES.items():
            kb.cdram[n] = nc.dram_tensor("c_" + n, shp, dt, kind="ExternalInput").ap()
        kb.cdram["efull_d"] = nc.dram_tensor("c_efull", [64, S], BF16, kind="ExternalInput").ap()
    stack = stack if stack is not None else kb.root
    C = {}
    sl = kb.slot(kb.name("consts"))
    for n, (shp, dt) in CONST_SHAPES.items():
        if names is not None and n not in names:
            continue
        d = kb.cdram[n]
        t = stack.enter_context(nc.sbuf_tensor(kb.name("cs_" + n), shp, dt))
        if len(shp) == 2:
            kb.dma(kb.sp, sl, t[:, :], d)
            C[n] = t[:, :]
        else:
            kb.dma(kb.sp, sl, t[:, :, :], d)
            C[n] = t[:, :, :]
    C["efull_d"] = kb.cdram["efull_d"]
    kb.barrier()
    return C


def stage_norm_all(kb, C, xin, xin_bufs, grow_d, hnT, hnT_b):
    nc = kb.nc
    with ExitStack() as st:
        sb = lambda n, shp, dt: st.enter_context(nc.sbuf_tensor(kb.name(n), shp, dt))
        ps = lambda n, shp, dt: st.enter_context(nc.psum_tensor(kb.name(n), shp, dt))
        grow = sb("grow", [128, 1024], F32)
        grow_b = Buf()
        xt = [sb("xt", [128, 1024], F32) for _ in range(8)]
        xt_b = [Buf() for _ in range(8)]
        xt_s = [kb.slot(kb.name("xt")) for _ in range(8)]
        hn = [sb("hn", [128, 1024], BF16) for _ in range(4)]
        hn_b = [Buf() for _ in range(4)]
        sq = sb("sq", [128, 1024], BF16)
        sq_b = Buf()
        st2 = [sb("st2", [128, 12], F32) for _ in range(2)]
        st2_b = [Buf() for _ in range(2)]
        pst = [ps("pst", [128, 8, 128], BF16) for _ in range(2)]
        pst_b = [Buf() for _ in range(2)]
        s_misc = kb.slot(kb.name("misc"))
        kb.dma(kb.pool, s_misc, grow[:, :], grow_d, wr=[grow_b])
        for tb in range(NT // 4):
            o = (tb % 2) * 4
            for t4 in range(4):
                tt = tb * 4 + t4
                kb.dma(kb.sp, xt_s[o + t4], xt[o + t4][:, :], xin[tt * 128:(tt + 1) * 128, :], rd=[xin_bufs[tt]],
                       wr=[xt_b[o + t4]])
            norm_block(kb, [xt[o + i][:, :] for i in range(4)], xt_b[o:o + 4], grow[:, :], grow_b,
                       [hn[i][:, :] for i in range(4)], hn_b[0:4], sq[:, :], sq_b, st2[tb % 2][:, :],
                       st2_b[tb % 2])
            for t4 in range(4):
                tt = tb * 4 + t4
                transpose_tile(kb, C, hn[t4], hn_b[t4], pst[tt % 2], pst_b[tt % 2],
                               hnT[:, :, tt * 128:(tt + 1) * 128], hnT_b[tt],
                               kb.act if tt % 2 == 0 else kb.dve)
        kb.barrier()


def stage_outproj(kb, C, oT, oT_b, wo_d, wo_buf, xin, xin_bufs, xout, xout_bufs):
    nc = kb.nc
    with ExitStack() as st:
        sb = lambda n, shp, dt: st.enter_context(nc.sbuf_tensor(kb.name(n), shp, dt))
        ps = lambda n, shp, dt: st.enter_context(nc.psum_tensor(kb.name(n), shp, dt))
        wo = sb("wo", [128, 8, 1024], BF16)
        wo_b = Buf()
        xr = [sb("xr", [128, 1024], F32) for _ in range(3)]
        xr_b = [Buf() for _ in range(3)]
        xr_s = [kb.slot(kb.name("xr")) for _ in range(3)]
        xo = [sb("xo", [128, 1024], F32) for _ in range(3)]
        xo_b = [Buf() for _ in range(3)]
        xo_s = [kb.slot(kb.name("xo")) for _ in range(3)]
        po = [ps("po", [128, 512], F32) for _ in range(4)]
        po_b = [Buf() for _ in range(4)]
        s_misc = kb.slot(kb.name("misc"))
        wo_v = wo_d.rearrange("p (c n) -> p c n", c=8)
        kb.dma(kb.pool, s_misc, wo[:, 0:4, :], wo_v[:, 0:4, :], rd=[wo_buf], wr=[wo_b])
        kb.dma(kb.pool, s_misc, wo[:, 4:8, :], wo_v[:, 4:8, :], rd=[wo_buf], wr=[wo_b])
        kb.batch_end(s_misc, [wo_b])
        for tt in range(NT):
            k = tt % 3
            kb.dma(kb.sp, xr_s[k], xr[k][:, :], xin[tt * 128:(tt + 1) * 128, :], rd=[xin_bufs[tt]], wr=[xr_b[k]])
            for nh in range(2):
                pi = (tt * 2 + nh) % 4
                terms = [(oT[:, c, tt * 128:(tt + 1) * 128], wo[:, c, nh * 512:(nh + 1) * 512], [oT_b[tt // 4], wo_b])
                         for c in range(8)]
                kb.mm_group(po[pi][:, :], po_b[pi], terms)
                kb.op(kb.dve, lambda: nc.vector.tensor_tensor(xo[k][:, nh * 512:(nh + 1) * 512], po[pi][:, :],
                                                              xr[k][:, nh * 512:(nh + 1) * 512], ALU.add),
                      rd=[po_b[pi], xr_b[k]], wr=[xo_b[k]])
            kb.dma(kb.pool, xo_s[k], xout[tt * 128:(tt + 1) * 128, :], xo[k][:, :], rd=[xo_b[k]], wr=[xout_bufs[tt]])
        kb.barrier()


def phase_sba(kb, C, xin, xin_bufs, xout, xout_bufs, grow_d, wqk_d, wqk_buf, wv_d, wv_buf, wo_d, wo_buf,
              ngroups=8, nq=8, bg=None):
    nc = kb.nc
    with ExitStack() as st0:
        sb0 = lambda n, shp, dt: st0.enter_context(nc.sbuf_tensor(kb.name(n), shp, dt))
        oT = sb0("oT", [128, 8, S], BF16)
        oT_b = [Buf() for _ in range(8)]
        if ngroups < 8 or nq < 8:
            kb.op(kb.pool, lambda: nc.gpsimd.memset(oT[:, :, :], 0.0), wr=oT_b)
        with ExitStack() as st1:
            sb1 = lambda n, shp, dt: st1.enter_context(nc.sbuf_tensor(kb.name(n), shp, dt))
            hnT = sb1("hnT", [128, 8, S], BF16)
            hnT_b = [Buf() for _ in range(NT)]
            stage_norm_all(kb, C, xin, xin_bufs, grow_d, hnT, hnT_b)
            with ExitStack() as st:
                sb = lambda n, shp, dt: st.enter_context(nc.sbuf_tensor(kb.name(n), shp, dt))
                ps = lambda n, shp, dt: st.enter_context(nc.psum_tensor(kb.name(n), shp, dt))
                wg = [sb("wg", [128, 3, 8, 128], BF16) for _ in range(1)]
                wg_b = [Buf() for _ in range(1)]
                wg_s = [kb.slot(kb.name("wg")) for _ in range(1)]
                qz = sb("qz", [128, 2, S], BF16)
                kT = sb("kT", [128, S], BF16)
                Vt = sb("Vt", [128, NT, 2, 128], BF16)
                qz_b = [Buf() for _ in range(8)]
                kT_b = [Buf() for _ in range(8)]
                Vt_b = [Buf() for _ in range(NT)]
                NB3 = 3
                E = [sb("E", [128, 2, 512], F32) for _ in range(2)]
                E_b = [Buf() for _ in range(2)]
                SP = [sb("SP", [128, 2, 512], BF16) for _ in range(2)]
                SP_b = [Buf() for _ in range(2)]
                A = [sb("A", [128, 2, 512], BF16) for _ in range(2)]
                A_b = [Buf() for _ in range(2)]
                R34 = sb("R34", [34, 512], F32)
                R34_b = Buf()
                RHL = [sb("RHL", [128, 512], BF16) for _ in range(2)]
                RHL_b = [Buf() for _ in range(2)]
                pz = [ps("pz", [128, 2, 512], F32) for _ in range(NB3)]
                pz_b = [Buf() for _ in range(NB3)]
                pr = ps("pr", [128, 512], F32)
                pr_b = Buf()
                po = ps("po", [128, 512], F32)
                po_b = Buf()
                pq = [pz[0][:, 0, :], pz[0][:, 1, :], pz[1][:, 0, :], pz[1][:, 1, :]]
                pq_b = [pz_b[0], pz_b[0], pz_b[1], pz_b[1]]

                def load_w(c):
                    k = 0
                    kb.dma(kb.sp, wg_s[k], wg[k][:, 0, :, :], wqk_d[c].rearrange("p (kc n) -> p kc n", kc=8),
                           rd=[wqk_buf], wr=[wg_b[k]])
                    kb.dma(kb.sp, wg_s[k], wg[k][:, 1, :, :], wqk_d[8 + c].rearrange("p (kc n) -> p kc n", kc=8),
                           rd=[wqk_buf], wr=[wg_b[k]])
                    wv_v = wv_d.rearrange("p (kc n) -> p kc n", kc=8)
                    for k0_ in range(0, 8, 2):
                        kb.dma(kb.sp, wg_s[k], wg[k][:, 2, k0_:k0_ + 2, :], wv_v[:, k0_:k0_ + 2, c * 128:(c + 1) * 128],
                               rd=[wv_buf], wr=[wg_b[k]])

                kb.op(kb.pool, lambda: nc.gpsimd.memset(Vt[:, :, :, :], 0.0), wr=Vt_b)
                kb.op(kb.pool, lambda: nc.gpsimd.memset(qz[:, :, :], 0.0), wr=qz_b)
                for i_ in range(2):
                    kb.op(kb.pool, lambda: nc.gpsimd.memset(RHL[i_][:, :], 0.0), wr=[RHL_b[i_]])
                load_w(0)
                qi = 0
                for c in range(ngroups):
                    k = 0
                    for which in (0, 1):
                        for tg in range(8):
                            p = qi % 4
                            qi += 1
                            ts_ = slice(tg * 512, (tg + 1) * 512)
                            terms = [(wg[k][:, which, kc, :], hnT[:, kc, ts_],
                                      [wg_b[k]] + hnT_b[tg * 4:(tg + 1) * 4]) for kc in range(8)]
                            kb.mm_group(pq[p], pq_b[p], terms)
                            if which == 0:
                                kb.op(kb.act, lambda: nc.scalar.mul(qz[0:64, 0, ts_], pq[p][0:64, :], 0.125),
                                      rd=[pq_b[p]], wr=[qz_b[tg]])
                                kb.op(kb.act, lambda: nc.scalar.mul(qz[64:128, 1, ts_], pq[p][64:128, :], 0.125),
                                      rd=[pq_b[p]], wr=[qz_b[tg]])
                            else:
                                kb.op(kb.dve, lambda: nc.vector.tensor_copy(kT[:, ts_], pq[p]),
                                      rd=[pq_b[p]], wr=[kT_b[tg]])
                    for tt4 in range(NT // 4):
                        p = qi % 4
                        qi += 1
                        pe = kb.pe
                        pe.wait(pq_b[p].wr_deps(), wg_b[k].w, [hnT_b[tt4 * 4 + i].w for i in range(4)])
                        ins = None
                        for i in range(4):
                            tt = tt4 * 4 + i
                            for kc in range(8):
                                ins = nc.tensor.matmul(pq[p][:, i * 128:(i + 1) * 128], hnT[:, kc, tt * 128:(tt + 1) * 128],
                                                       wg[k][:, 2, kc, :], start=(kc == 0), stop=(kc == 7))
                        tok = pe.done(ins)
                        wg_b[k].note_read(tok)
                        pq_b[p].note_write(tok)
                        src = pq[p].rearrange("p (i n) -> p i n", i=4)
                        kb.op(kb.act, lambda: nc.scalar.copy(Vt[:, tt4 * 4:(tt4 + 1) * 4, 0, 0:64], src[:, :, 0:64]),
                              rd=[pq_b[p]], wr=Vt_b[tt4 * 4:(tt4 + 1) * 4])
                        kb.op(kb.act, lambda: nc.scalar.copy(Vt[:, tt4 * 4:(tt4 + 1) * 4, 1, 64:128], src[:, :, 64:128]),
                              rd=[pq_b[p]], wr=Vt_b[tt4 * 4:(tt4 + 1) * 4])
                    if c + 1 < ngroups:
                        load_w(c + 1)
                    for g in range(nq):
                        qs = slice(g * 512, (g + 1) * 512)
                        nsteps = 4 * g + 4
                        kts = [4 * g + 3 - i for i in range(nsteps)]
                        kb.op(kb.dve, lambda: nc.vector.memset(R34[:, :], 0.0), wr=[R34_b])
                        kb.op(kb.dve, lambda: nc.vector.memset(RHL[0][0:34, :], 0.0), wr=[RHL_b[0]])

                        def c0_of(i):
                            return 128 * (3 - i) if i < 4 else 0

                        def emit_Z(i):
                            kt = kts[i]
                            ks = slice(kt * 128, (kt + 1) * 128)
                            b3 = i % NB3
                            c0 = c0_of(i)
                            pe = kb.pe
                            pe.wait(pz_b[b3].wr_deps(), kT_b[kt // 4].w, qz_b[g].w)
                            ins = None
                            diag = kt >= 4 * g
                            for hh in range(2):
                                ins = nc.tensor.matmul(pz[b3][:, hh, c0:512], kT[:, ks], qz[:, hh, g * 512 + c0:(g + 1) * 512],
                                                       start=True, stop=not diag)
                                if diag:
                                    ins = nc.tensor.matmul(pz[b3][:, hh, c0:512], C["ident"],
                                                           C["negmask"][:, kt - 4 * g, c0:512], start=False, stop=True)
                            tok = pe.done(ins)
                            kT_b[kt // 4].note_read(tok)
                            qz_b[g].note_read(tok)
                            pz_b[b3].note_write(tok)
                            e = i % 2
                            kb.op(kb.act, lambda: nc.scalar.activation(out=E[e][:, :, c0:512], in_=pz[b3][:, :, c0:512],
                                                                       func=AF.Exp),
                                  rd=[pz_b[b3]], wr=[E_b[e]])
                            kb.op(kb.act, lambda: nc.scalar.activation(out=SP[e][:, :, c0:512], in_=E[e][:, :, c0:512],
                                                                       func=AF.Ln, bias=1.0),
                                  rd=[E_b[e]], wr=[SP_b[e]])

                        def emit_R(i):
                            b3 = i % 2
                            c0 = c0_of(i)
                            terms = [(C["onesel"][:, hh, :], SP[b3][:, hh, c0:512], [SP_b[b3]]) for hh in range(2)]
                            kb.mm_group(pr[:, c0:512], pr_b, terms)
                            kb.op(kb.dve, lambda: nc.vector.tensor_tensor(R34[:, c0:512], R34[:, c0:512], pr[0:34, c0:512],
                                                                          ALU.subtract),
                                  rd=[pr_b], wr=[R34_b])
                            nx = RHL[(i + 1) % 2]
                            nx_b = RHL_b[(i + 1) % 2]
                            kb.op(kb.dve, lambda: nc.vector.tensor_copy(nx[0:34, :], R34[:, :]), rd=[R34_b], wr=[nx_b])
                            kb.op(kb.dve, lambda: nc.vector.tensor_tensor(nx[32:34, :], R34[32:34, :], nx[32:34, :],
                                                                          ALU.subtract),
                                  rd=[R34_b], wr=[nx_b])

                        def emit_C(i):
                            b3 = i % NB3
                            b2 = i % 2
                            c0 = c0_of(i)
                            pe = kb.pe
                            pe.wait(pz_b[b3].wr_deps(), SP_b[b2].w, RHL_b[i % 2].w)
                            ins = None
                            for hh in range(2):
                                nc.tensor.matmul(pz[b3][:, hh, c0:512], C["uincneg"], SP[b2][:, hh, c0:512], start=False,
                                                 stop=False, skip_group_check=True)
                                ins = nc.tensor.matmul(pz[b3][:, hh, c0:512], C["sel"][:, hh, :], RHL[i % 2][:, c0:512],
                                                       start=False, stop=True, skip_group_check=True)
                            tok = pe.done(ins)
                            SP_b[b2].note_read(tok)
                            RHL_b[i % 2].note_read(tok)
                            pz_b[b3].note_write(tok)
                            kb.op(kb.act, lambda: nc.scalar.activation(out=A[b2][:, :, c0:512], in_=pz[b3][:, :, c0:512],
                                                                       func=AF.Exp),
                                  rd=[pz_b[b3]], wr=[A_b[b2]])

                        def emit_AV(i):
                            kt = kts[i]
                            b3 = i % 2
                            c0 = c0_of(i)
                            pe = kb.pe
                            deps = [A_b[b3].w, Vt_b[kt].w]
                            if i == 0:
                                deps += po_b.wr_deps()
                            pe.wait(deps)
                            ins = None
                            if i == 0:
                                nc.tensor.matmul(po[:, :], C["ident"], C["zeros512"], start=True, stop=False,
                                                 skip_group_check=True)
                            for hh in range(2):
                                ins = nc.tensor.matmul(po[:, c0:512], Vt[:, kt, hh, :], A[b3][:, hh, c0:512],
                                                       start=False, stop=(i == nsteps - 1 and hh == 1),
                                                       skip_group_check=True)
                            tok = pe.done(ins)
                            A_b[b3].note_read(tok)
                            Vt_b[kt].note_read(tok)
                            if i == nsteps - 1:
                                po_b.note_write(tok)

                        lvl = DBG.get("lvl", 9)
                        emit_Z(0)
                        for i in range(nsteps):
                            if i + 1 < nsteps:
                                emit_Z(i + 1)
                                emit_R(i)
                            emit_C(i)
                            if i >= 1:
                                emit_AV(i - 1)
                        emit_AV(nsteps - 1)
                        kb.op(kb.dve, lambda: nc.vector.tensor_copy(oT[:, c, qs], po[:, :]), rd=[po_b], wr=[oT_b[g]])
                        if bg is not None:
                            bg.emit(6)
                if bg is not None:
                    bg.flush()
                kb.barrier()
        stage_outproj(kb, C, oT, oT_b, wo_d, wo_buf, xin, xin_bufs, xout, xout_bufs)


TWO_PI = 6.283185307179586
NCMP = 255


def nsa_stage_proj(kb, C, hnT, hnT_b, posb_d, wfm_d, wfm_buf, wgt_d, wgt_buf, wtm_d, wtm_buf,
                   q_s, q_sb, kv_s, kv_sb, Vtm, Vtm_b, sigT, sigT_b, cmpw, kcmp, kcmp_b, vcmp, vcmp_b):
    nc = kb.nc
    with ExitStack() as st:
        sb = lambda n, shp, dt: st.enter_context(nc.sbuf_tensor(kb.name(n), shp, dt))
        ps = lambda n, shp, dt: st.enter_context(nc.psum_tensor(kb.name(n), shp, dt))
        cosF = sb("cosF", [128, S], F32)
        sinS = sb("sinS", [128, S], F32)
        cs_b = Buf()
        t_b = Buf()
        s_misc = kb.slot(kb.name("misc"))
        s_miscp = kb.slot(kb.name("miscp"))
        st_tmp = ExitStack()
        HS = S // 2
        posi = st_tmp.enter_context(nc.sbuf_tensor(kb.name("posi"), [128, HS], I32))
        ang = st_tmp.enter_context(nc.sbuf_tensor(kb.name("ang"), [128, HS], F32))
        tmp = st_tmp.enter_context(nc.sbuf_tensor(kb.name("tmpang"), [128, HS], F32))
        kf = st_tmp.enter_context(nc.sbuf_tensor(kb.name("kfang"), [128, HS], F32))
        C1 = 6.28125
        C2 = TWO_PI - 6.28125

        def reduce_sin(dst, shift, post_scale):
            if shift != 0.0:
                kb.op(kb.dve, lambda: nc.vector.tensor_scalar(tmp[:, :], ang[:, :], shift, None, ALU.add),
                      rd=[t_b], wr=[t_b])
                src = tmp
            else:
                src = ang
            kb.op(kb.dve, lambda: nc.vector.tensor_scalar(kf[:, :], src[:, :], 1.0 / TWO_PI, None, ALU.mult),
                  rd=[t_b], wr=[t_b])
            kb.op(kb.dve, lambda: nc.vector.tensor_copy(posi[:, :], kf[:, :]), rd=[t_b], wr=[t_b])
            kb.op(kb.dve, lambda: nc.vector.tensor_copy(kf[:, :], posi[:, :]), rd=[t_b], wr=[t_b])
            kb.op(kb.dve, lambda: nc.vector.scalar_tensor_tensor(out=tmp[:, :], in0=kf[:, :], scalar=-C1, in1=src[:, :],
                                                                 op0=ALU.mult, op1=ALU.add), rd=[t_b], wr=[t_b])
            kb.op(kb.dve, lambda: nc.vector.scalar_tensor_tensor(out=tmp[:, :], in0=kf[:, :], scalar=-C2, in1=tmp[:, :],
                                                                 op0=ALU.mult, op1=ALU.add), rd=[t_b], wr=[t_b])
            kb.op(kb.dve, lambda: nc.vector.tensor_scalar(kf[:, :], tmp[:, :], float(np.pi), TWO_PI, ALU.is_gt, ALU.mult),
                  rd=[t_b], wr=[t_b])
            kb.op(kb.dve, lambda: nc.vector.tensor_tensor(tmp[:, :], tmp[:, :], kf[:, :], ALU.subtract),
                  rd=[t_b], wr=[t_b])
            kb.op(kb.dve, lambda: nc.vector.tensor_scalar(kf[:, :], tmp[:, :], -float(np.pi), TWO_PI, ALU.is_lt, ALU.mult),
                  rd=[t_b], wr=[t_b])
            kb.op(kb.dve, lambda: nc.vector.tensor_tensor(tmp[:, :], tmp[:, :], kf[:, :], ALU.add),
                  rd=[t_b], wr=[t_b])
            kb.op(kb.act, lambda: nc.scalar.activation(out=tmp[:, :], in_=tmp[:, :], func=AF.Sin), rd=[t_b], wr=[t_b])
            if post_scale is None:
                kb.op(kb.dve, lambda: nc.vector.tensor_copy(dst, tmp[:, :]), rd=[t_b], wr=[cs_b])
            else:
                kb.op(kb.dve, lambda: nc.vector.tensor_scalar(dst, tmp[:, :], post_scale, None, ALU.mult),
                      rd=[t_b], wr=[cs_b])

        for hf in range(2):
            cols = slice(hf * HS, (hf + 1) * HS)
            kb.dma(kb.sp, s_misc, posi[:, :], posb_d[:, cols], wr=[t_b])
            kb.op(kb.dve, lambda: nc.vector.tensor_copy(ang[:, :], posi[:, :]), rd=[t_b], wr=[t_b])
            kb.op(kb.dve, lambda: nc.vector.tensor_scalar(ang[:, :], ang[:, :], C["ropec"][:, 0:1], None, ALU.mult),
                  rd=[t_b], wr=[t_b])
            reduce_sin(sinS[:, cols], 0.0, C["ropec"][:, 1:2])
            reduce_sin(cosF[:, cols], float(np.pi / 2), None)
        kb.barrier()
        st_tmp.close()

        st_p = ExitStack()
        sbp = lambda n, shp, dt: st_p.enter_context(nc.sbuf_tensor(kb.name(n), shp, dt))
        wch = [sbp("wch", [128, 2, 8, 128], BF16) for _ in range(2)]
        wch_b = [Buf() for _ in range(2)]
        wch_s = [kb.slot(kb.name("wch")) for _ in range(2)]
        wgt = sbp("wgt", [128, 8, 48], BF16)
        wtm = sbp("wtm", [128, 8, 256], BF16)
        wgt_b = Buf()
        wtm_b = Buf()
        kb.dma(kb.pool, s_miscp, wgt[:, :, :], wgt_d.rearrange("p (kc n) -> p kc n", kc=8), rd=[wgt_buf], wr=[wgt_b])
        kb.dma(kb.pool, s_miscp, wtm[:, :, :], wtm_d.rearrange("p (kc n) -> p kc n", kc=8), rd=[wtm_buf], wr=[wtm_b])
        kb.batch_end(s_miscp, [wgt_b, wtm_b])
        pa = [ps("pa", [128, 512], F32) for _ in range(2)]
        pa_b = [Buf() for _ in range(2)]
        pb = [ps("pb", [128, 512], F32) for _ in range(2)]
        pb_b = [Buf() for _ in range(2)]
        t1 = [sbp("t1", [128, 512], F32) for _ in range(2)]
        t1_b = [Buf() for _ in range(2)]
        t2 = [sbp("t2", [128, 512], F32) for _ in range(2)]
        t2_b = [Buf() for _ in range(2)]
        ro = [sbp("ro", [128, 512], BF16) for _ in range(3)]
        ro_b = [Buf() for _ in range(3)]
        ro_s = [kb.slot(kb.name("ro")) for _ in range(3)]
        jobs = [(c, 8 + c, ("q", c)) for c in range(8)]
        jobs += [(16, None, ("kv", 0)), (17, None, ("kv", 1)), (18, 19, ("kv", 2)), (20, 21, ("kv", 3))]
        ci = 0
        ri = 0
        PL = DBG.get("proj", 9)
        for (cp, cq, dest) in (jobs if PL >= 2 else []):
            k = ci % 2
            ci += 1
            kb.dma(kb.sp, wch_s[k], wch[k][:, 0, :, :], wfm_d[cp].rearrange("p (kc n) -> p kc n", kc=8),
                   rd=[wfm_buf], wr=[wch_b[k]])
            if cq is not None:
                kb.dma(kb.sp, wch_s[k], wch[k][:, 1, :, :], wfm_d[cq].rearrange("p (kc n) -> p kc n", kc=8),
                       rd=[wfm_buf], wr=[wch_b[k]])
            for tg in range(8):
                ts_ = slice(tg * 512, (tg + 1) * 512)
                p = tg % 2
                r = ri % 3
                ri += 1
                terms = [(wch[k][:, 0, kc, :], hnT[:, kc, ts_], [wch_b[k]] + hnT_b[tg * 4:(tg + 1) * 4]) for kc in range(8)]
                kb.mm_group(pa[p][:, :], pa_b[p], terms)
                if cq is not None:
                    terms = [(wch[k][:, 1, kc, :], hnT[:, kc, ts_], [wch_b[k]] + hnT_b[tg * 4:(tg + 1) * 4])
                             for kc in range(8)]
                    kb.mm_group(pb[p][:, :], pb_b[p], terms)
                    kb.op(kb.dve, lambda: nc.vector.tensor_tensor(t1[p][:, :], pa[p][:, :], cosF[:, ts_], ALU.mult),
                          rd=[pa_b[p], cs_b], wr=[t1_b[p]])
                    kb.op(kb.dve, lambda: nc.vector.tensor_tensor(t2[p][:, :], pb[p][:, :], sinS[:, ts_], ALU.mult),
                          rd=[pb_b[p], cs_b], wr=[t2_b[p]])
                    kb.op(kb.pool, lambda: nc.gpsimd.tensor_tensor(ro[r][:, :], t1[p][:, :], t2[p][:, :], ALU.add),
                          rd=[t1_b[p], t2_b[p]], wr=[ro_b[r]])
                else:
                    kb.op(kb.act, lambda: nc.scalar.copy(ro[r][:, :], pa[p][:, :]), rd=[pa_b[p]], wr=[ro_b[r]])
                if dest[0] == "q":
                    c = dest[1]
                    kb.dma(kb.pool, ro_s[r], q_s[2 * c:2 * c + 2, :, ts_].rearrange("h d t -> (h d) t"), ro[r][:, :],
                           rd=[ro_b[r]], wr=[q_sb])
                else:
                    kb.dma(kb.pool, ro_s[r], kv_s[dest[1], :, ts_], ro[r][:, :], rd=[ro_b[r]], wr=[kv_sb])
        kb.op(kb.pool, lambda: nc.gpsimd.memset(sigT[:, :], 0.0), wr=[sigT_b])
        for tg in range(8 if PL >= 3 else 0):
            ts_ = slice(tg * 512, (tg + 1) * 512)
            p = tg % 2
            terms = [(wgt[:, kc, :], hnT[:, kc, ts_], [wgt_b] + hnT_b[tg * 4:(tg + 1) * 4]) for kc in range(8)]
            kb.mm_group(pa[p][0:48, :], pa_b[p], terms)
            kb.op(kb.act, lambda: nc.scalar.activation(out=sigT[0:48, ts_], in_=pa[p][0:48, :], func=AF.Sigmoid),
                  rd=[pa_b[p]], wr=[sigT_b])
        kb.op(kb.pool, lambda: nc.gpsimd.memset(Vtm[:, :, :, :], 1.0), wr=Vtm_b)
        for tt2 in range(NT // 2 if PL >= 4 else 0):
            p = tt2 % 2
            pe = kb.pe
            pe.wait(pb_b[p].wr_deps(), wtm_b.w, [hnT_b[tt2 * 2 + i].w for i in range(2)])
            ins = None
            for i in range(2):
                tt = tt2 * 2 + i
                for kc in range(8):
                    ins = nc.tensor.matmul(pb[p][:, i * 256:(i + 1) * 256], hnT[:, kc, tt * 128:(tt + 1) * 128],
                                           wtm[:, kc, :], start=(kc == 0), stop=(kc == 7))
            tok = pe.done(ins)
            wtm_b.note_read(tok)
            pb_b[p].note_write(tok)
            src = pb[p][:, :].rearrange("p (i g d) -> p i g d", i=2, g=4)
            eng = kb.act if tt2 % 2 == 0 else kb.dve
            for i in range(2):
                if True:
                    kb.op(kb.act, lambda: nc.scalar.copy(Vtm[:, tt2 * 2 + i, :, 0:64], src[:, i, :, :]),
                          rd=[pb_b[p]], wr=[Vtm_b[tt2 * 2 + i]])
                else:
                    kb.op(kb.dve, lambda: nc.vector.tensor_copy(Vtm[:, tt2 * 2 + i, :, 0:64], src[:, i, :, :]),
                          rd=[pb_b[p]], wr=[Vtm_b[tt2 * 2 + i]])
        kb.barrier()
        st_p.close()
        t1 = [sb("t1c", [64, 256], F32)]
        t2 = [sb("t2c", [64, 256], F32)]
        t1_b = [Buf()]
        t2_b = [Buf()]
        (w1_d, w1_buf, w2_d, w2_buf, pe_d, pe_buf) = cmpw
        W2 = sb("W2", [128, 192], BF16)
        peT = sb("peT", [64, 2, 32], BF16)
        w_b = Buf()
        kb.dma(kb.sp, s_misc, W2[:, :], w2_d, rd=[w2_buf], wr=[w_b])
        kb.dma(kb.sp, s_misc, peT[:, :, :], pe_d.rearrange("p (k l) -> p k l", k=2), rd=[pe_buf], wr=[w_b])
        Xs = [sb("Xc", [64, S], BF16) for _ in range(2)]
        X_bs = [Buf() for _ in range(2)]
        X_ss = [kb.slot(kb.name("Xc")) for _ in range(2)]
        W1s = [sb("W1", [64, 32, 128], BF16) for _ in range(2)]
        hb = sb("hb", [128, 1], F32)
        u = sb("cu", [128, NCMP], F32)
        u2 = sb("cu2", [128, NCMP], F32)
        sg = sb("csg", [128, NCMP], F32)
        hd = sb("chd", [128, 256], BF16)
        c_b = Buf()
        ph = ps("ph", [128, 256], F32)
        ph_b = Buf()
        phb = ps("phb", [128, 2], F32)
        phb_b = Buf()
        kb.op(kb.pool, lambda: nc.gpsimd.memset(kcmp[:, :, :], 0.0), wr=[kcmp_b])
        kb.op(kb.pool, lambda: nc.gpsimd.memset(vcmp[:, :, :, :], 1.0), wr=[vcmp_b])
        for kv in range(2 if PL >= 5 else 0):
            W1 = W1s[kv]
            kb.dma(kb.sp, s_misc, W1[:, :, :], w1_d[kv].rearrange("p (l n) -> p l n", l=32), rd=[w1_buf], wr=[w_b])
            for g in range(2):
                X = Xs[g]
                X_b = X_bs[g]
                kb.dma(kb.sp, X_ss[g], X[:, :], kv_s[kv, g * 64:(g + 1) * 64, :], rd=[kv_sb], wr=[X_b])
                terms = [(W1[:, l, :], X[:, l:l + 16 * (NCMP - 1) + 1:16], [w_b, X_b]) for l in range(32)]
                kb.mm_group(ph[:, 0:NCMP], ph_b, terms)
                terms = [(W1[:, l, :], peT[:, kv, l:l + 1], [w_b]) for l in range(32)]
                kb.mm_group(phb[:, 0:1], phb_b, terms)
                kb.op(kb.dve, lambda: nc.vector.tensor_copy(hb[:, :], phb[:, 0:1]), rd=[phb_b], wr=[c_b])
                kb.op(kb.act, lambda: nc.scalar.activation(out=u[:, :], in_=ph[:, 0:NCMP], func=AF.Identity,
                                                           bias=hb[:, 0:1]), rd=[ph_b, c_b], wr=[c_b])
                kb.op(kb.dve, lambda: nc.vector.tensor_tensor(u2[:, :], u[:, :], u[:, :], ALU.mult), rd=[c_b], wr=[c_b])
                kb.op(kb.dve, lambda: nc.vector.tensor_scalar(u2[:, :], u2[:, :], 0.044715, 1.0, ALU.mult, ALU.add),
                      rd=[c_b], wr=[c_b])
                kb.op(kb.dve, lambda: nc.vector.tensor_tensor(u2[:, :], u2[:, :], u[:, :], ALU.mult), rd=[c_b], wr=[c_b])
                kb.op(kb.act, lambda: nc.scalar.activation(out=sg[:, :], in_=u2[:, :], func=AF.Sigmoid,
                                                           scale=1.5957691216057308), rd=[c_b], wr=[c_b])
                kb.op(kb.dve, lambda: nc.vector.tensor_tensor(hd[:, 0:NCMP], u[:, :], sg[:, :], ALU.mult),
                      rd=[c_b], wr=[c_b])
                if kv == 0:
                    kb.mm_group(pa[0][0:64, 0:NCMP], pa_b[0], [(W2[:, 0:64], hd[:, 0:NCMP], [w_b, c_b])])
                    kb.mm_group(pb[0][0:64, 0:NCMP], pb_b[0], [(W2[:, 64:128], hd[:, 0:NCMP], [w_b, c_b])])
                    cs_sl = slice(31, 31 + 16 * (NCMP - 1) + 1, 16)
                    kb.op(kb.dve, lambda: nc.vector.tensor_tensor(t1[0][0:64, 0:NCMP], pa[0][0:64, 0:NCMP],
                                                                  cosF[0:64, cs_sl], ALU.mult),
                          rd=[pa_b[0], cs_b], wr=[t1_b[0]])
                    kb.op(kb.dve, lambda: nc.vector.tensor_tensor(t2[0][0:64, 0:NCMP], pb[0][0:64, 0:NCMP],
                                                                  sinS[0:64, cs_sl], ALU.mult),
                          rd=[pb_b[0], cs_b], wr=[t2_b[0]])
                    kb.op(kb.dve, lambda: nc.vector.tensor_tensor(kcmp[0:64, g, 0:NCMP], t1[0][0:64, 0:NCMP],
                                                                  t2[0][0:64, 0:NCMP], ALU.add),
                          rd=[t1_b[0], t2_b[0]], wr=[kcmp_b])
                else:
                    for ct in range(2):
                        rows = 128 if ct == 0 else NCMP - 128
                        kb.mm_group(pa[1][0:rows, 0:64], pa_b[1],
                                    [(hd[:, ct * 128:ct * 128 + rows], W2[:, 128:192], [w_b, c_b])])
                        kb.op(kb.dve, lambda: nc.vector.tensor_copy(vcmp[0:rows, ct, g, 0:64], pa[1][0:rows, 0:64]),
                              rd=[pa_b[1]], wr=[vcmp_b])
        kb.barrier()


def nsa_stage_select(kb, C, q_s, q_sb, kcmp, kcmp_b, mb_s, mb_sb, nqt=NT):
    nc = kb.nc
    with ExitStack() as st:
        sb = lambda n, shp, dt: st.enter_context(nc.sbuf_tensor(kb.name(n), shp, dt))
        ps = lambda n, shp, dt: st.enter_context(nc.psum_tensor(kb.name(n), shp, dt))
        Q8 = sb("Q8", [64, 8, S], BF16)
        Q8_b = Buf()
        MBT = sb("MBT", [64, S], BF16)
        MBT_b = Buf()
        P = [[sb("P1", [128, NCMP], F32) for _ in range(8)] for _ in range(3)]
        P_b = [[Buf() for _ in range(8)] for _ in range(3)]
        den = [sb("den1", [128, 16], F32) for _ in range(3)]
        den_b = [Buf() for _ in range(3)]
        accs = [sb("acc1", [128, 256], F32) for _ in range(2)]
        accs_b = [Buf() for _ in range(2)]
        accTs = [sb("accT", [128, 2, 128], F32) for _ in range(2)]
        accTs_b = [Buf() for _ in range(2)]
        score = sb("score", [128, 64], F32)
        work = sb("work", [128, 64], F32)
        m8 = sb("m8", [128, 16], F32)
        thr = sb("thr", [128, 1], F32)
        mbq = sb("mbq", [128, 64], BF16)
        sc_b = Buf()
        NPS = 3
        pss = [ps("pss", [128, 256], F32) for _ in range(NPS)]
        pss_b = [Buf() for _ in range(NPS)]
        pts = [ps("pt1", [128, 2, 128], F32) for _ in range(2)]
        pts_b = [Buf() for _ in range(2)]
        pimps = [ps("pimp", [128, 64], F32) for _ in range(2)]
        pimps_b = [Buf() for _ in range(2)]
        pmt = ps("pmt", [64, 128], BF16)
        pmt_b = Buf()
        s_misc = kb.slot(kb.name("misc"))
        s_out = kb.slot(kb.name("mbout"))
        hi = 0
        for g in range(2):
            for j in range(8):
                kb.dma(kb.sp, s_misc, Q8[:, j, :], q_s[g * 8 + j], rd=[q_sb], wr=[Q8_b])
            for i_ in range(2):
                kb.op(kb.dve, lambda: nc.vector.memset(accs[i_][:, :], 0.0), wr=[accs_b[i_]])

            def stage_a(qt):
                nonlocal hi
                qs = slice(qt * 128, (qt + 1) * 128)
                b2 = qt % 3
                for j in range(8):
                    p4 = hi % NPS
                    hi += 1
                    terms = [(Q8[:, j, qs], kcmp[0:64, g, 0:NCMP], [Q8_b, kcmp_b]),
                             (C["ident"], C["cmask_tm"][:, 248 - 8 * qt:248 - 8 * qt + NCMP], [])]
                    kb.mm_group(pss[p4][:, 0:NCMP], pss_b[p4], terms)
                    kb.op(kb.act, lambda: nc.scalar.activation(out=P[b2][j][:, :], in_=pss[p4][:, 0:NCMP], func=AF.Exp,
                                                               scale=0.125, accum_out=den[b2][:, j:j + 1]),
                          rd=[pss_b[p4]], wr=[P_b[b2][j], den_b[b2]])

            def stage_b1(qt):
                qs = slice(qt * 128, (qt + 1) * 128)
                b2 = qt % 2
                acc, acc_b, accT, accT_b = accs[b2], accs_b[b2], accTs[b2], accTs_b[b2]
                pt, pt_b, pimp, pimp_b = pts[b2], pts_b[b2], pimps[b2], pimps_b[b2]
                b2 = qt % 3
                kb.op(kb.dve, lambda: nc.vector.tensor_scalar(den[b2][:, 8:16], den[b2][:, 0:8], 1e-30, None, ALU.add),
                      rd=[den_b[b2]], wr=[den_b[b2]])
                kb.op(kb.dve, lambda: nc.vector.reciprocal(den[b2][:, 8:16], den[b2][:, 8:16]),
                      rd=[den_b[b2]], wr=[den_b[b2]])
                for j in range(8):
                    if j == 0:
                        kb.op(kb.dve, lambda: nc.vector.tensor_scalar(acc[:, 0:NCMP], P[b2][j][:, :], den[b2][:, 8:9], None,
                                                                      ALU.mult),
                              rd=[P_b[b2][j], den_b[b2]], wr=[acc_b])
                    else:
                        kb.op(kb.dve, lambda: nc.vector.scalar_tensor_tensor(out=acc[:, 0:NCMP], in0=P[b2][j][:, :],
                                                                             scalar=den[b2][:, 8 + j:9 + j],
                                                                             in1=acc[:, 0:NCMP],
                                                                             op0=ALU.mult, op1=ALU.add),
                              rd=[P_b[b2][j], den_b[b2]], wr=[acc_b])
                pe = kb.pe
                pe.wait(pt_b.wr_deps(), acc_b.w)
                nc.tensor.transpose(pt[:, 0, :], acc[:, 0:128], C["ident32"])
                ins = nc.tensor.transpose(pt[:, 1, :], acc[:, 128:256], C["ident32"])
                tok = pe.done(ins)
                acc_b.note_read(tok)
                pt_b.note_write(tok)
                kb.op(kb.act, lambda: nc.scalar.copy(accT[:, :, :], pt[:, :, :]), rd=[pt_b], wr=[accT_b])
                terms = [(accT[:, 0, :], C["overlap"][:, 0, :], [accT_b]),
                         (accT[0:127, 1, :], C["overlap"][0:127, 1, :], [accT_b])]
                kb.mm_group(pimp[:, :], pimp_b, terms)

            def stage_b2(qt):
                qs = slice(qt * 128, (qt + 1) * 128)
                b2 = qt % 2
                pimp, pimp_b = pimps[b2], pimps_b[b2]
                pe = kb.pe
                kb.op(kb.dve, lambda: nc.vector.tensor_tensor(score[:, :], pimp[:, :],
                                                              C["bonus"][:, 62 - 2 * qt:62 - 2 * qt + 64], ALU.add),
                      rd=[pimp_b], wr=[sc_b])
                kb.op(kb.dve, lambda: nc.vector.tensor_tensor(score[:, :], score[:, :], C["bonus"][:, 128:192], ALU.add),
                      rd=[sc_b], wr=[sc_b])
                kb.op(kb.dve, lambda: nc.vector.max(out=m8[:, 0:8], in_=score[:, :]), rd=[sc_b], wr=[sc_b])
                kb.op(kb.dve, lambda: nc.vector.match_replace(out=work[:, :], in_to_replace=m8[:, 0:8],
                                                              in_values=score[:, :], imm_value=-3.0e38),
                      rd=[sc_b], wr=[sc_b])
                kb.op(kb.dve, lambda: nc.vector.max(out=m8[:, 8:16], in_=work[:, :]), rd=[sc_b], wr=[sc_b])
                kb.op(kb.dve, lambda: nc.vector.tensor_reduce(out=thr[:, :], in_=m8[:, 8:16], axis=AX.X, op=ALU.min),
                      rd=[sc_b], wr=[sc_b])
                kb.op(kb.dve, lambda: nc.vector.tensor_scalar(mbq[:, :], score[:, :], thr[:, 0:1], NEGM, ALU.is_lt,
                                                              ALU.mult),
                      rd=[sc_b], wr=[sc_b])
                pe.wait(pmt_b.wr_deps(), sc_b.w)
                ins = nc.tensor.transpose(pmt[:, :], mbq[:, :], C["ident"])
                tok = pe.done(ins)
                sc_b.note_read(tok)
                pmt_b.note_write(tok)
                kb.op(kb.act, lambda: nc.scalar.copy(MBT[:, qs], pmt[:, :]), rd=[pmt_b], wr=[MBT_b])

            stage_a(0)
            if nqt > 1:
                stage_a(1)
            stage_b1(0)
            for qt in range(nqt):
                if qt + 2 < nqt:
                    stage_a(qt + 2)
                if qt + 1 < nqt:
                    stage_b1(qt + 1)
                stage_b2(qt)
            kb.dma(kb.pool, s_out, mb_s[g], MBT[:, :], rd=[MBT_b], wr=[mb_sb])
        kb.barrier()


def nsa_stage_attn(kb, C, q_s, q_sb, kv_s, kv_sb, mb_s, mb_sb, efull_d, kcmp, kcmp_b, vcmp, vcmp_b, Vtm, Vtm_b,
                   sigT, sigT_b, o_s, o_sb, heads=range(16), nqg=8):
    nc = kb.nc
    with ExitStack() as st:
        sb = lambda n, shp, dt: st.enter_context(nc.sbuf_tensor(kb.name(n), shp, dt))
        ps = lambda n, shp, dt: st.enter_context(nc.psum_tensor(kb.name(n), shp, dt))
        QM = [sb("QM", [128, S], BF16) for _ in range(2)]
        QM_b = [Buf() for _ in range(2)]
        QM_s = [kb.slot(kb.name("QM")) for _ in range(2)]
        ksE = sb("ksE", [128, S], BF16)
        kwT = sb("kwT", [128, S], BF16)
        kk_b = Buf()
        s_kk = kb.slot(kb.name("kk"))
        NPT = 6
        PT = [sb("PT", [128, 512], BF16) for _ in range(NPT)]
        PT_b = [Buf() for _ in range(NPT)]
        dhl = [sb("dhl", [128, 2, 512], BF16) for _ in range(2)]
        dhl_b = [Buf() for _ in range(2)]
        rd = [sb("rd", [64, 512], F32) for _ in range(2)]
        rd_b = [Buf() for _ in range(2)]
        wv = sb("wv", [64, 512], F32)
        tm = sb("tm", [64, 512], F32)
        e_b = Buf()
        oacc = [sb("oacc", [64, 512], F32) for _ in range(2)]
        oacc_b = [Buf() for _ in range(2)]
        obf = [sb("obf", [64, 512], BF16) for _ in range(2)]
        obf_b = [Buf() for _ in range(2)]
        obf_s = [kb.slot(kb.name("obf")) for _ in range(2)]
        NZ = 3
        pz = [ps("pz", [128, 512], F32) for _ in range(NZ)]
        pz_b = [Buf() for _ in range(NZ)]
        po = [ps("po2", [128, 512], F32) for _ in range(3)]
        po_b = [Buf() for _ in range(3)]
        pg = [ps("pg", [128, 512], F32) for _ in range(2)]
        pg_b = [Buf() for _ in range(2)]
        kb.op(kb.pool, lambda: nc.gpsimd.memset(kwT[:, :], 0.0), wr=[kk_b])
        for i_ in range(2):
            kb.op(kb.pool, lambda: nc.gpsimd.memset(dhl[i_][:, :, :], 0.0), wr=[dhl_b[i_]])
        cur_g = -1
        zi = 0
        pi = 0
        oi = 0
        ei = 0
        defer = []

        def run_deferred(force=False):
            keep = []
            for item in defer:
                item[0] -= 1
                if item[0] <= 0 or force:
                    item[1]()
                else:
                    keep.append(item)
            defer[:] = keep

        for hn_, h in enumerate(heads):
            g = h // 8
            qm = QM[hn_ % 2]
            qm_b = QM_b[hn_ % 2]
            kb.dma(kb.sp, QM_s[hn_ % 2], qm[0:64, :], q_s[h], rd=[q_sb], wr=[qm_b])
            kb.dma(kb.sp, QM_s[hn_ % 2], qm[64:128, :], mb_s[g], rd=[mb_sb], wr=[qm_b])
            if g != cur_g:
                cur_g = g
                kb.dma(kb.sp, s_kk, ksE[0:64, :], kv_s[2, g * 64:(g + 1) * 64, :], rd=[kv_sb], wr=[kk_b])
                kb.dma(kb.sp, s_kk, ksE[64:128, :], efull_d, wr=[kk_b])
                kb.dma(kb.sp, s_kk, kwT[0:64, :], kv_s[3, g * 64:(g + 1) * 64, :], rd=[kv_sb], wr=[kk_b])
            for qg in range(nqg):
                qs = slice(qg * 512, (qg + 1) * 512)
                tiles = []
                m0 = C["negmask_cmp"][:, qg, :] if qg <= 4 else None
                tiles.append((0, kcmp[:, g, 0:128], m0, 128, vcmp[:, 0, g, 0:65], [kcmp_b, vcmp_b], 0, 512))
                if qg >= 4:
                    tiles.append((0, kcmp[:, g, 128:NCMP], C["negmask_cmp"][0:127, qg - 4, :], 127,
                                  vcmp[0:127, 1, g, 0:65], [kcmp_b, vcmp_b], 0, 512))
                for kt in range(0, 4 * qg + 4):
                    j = kt - 4 * qg
                    m = C["negmask_le"][:, j, :] if j >= 0 else None
                    tiles.append((1, ksE[:, kt * 128:(kt + 1) * 128], m, 128, Vtm[:, kt, g, 0:65], [kk_b, Vtm_b[kt]],
                                  128 * j if j >= 0 else 0, 512))
                wl = []
                for kt in range(max(0, 4 * qg - 4), 4 * qg):
                    jp = kt - (4 * qg - 4)
                    wl.append((2, kwT[:, kt * 128:(kt + 1) * 128], C["negmask_win"][:, jp, :], 128,
                               Vtm[:, kt, 2 + g, 0:65], [kk_b, Vtm_b[kt]], 0, 128 * (jp + 1)))
                wl.reverse()
                for kt in range(4 * qg, 4 * qg + 4):
                    j = kt - 4 * qg
                    wl.append((2, kwT[:, kt * 128:(kt + 1) * 128], C["negmask_le"][:, j, :], 128,
                               Vtm[:, kt, 2 + g, 0:65], [kk_b, Vtm_b[kt]], 128 * j, 512))
                tiles += wl
                nt_ = len(tiles)
                first = {}
                last = {}
                for i, t in enumerate(tiles):
                    first.setdefault(t[0], i)
                    last[t[0]] = i
                slots = {}
                oa = oacc[oi % 2]
                oa_b = oacc_b[oi % 2]
                ob = obf[oi % 2]
                ob_b = obf_b[oi % 2]
                ob_s = obf_s[oi % 2]
                oi += 1
                done_br = []

                def emit_qk(i, tiles=tiles, slots=slots, qm=qm, qm_b=qm_b, qs=qs, qg=qg):
                    nonlocal zi, pi
                    br, l, m, rows, va, bufs, c0, c1 = tiles[i]
                    z = zi % NZ
                    zi += 1
                    p = pi % NPT
                    pi += 1
                    slots[i] = p
                    terms = [(l, qm[:, qg * 512 + c0:qg * 512 + c1], [bufs[0], qm_b])]
                    if m is not None:
                        terms.append((C["ident"][0:rows, 0:rows], m[:, c0:c1], []))
                    kb.mm_group(pz[z][0:rows, c0:c1], pz_b[z], terms)
                    kb.op(kb.act, lambda: nc.scalar.activation(out=PT[p][0:rows, c0:c1], in_=pz[z][0:rows, c0:c1],
                                                               func=AF.Exp, scale=0.125), rd=[pz_b[z]], wr=[PT_b[p]])

                def make_epilogue(br, h=h, qs=qs, oa=oa, oa_b=oa_b, ob=ob, ob_b=ob_b, ob_s=ob_s, done_br=done_br):
                    nonlocal ei
                    e = ei % 2
                    ei += 1
                    kb.op(kb.dve, lambda: nc.vector.tensor_copy(dhl[e][64:65, 0, :], po[br][64:65, :]), rd=[po_b[br]],
                          wr=[dhl_b[e]])
                    kb.op(kb.dve, lambda: nc.vector.scalar_tensor_tensor(out=dhl[e][64:65, 1, :], in0=po[br][64:65, :],
                                                                         scalar=1e-30, in1=dhl[e][64:65, 0, :],
                                                                         op0=ALU.add, op1=ALU.subtract),
                          rd=[po_b[br]], wr=[dhl_b[e]])

                    def part_b():
                        kb.mm_group(pg[0][:, :], pg_b[0], [(C["onesrow64"], dhl[e][:, 0, :], [dhl_b[e]]),
                                                           (C["onesrow64"], dhl[e][:, 1, :], [dhl_b[e]])])
                        kb.mm_group(pg[1][:, :], pg_b[1], [(C["gsel"][:, h * 3 + br, :], sigT[:, qs], [sigT_b])])
                        kb.op(kb.act, lambda: nc.scalar.activation(out=rd[e][:, :], in_=pg[0][0:64, :], func=AF.Ln),
                              rd=[pg_b[0]], wr=[rd_b[e]])
                        kb.op(kb.act, lambda: nc.scalar.activation(out=rd[e][:, :], in_=rd[e][:, :], func=AF.Exp,
                                                                   scale=-1.0), rd=[rd_b[e]], wr=[rd_b[e]])
                        kb.op(kb.dve, lambda: nc.vector.tensor_tensor(wv[:, :], pg[1][0:64, :], rd[e][:, :], ALU.mult),
                              rd=[pg_b[1], rd_b[e]], wr=[e_b])
                        if not done_br:
                            kb.op(kb.dve, lambda: nc.vector.tensor_tensor(oa[:, :], po[br][0:64, :], wv[:, :], ALU.mult),
                                  rd=[po_b[br], e_b], wr=[oa_b])
                        else:
                            kb.op(kb.dve, lambda: nc.vector.tensor_tensor(tm[:, :], po[br][0:64, :], wv[:, :], ALU.mult),
                                  rd=[po_b[br], e_b], wr=[e_b])
                            kb.op(kb.dve, lambda: nc.vector.tensor_tensor(oa[:, :], oa[:, :], tm[:, :], ALU.add),
                                  rd=[e_b], wr=[oa_b])
                        done_br.append(br)
                        if len(done_br) == 3:
                            kb.op(kb.pool, lambda: nc.gpsimd.tensor_copy(ob[:, :], oa[:, :]), rd=[oa_b], wr=[ob_b])
                            kb.dma(kb.pool, ob_s, o_s[h, :, qs], ob[:, :], rd=[ob_b], wr=[o_sb])
                    defer.append([4, part_b])

                def emit_av(i, tiles=tiles, slots=slots):
                    br, l, m, rows, va, bufs, c0, c1 = tiles[i]
                    p = slots[i]
                    pe = kb.pe
                    deps = [PT_b[p].w, bufs[1].w]
                    if i == first[br]:
                        deps += po_b[br].wr_deps()
                        assert c0 == 0 and c1 == 512
                    pe.wait(deps)
                    ins = nc.tensor.matmul(po[br][0:65, c0:c1], va, PT[p][0:rows, c0:c1], start=(i == first[br]),
                                           stop=(i == last[br]), skip_group_check=True)
                    tok = pe.done(ins)
                    PT_b[p].note_read(tok)
                    bufs[1].note_read(tok)
                    if i == last[br]:
                        po_b[br].note_write(tok)
                        make_epilogue(br)

                emit_qk(0)
                if nt_ > 1:
                    emit_qk(1)
                for i in range(nt_):
                    if i + 2 < nt_:
                        emit_qk(i + 2)
                    emit_av(i)
                    run_deferred()
        run_deferred(force=True)
        run_deferred(force=True)
        kb.barrier()


def phase_nsa(kb, C, xin, xin_bufs, xout, xout_bufs, grow_d, posb_d, W, scr, heads=range(16), nqg=8, nqt=NT):
    nc = kb.nc
    q_s, kv_s, mb_s, o_s = scr["q_s"], scr["kv_s"], scr["mb_s"], scr["o_s"]
    q_sb, kv_sb, mb_sb, o_sb = Buf(), Buf(), Buf(), Buf()
    with ExitStack() as st0:
        sb0 = lambda n, shp, dt: st0.enter_context(nc.sbuf_tensor(kb.name(n), shp, dt))
        Vtm = sb0("Vtm", [128, NT, 4, 80], BF16)
        Vtm_b = [Buf() for _ in range(NT)]
        sigT = sb0("sigT", [128, S], BF16)
        sigT_b = Buf()
        kcmp = sb0("kcmp", [128, 2, 256], BF16)
        kcmp_b = Buf()
        vcmp = sb0("vcmp", [128, 2, 2, 80], BF16)
        vcmp_b = Buf()
        with ExitStack() as st1:
            hnT = st1.enter_context(nc.sbuf_tensor(kb.name("hnT"), [128, 8, S], BF16))
            hnT_b = [Buf() for _ in range(NT)]
            stage_norm_all(kb, C, xin, xin_bufs, grow_d, hnT, hnT_b)
            nsa_stage_proj(kb, C, hnT, hnT_b, posb_d, W["wfm"][0], W["wfm"][1], W["wgt"][0], W["wgt"][1],
                           W["wtm"][0], W["wtm"][1], q_s, q_sb, kv_s, kv_sb, Vtm, Vtm_b, sigT, sigT_b,
                           (W["w1"][0], W["w1"][1], W["w2"][0], W["w2"][1], W["pe"][0], W["pe"][1]),
                           kcmp, kcmp_b, vcmp, vcmp_b)
        if DBG.get("nsa", 9) >= 2:
            nsa_stage_select(kb, C, q_s, q_sb, kcmp, kcmp_b, mb_s, mb_sb, nqt=nqt)
        if DBG.get("nsa", 9) >= 3:
          nsa_stage_attn(kb, C, q_s, q_sb, kv_s, kv_sb, mb_s, mb_sb, C["efull_d"], kcmp, kcmp_b, vcmp, vcmp_b,
                       Vtm, Vtm_b, sigT, sigT_b, o_s, o_sb, heads=heads, nqg=nqg)
    with ExitStack() as st2:
        oT = st2.enter_context(nc.sbuf_tensor(kb.name("oT"), [128, 8, S], BF16))
        oT_b = [Buf() for _ in range(8)]
        sl = kb.slot(kb.name("oTl"))
        for c in range(8):
            kb.dma(kb.sp, sl, oT[:, c, :], o_s[2 * c:2 * c + 2].rearrange("h d t -> (h d) t"), rd=[o_sb], wr=oT_b)
        stage_outproj(kb, C, oT, oT_b, W["wo"][0], W["wo"][1], xin, xin_bufs, xout, xout_bufs)


def host_nsa_weights(w_in, pe_k, pe_v, k_w1, k_w2, v_w1, v_w2, w_out):
    w_in = np.asarray(w_in, np.float32)
    perm64 = (np.arange(64) + 32) % 64
    qperm = (np.arange(1024) // 64) * 64 + perm64[np.arange(1024) % 64]
    kperm = (np.arange(128) // 64) * 64 + perm64[np.arange(128) % 64]
    q = w_in[:, 0:1024]
    blk = lambda i: w_in[:, 1024 + 128 * i:1024 + 128 * (i + 1)]
    kc, vc, ks, vs, kw, vw = [blk(i) for i in range(6)]
    fm = np.concatenate([q, q[:, qperm], kc, vc, ks, ks[:, kperm], kw, kw[:, kperm]], axis=1)
    out = {}
    out["wfm_h"] = host_chunks(fm)
    out["wgt_h"] = host_rows(w_in[:, 1792:1840])
    out["wtm_h"] = host_rows(np.concatenate([vs, vw], axis=1))
    w1 = lambda w: np.ascontiguousarray(np.asarray(w, np.float32).reshape(32, 64, 128).transpose(1, 0, 2)).reshape(64, 4096)
    out["w1_h"] = np.stack([w1(k_w1), w1(v_w1)], axis=0)
    k_w2 = np.asarray(k_w2, np.float32)
    out["w2_h"] = np.ascontiguousarray(np.concatenate([k_w2, k_w2[:, perm64], np.asarray(v_w2, np.float32)], axis=1))
    out["pe_h"] = np.ascontiguousarray(np.concatenate([np.asarray(pe_k, np.float32).T, np.asarray(pe_v, np.float32).T],
                                                      axis=1))
    out["wo_h"] = host_rows(w_out)
    return out


W_SHAPES = {
    "sba_wqk": ([16 * 128, 1024], 128), "sba_wv": ([128, 8192], 128), "sba_wo": ([128, 8192], 128),
    "ffn0_wup": ([NPAIR * 128, 2048], 128), "ffn0_wdn": ([DFF, D], 128),
    "ffn1_wup": ([NPAIR * 128, 2048], 128), "ffn1_wdn": ([DFF, D], 128),
    "nsa_wfm": ([22 * 128, 1024], 128), "nsa_wgt": ([128, 384], 128), "nsa_wtm": ([128, 2048], 128),
    "nsa_w1": ([128, 4096], 64), "nsa_w2": ([128, 192], 128), "nsa_pe": ([64, 64], 64), "nsa_wo": ([128, 8192], 128),
}


def build_full():
    kb = KB()
    nc = kb.nc
    x_d = nc.dram_tensor("x", [S, D], F32, kind="ExternalInput").ap()
    posb_d = nc.dram_tensor("posb", [128, S], I32, kind="ExternalInput").ap()
    g_d = {n: nc.dram_tensor(n, [128, D], F32, kind="ExternalInput").ap()
           for n in ("g_mix0", "g_ffn0", "g_mix1", "g_ffn1", "g_fin")}
    cwb_d = [nc.dram_tensor(f"cwb{l}", [128, 4 * 44], F32, kind="ExternalInput").ap() for l in range(2)]
    y_d = nc.dram_tensor("y", [S, D], F32, kind="ExternalOutput").ap()
    H, Sx, Wb = {}, {}, {}
    for n, (shp, rc) in W_SHAPES.items():
        H[n] = nc.dram_tensor(n + "_h", shp, F32, kind="ExternalInput").ap()
        Sx[n] = nc.dram_tensor(n + "_s", shp, BF16).ap()
        Wb[n] = Buf()
    xa = nc.dram_tensor("xa", [S, D], F32).ap()
    xb = nc.dram_tensor("xb", [S, D], F32).ap()
    xc = nc.dram_tensor("xc", [S, D], F32).ap()
    scr = {"q_s": nc.dram_tensor("q_s", [16, 64, S], BF16).ap(), "kv_s": nc.dram_tensor("kv_s", [4, 128, S], BF16).ap(),
           "mb_s": nc.dram_tensor("mb_s", [2, 64, S], BF16).ap(), "o_s": nc.dram_tensor("o_s", [16, 64, S], BF16).ap()}
    jobs = []
    jobs_bg = []
    for n, (shp, rc) in W_SHAPES.items():
        for r0 in range(0, shp[0], rc):
            (jobs if n.startswith("sba_") else jobs_bg).append((H[n][r0:r0 + rc, :], Sx[n][r0:r0 + rc, :], Wb[n]))
    phase_convert(kb, jobs)
    chunk = lambda ap, p: ap.rearrange("(c p) n -> c p n", p=p)
    x_b = [Buf() for _ in range(NT)]
    xa_b = [Buf() for _ in range(NT)]
    xb_b = [Buf() for _ in range(NT)]
    xc_b = [Buf() for _ in range(NT)]
    y_b = [Buf() for _ in range(NT)]
    with ExitStack() as cst:
        C = load_consts(kb, SBA_CONSTS, cst)
        bg = BgConv(kb, cst, jobs_bg)
        phase_sba(kb, C, x_d, x_b, xa, xa_b, g_d["g_mix0"], chunk(Sx["sba_wqk"], 128), Wb["sba_wqk"],
                  Sx["sba_wv"], Wb["sba_wv"], Sx["sba_wo"], Wb["sba_wo"], bg=bg)
    with ExitStack() as cst:
        C = load_consts(kb, FFN_CONSTS, cst)
        phase_ffn(kb, C, 0, xa, xa_b, xb, xb_b, chunk(Sx["ffn0_wup"], 128), Wb["ffn0_wup"],
                  chunk(Sx["ffn0_wdn"], 128), Wb["ffn0_wdn"], cwb_d[0], g_d["g_ffn0"])
    W = {"wfm": (chunk(Sx["nsa_wfm"], 128), Wb["nsa_wfm"]), "wgt": (Sx["nsa_wgt"], Wb["nsa_wgt"]),
         "wtm": (Sx["nsa_wtm"], Wb["nsa_wtm"]), "w1": (chunk(Sx["nsa_w1"], 64), Wb["nsa_w1"]),
         "w2": (Sx["nsa_w2"], Wb["nsa_w2"]), "pe": (Sx["nsa_pe"], Wb["nsa_pe"]), "wo": (Sx["nsa_wo"], Wb["nsa_wo"])}
    with ExitStack() as cst:
        C = load_consts(kb, NSA_CONSTS, cst)
        phase_nsa(kb, C, xb, xb_b, xc, xc_b, g_d["g_mix1"], posb_d, W, scr)
    with ExitStack() as cst:
        C = load_consts(kb, FFN_CONSTS, cst)
        phase_ffn(kb, C, 1, xc, xc_b, y_d, y_b, chunk(Sx["ffn1_wup"], 128), Wb["ffn1_wup"],
                  chunk(Sx["ffn1_wdn"], 128), Wb["ffn1_wdn"], cwb_d[1], g_d["g_ffn1"], final_grow_d=g_d["g_fin"])
    return kb


def kernel(x, positions, norm_mix, sba_w_in, sba_w_out, nsa_w_in, nsa_cmp_pos_k, nsa_cmp_pos_v, nsa_cmp_k_w1,
           nsa_cmp_k_w2, nsa_cmp_v_w1, nsa_cmp_v_w2, nsa_w_out, norm_ffn, ffn_w_up, ffn_conv_w, ffn_conv_b,
           ffn_w_down, norm_final):
    x = np.asarray(x, np.float32)
    positions = np.asarray(positions)
    B = x.shape[0]
    shared = {}
    sw = np.asarray(sba_w_in[0], np.float32)
    shared["sba_wqk_h"] = host_chunks(sw[:, :2048])
    shared["sba_wv_h"] = host_rows(sw[:, 2048:])
    shared["sba_wo_h"] = host_rows(sba_w_out[0])
    for l in range(2):
        shared[f"ffn{l}_wup_h"] = host_wup(ffn_w_up[l])
        shared[f"ffn{l}_wdn_h"] = np.ascontiguousarray(np.asarray(ffn_w_down[l], np.float32))
        shared[f"cwb{l}"] = host_cwb(ffn_conv_w[l], ffn_conv_b[l])
    hw = host_nsa_weights(nsa_w_in[0], nsa_cmp_pos_k[0], nsa_cmp_pos_v[0], nsa_cmp_k_w1[0], nsa_cmp_k_w2[0],
                          nsa_cmp_v_w1[0], nsa_cmp_v_w2[0], nsa_w_out[0])
    for k_ in ("wfm", "wgt", "wtm", "w1", "w2", "pe", "wo"):
        shared["nsa_" + k_ + "_h"] = np.ascontiguousarray(hw[k_ + "_h"].reshape(W_SHAPES["nsa_" + k_][0]))
    shared["g_mix0"] = host_row(norm_mix[0])
    shared["g_mix1"] = host_row(norm_mix[1])
    shared["g_ffn0"] = host_row(norm_ffn[0])
    shared["g_ffn1"] = host_row(norm_ffn[1])
    shared["g_fin"] = host_row(norm_final)
    for k_, v_ in consts_host().items():
        shared["c_" + k_] = v_
    in_maps = []
    for b in range(B):
        m = dict(shared)
        m["x"] = np.ascontiguousarray(x[b])
        m["posb"] = np.ascontiguousarray(np.broadcast_to(positions[b].astype(np.int32)[None, :], (128, S)))
        in_maps.append(m)
    kb = build_full()
    res = run_bass_kernel_spmd(kb.nc, in_maps, core_ids=list(range(B)))
    return np.stack([np.asarray(r["y"], np.float32) for r in res.results], axis=0)
```

```python
from contextlib import ExitStack
import numpy as np
import ml_dtypes
import concourse.bass as bass
import concourse.mybir as mybir
from concourse.bass_utils import run_bass_kernel_spmd

F32 = mybir.dt.float32
BF16 = mybir.dt.bfloat16
I32 = mybir.dt.int32
AF = mybir.ActivationFunctionType
ALU = mybir.AluOpType
AX = mybir.AxisListType

DBG = {}
S = 4096
D = 1024
NT = S // 128
DFF = 2816
NPAIR = DFF // 128
EPS = 1e-6
NEGM = -30000.0


def _flat(ts):
    for t in ts:
        if t is None:
            continue
        if isinstance(t, tuple) and len(t) == 3 and isinstance(t[1], int):
            yield t
        else:
            yield from _flat(t)


class Buf:
    def __init__(self, name=""):
        self.name = name
        self.w = None
        self.r = {}

    def rd_deps(self):
        return [self.w]

    def wr_deps(self):
        return [self.w] + list(self.r.values())

    def note_read(self, tok):
        if tok is None:
            return
        k = tok[2]
        if k not in self.r or self.r[k][1] < tok[1]:
            self.r[k] = tok

    def note_write(self, tok):
        self.w = tok
        self.r = {}


class Eng:
    def __init__(self, kb, name, eng):
        self.kb = kb
        self.name = name
        self.eng = eng
        self.sem = kb.newsem("e_" + name)
        self.n = 0
        self.seen = {}

    def wait(self, *toks):
        for t in _flat(toks):
            sem, val, key = t
            if self.seen.get(key, 0) >= val:
                continue
            self.seen[key] = val
            self.eng.wait_ge(sem, val)

    def done(self, ins):
        self.n += 1
        ins.then_inc(self.sem, 1)
        return (self.sem, self.n, self.name)


class Slot:
    def __init__(self, kb, name):
        self.sem = kb.newsem("d_" + name)
        self.val = 0
        self.key = "d_" + name

    def done(self, ins):
        self.val += 16
        ins.then_inc(self.sem, 16)
        return (self.sem, self.val, self.key)


class KB:
    def __init__(self):
        self.nc = bass.Bass("TRN2", target_bir_lowering=False)
        self.root = ExitStack()
        self.nsem = 0
        nc = self.nc
        self.pe = Eng(self, "pe", nc.tensor)
        self.act = Eng(self, "act", nc.scalar)
        self.dve = Eng(self, "dve", nc.vector)
        self.pool = Eng(self, "pool", nc.gpsimd)
        self.sp = Eng(self, "sp", nc.sync)
        self.engs = [self.pe, self.act, self.dve, self.pool, self.sp]
        self.slots = []
        self.uid = 0

    def newsem(self, name):
        self.nsem += 1
        return self.root.enter_context(self.nc.semaphore(name))

    def slot(self, name):
        s = Slot(self, name)
        self.slots.append(s)
        return s

    def name(self, p):
        self.uid += 1
        return f"{p}_{self.uid}"

    def op(self, E, make, rd=(), wr=(), sig=True, extra=()):
        deps = list(extra)
        for b in rd:
            deps.append(b.w)
        for b in wr:
            deps.extend(b.wr_deps())
        E.wait(deps)
        ins = make()
        tok = E.done(ins) if sig else None
        if tok is not None:
            for b in rd:
                b.note_read(tok)
            for b in wr:
                b.note_write(tok)
        return tok

    def dma(self, Q, slot, out, in_, rd=(), wr=(), extra=()):
        deps = list(extra)
        for b in rd:
            deps.append(b.w)
        for b in wr:
            deps.extend(b.wr_deps())
        Q.wait(deps)
        ins = Q.eng.dma_start(out=out, in_=in_)
        tok = slot.done(ins)
        for b in rd:
            b.note_read(tok)
        for b in wr:
            b.note_write(tok)
        return tok

    def batch_end(self, slot, bufs):
        tok = (slot.sem, slot.val, slot.key)
        for b in bufs:
            b.w = tok

    def mm_group(self, out_ap, obuf, terms, sig=True):
        pe = self.pe
        deps = list(obuf.wr_deps())
        for (_, _, bufs) in terms:
            for b in bufs:
                deps.append(b.w)
        pe.wait(deps)
        n = len(terms)
        ins = None
        for i, (l, r, _) in enumerate(terms):
            ins = self.nc.tensor.matmul(out_ap, l, r, start=(i == 0), stop=(i == n - 1))
        tok = pe.done(ins)
        for (_, _, bufs) in terms:
            for b in bufs:
                b.note_read(tok)
        obuf.note_write(tok)
        return tok

    def barrier(self):
        toks = []
        for e in self.engs:
            if e.n > 0:
                toks.append((e.sem, e.n, e.name))
        for s in self.slots:
            if s.val > 0:
                toks.append((s.sem, s.val, s.key))
        for e in self.engs:
            e.wait(toks)


def phase_convert(kb, jobs):
    nc = kb.nc
    CH = 2048
    with ExitStack() as st:
        NB = 3
        tin = [st.enter_context(nc.sbuf_tensor(kb.name("cvi"), [128, CH], F32)) for _ in range(NB)]
        tout = [st.enter_context(nc.sbuf_tensor(kb.name("cvo"), [128, CH], BF16)) for _ in range(NB)]
        bin_ = [Buf() for _ in range(NB)]
        bout = [Buf() for _ in range(NB)]
        sin = [kb.slot(kb.name("cvin")) for _ in range(NB)]
        sout = [kb.slot(kb.name("cvout")) for _ in range(NB)]
        i = 0
        for (src, dst, dbuf) in jobs:
            R, Fd = src.shape[0], src.shape[1]
            for c0 in range(0, Fd, CH):
                w = min(CH, Fd - c0)
                k = i % NB
                kb.dma(kb.sp, sin[k], tin[k][0:R, 0:w], src[:, c0:c0 + w], wr=[bin_[k]])
                sel = i % 3
                if sel == 0:
                    kb.op(kb.dve, lambda: nc.vector.tensor_copy(tout[k][0:R, 0:w], tin[k][0:R, 0:w]),
                          rd=[bin_[k]], wr=[bout[k]])
                elif sel == 1:
                    kb.op(kb.pool, lambda: nc.gpsimd.tensor_copy(tout[k][0:R, 0:w], tin[k][0:R, 0:w]),
                          rd=[bin_[k]], wr=[bout[k]])
                else:
                    kb.op(kb.act, lambda: nc.scalar.copy(tout[k][0:R, 0:w], tin[k][0:R, 0:w]),
                          rd=[bin_[k]], wr=[bout[k]])
                kb.dma(kb.pool, sout[k], dst[:, c0:c0 + w], tout[k][0:R, 0:w], rd=[bout[k]], wr=[dbuf])
                i += 1
        kb.barrier()


class BgConv:
    def __init__(self, kb, stack, jobs, CH=512, NB=2):
        nc = kb.nc
        self.kb = kb
        self.NB = NB
        self.tin = [stack.enter_context(nc.sbuf_tensor(kb.name("bgi"), [128, CH], F32)) for _ in range(NB)]
        self.tout = [stack.enter_context(nc.sbuf_tensor(kb.name("bgo"), [128, CH], BF16)) for _ in range(NB)]
        self.bin = [Buf() for _ in range(NB)]
        self.bout = [Buf() for _ in range(NB)]
        self.sin = [kb.slot(kb.name("bgin")) for _ in range(NB)]
        self.sout = [kb.slot(kb.name("bgout")) for _ in range(NB)]
        self.tiles = []
        for (src, dst, dbuf) in jobs:
            R, Fd = src.shape[0], src.shape[1]
            for c0 in range(0, Fd, CH):
                w = min(CH, Fd - c0)
                self.tiles.append((src[:, c0:c0 + w], dst[:, c0:c0 + w], R, w, dbuf))
        self.pos = 0

    def emit(self, n):
        kb = self.kb
        nc = kb.nc
        for _ in range(n):
            if self.pos >= len(self.tiles):
                return
            src, dst, R, w, dbuf = self.tiles[self.pos]
            k = self.pos % self.NB
            self.pos += 1
            kb.dma(kb.sp, self.sin[k], self.tin[k][0:R, 0:w], src, wr=[self.bin[k]])
            kb.op(kb.pool, lambda: nc.gpsimd.tensor_copy(self.tout[k][0:R, 0:w], self.tin[k][0:R, 0:w]),
                  rd=[self.bin[k]], wr=[self.bout[k]])
            kb.dma(kb.pool, self.sout[k], dst, self.tout[k][0:R, 0:w], rd=[self.bout[k]], wr=[dbuf])

    def flush(self):
        self.emit(len(self.tiles))


def norm_block(kb, xts, xbufs, grow, gbuf, hns, hnbufs, sq, sqbuf, st, stbuf):
    nc = kb.nc
    n = len(xts)
    for i in range(n):
        kb.op(kb.act, lambda: nc.scalar.activation(out=sq, in_=xts[i], func=AF.Square, accum_out=st[:, i:i + 1]),
              rd=[xbufs[i]], wr=[sqbuf, stbuf])
    kb.op(kb.dve, lambda: nc.vector.tensor_scalar(st[:, 4:4 + n], st[:, 0:n], 1.0 / D, EPS, ALU.mult, ALU.add),
          rd=[stbuf], wr=[stbuf])
    kb.op(kb.act, lambda: nc.scalar.activation(out=st[:, 8:8 + n], in_=st[:, 4:4 + n], func=AF.Sqrt),
          rd=[stbuf], wr=[stbuf])
    kb.op(kb.dve, lambda: nc.vector.reciprocal(st[:, 4:4 + n], st[:, 8:8 + n]), rd=[stbuf], wr=[stbuf])
    for i in range(n):
        kb.op(kb.dve, lambda: nc.vector.scalar_tensor_tensor(out=hns[i], in0=xts[i], scalar=st[:, 4 + i:5 + i],
                                                             in1=grow, op0=ALU.mult, op1=ALU.mult),
              rd=[xbufs[i], stbuf, gbuf], wr=[hnbufs[i]])


def transpose_tile(kb, C, hn, hnbuf, pst, pstbuf, dst_ap, dstbuf, evac_eng):
    nc = kb.nc
    pe = kb.pe
    pe.wait(pstbuf.wr_deps(), hnbuf.w)
    ins = None
    for kc in range(8):
        ins = nc.tensor.transpose(pst[:, kc, :], hn[:, kc * 128:(kc + 1) * 128], C["ident"])
    tok = pe.done(ins)
    hnbuf.note_read(tok)
    pstbuf.note_write(tok)
    if evac_eng is kb.act:
        kb.op(kb.act, lambda: nc.scalar.copy(dst_ap, pst[:, :, :]), rd=[pstbuf], wr=[dstbuf])
    else:
        kb.op(kb.dve, lambda: nc.vector.tensor_copy(dst_ap, pst[:, :, :]), rd=[pstbuf], wr=[dstbuf])


def phase_ffn(kb, C, layer, xin, xin_bufs, xout, xout_bufs, wup_d, wup_buf, wdn_d, wdn_buf,
              cwb_d, grow_d, final_grow_d=None):
    nc = kb.nc
    with ExitStack() as st:
        sb = lambda n, shp, dt: st.enter_context(nc.sbuf_tensor(kb.name(n), shp, dt))
        ps = lambda n, shp, dt: st.enter_context(nc.psum_tensor(kb.name(n), shp, dt))
        wdn = sb("wdn", [128, NPAIR, 1024], BF16)
        wdn_b = Buf()
        cw = sb("cw", [128, 4, 44], F32)
        cw_b = Buf()
        grow = sb("grow", [128, 1024], F32)
        grow_b = Buf()
        NW = 4
        wup = [sb("wup", [128, 8, 256], BF16) for _ in range(NW)]
        wup_b = [Buf() for _ in range(NW)]
        wup_s = [kb.slot(kb.name("wup")) for _ in range(NW)]
        hnT = [sb("hnT", [128, 8, 512], BF16) for _ in range(2)]
        hnT_b = [[Buf() for _ in range(4)] for _ in range(2)]
        G2 = [sb("G", [128, NPAIR, 512], BF16) for _ in range(2)]
        G2_b = [[Buf() for _ in range(NPAIR)] for _ in range(2)]
        T0 = [sb("T0", [128, 512], F32) for _ in range(4)]
        T0_b = [Buf() for _ in range(4)]
        xt = [sb("xt", [128, 1024], F32) for _ in range(4)]
        xt_b = [Buf() for _ in range(4)]
        xt_s = [kb.slot(kb.name("xt")) for _ in range(4)]
        hn = [sb("hn", [128, 1024], BF16) for _ in range(4)]
        hn_b = [Buf() for _ in range(4)]
        sq = sb("sq", [128, 1024], BF16)
        sq_b = Buf()
        st2 = sb("st2", [128, 12], F32)
        st2_b = Buf()
        U = [sb("U", [128, 514], F32) for _ in range(4)]
        U_b = [Buf() for _ in range(4)]
        Cc = [sb("Cc", [128, 512], F32) for _ in range(4)]
        Cc_b = [Buf() for _ in range(4)]
        Sg = [sb("Sg", [128, 512], F32) for _ in range(2)]
        Sg_b = [Buf() for _ in range(2)]
        halo = sb("halo", [128, 44, 2], F32)
        halo_b = [Buf() for _ in range(44)]
        xr = [sb("xr", [128, 1024], F32) for _ in range(2)]
        xr_b = [Buf() for _ in range(2)]
        xr_s = [kb.slot(kb.name("xr")) for _ in range(2)]
        NXO = 2 if final_grow_d is None else 4
        xo = [sb("xo", [128, 1024], F32) for _ in range(NXO)]
        xo_b = [Buf() for _ in range(NXO)]
        xo_s = [kb.slot(kb.name("xo")) for _ in range(4)]
        if final_grow_d is not None:
            fgrow = sb("fgrow", [128, 1024], F32)
            fgrow_b = Buf()
            fst = sb("fst", [128, 12], F32)
            fst_b = Buf()
        pst = ps("pst", [128, 8, 128], BF16)
        pst_b = Buf()
        pu = [ps("pu", [128, 512], F32) for _ in range(4)]
        pu_b = [Buf() for _ in range(4)]
        po = [ps("po", [128, 512], F32) for _ in range(2)]
        po_b = [Buf() for _ in range(2)]

        s_misc = kb.slot(kb.name("misc"))
        for c0_, c1_ in ((0, 6), (6, 11), (11, 17), (17, 22)):
            kb.dma(kb.pool, s_misc, wdn[:, c0_:c1_, :], wdn_d[c0_:c1_].rearrange("c p n -> p c n"), rd=[wdn_buf],
                   wr=[wdn_b])
        kb.dma(kb.pool, s_misc, cw[:, :, :], cwb_d.rearrange("p (j c) -> p j c", j=4), wr=[cw_b])
        kb.dma(kb.pool, s_misc, grow[:, :], grow_d, wr=[grow_b])
        if final_grow_d is not None:
            kb.dma(kb.pool, s_misc, fgrow[:, :], final_grow_d, wr=[fgrow_b])
            kb.batch_end(s_misc, [fgrow_b])
        kb.batch_end(s_misc, [wdn_b, cw_b, grow_b])
        kb.op(kb.dve, lambda: nc.vector.memset(halo[:, :, :], 0.0), wr=halo_b)

        wi = 0
        ui = 0
        oi = 0
        NBLK = S // 512

        def norm_stage(tb):
            hb = tb % 2
            for t4 in range(4):
                tt = tb * 4 + t4
                kb.dma(kb.sp, xt_s[t4], xt[t4][:, :], xin[tt * 128:(tt + 1) * 128, :], rd=[xin_bufs[tt]],
                       wr=[xt_b[t4]])
            norm_block(kb, [xt[i][:, :] for i in range(4)], xt_b, grow[:, :], grow_b,
                       [hn[i][:, :] for i in range(4)], hn_b, sq[:, :], sq_b, st2[:, :], st2_b)
            for t4 in range(4):
                transpose_tile(kb, C, hn[t4], hn_b[t4], pst, pst_b, hnT[hb][:, :, t4 * 128:(t4 + 1) * 128],
                               hnT_b[hb][t4], kb.act)

        def down_group(tb, gi):
            nonlocal oi
            t4, nh = divmod(gi, 2)
            tt = tb * 4 + t4
            Gd = G2[tb % 2]
            Gd_b = G2_b[tb % 2]
            k = t4 % 2
            ko = t4 % NXO
            if nh == 0:
                kb.dma(kb.sp, xr_s[k], xr[k][:, :], xin[tt * 128:(tt + 1) * 128, :], rd=[xin_bufs[tt]], wr=[xr_b[k]])
            terms = [(Gd[:, fc, t4 * 128:(t4 + 1) * 128], wdn[:, fc, nh * 512:(nh + 1) * 512], [Gd_b[fc], wdn_b])
                     for fc in range(NPAIR)]
            kb.mm_group(po[nh][:, :], po_b[nh], terms)
            kb.op(kb.dve, lambda: nc.vector.tensor_tensor(xo[ko][:, nh * 512:(nh + 1) * 512], po[nh][:, :],
                                                          xr[k][:, nh * 512:(nh + 1) * 512], ALU.add),
                  rd=[po_b[nh], xr_b[k]], wr=[xo_b[ko]])
            if nh == 1 and final_grow_d is None:
                kb.dma(kb.pool, xo_s[ko], xout[tt * 128:(tt + 1) * 128, :], xo[ko][:, :], rd=[xo_b[ko]],
                       wr=[xout_bufs[tt]])
            if gi == 7 and final_grow_d is not None:
                norm_block(kb, [xo[i][:, :] for i in range(4)], xo_b, fgrow[:, :], fgrow_b,
                           [xo[i][:, :] for i in range(4)], xo_b, sq[:, :], sq_b, fst[:, :], fst_b)
                for t4_ in range(4):
                    tt_ = tb * 4 + t4_
                    kb.dma(kb.pool, xo_s[t4_], xout[tt_ * 128:(tt_ + 1) * 128, :], xo[t4_][:, :], rd=[xo_b[t4_]],
                           wr=[xout_bufs[tt_]])

        norm_stage(0)
        for tb in range(NBLK):
            hb = tb % 2
            Gw = G2[tb % 2]
            Gw_b = G2_b[tb % 2]
            for j in range(NPAIR):
                k = wi % NW
                wi += 1
                kb.dma(kb.sp, wup_s[k], wup[k][:, :, :], wup_d[j].rearrange("p (kc n) -> p kc n", kc=8),
                       rd=[wup_buf], wr=[wup_b[k]])
                cs = []
                for gv in range(2):
                    u = ui % 4
                    ui += 1
                    ch = gv * NPAIR + j
                    terms = [(wup[k][:, kc, gv * 128:(gv + 1) * 128], hnT[hb][:, kc, :], [wup_b[k]] + hnT_b[hb])
                             for kc in range(8)]
                    kb.mm_group(pu[u][:, :], pu_b[u], terms)
                    kb.op(kb.act, lambda: nc.scalar.activation(out=Cc[u][:, :], in_=pu[u][:, :], func=AF.Identity,
                                                               bias=cw[:, 3, ch:ch + 1], scale=cw[:, 2, ch:ch + 1]),
                          rd=[pu_b[u], cw_b], wr=[Cc_b[u]])
                    kb.op(kb.act, lambda: nc.scalar.copy(U[u][:, 2:514], pu[u][:, :]), rd=[pu_b[u]], wr=[U_b[u]])
                    kb.op(kb.pool, lambda: nc.gpsimd.tensor_copy(U[u][:, 0:2], halo[:, ch, :]),
                          rd=[halo_b[ch]], wr=[U_b[u]])
                    kb.op(kb.pool, lambda: nc.gpsimd.tensor_copy(halo[:, ch, :], U[u][:, 512:514]),
                          rd=[U_b[u]], wr=[halo_b[ch]])
                    kb.op(kb.pool, lambda: nc.gpsimd.tensor_scalar(T0[u][:, :], U[u][:, 0:512], cw[:, 0, ch:ch + 1], 0.0,
                                                                   ALU.mult, ALU.add),
                          rd=[U_b[u], cw_b], wr=[T0_b[u]])
                    kb.op(kb.dve, lambda: nc.vector.scalar_tensor_tensor(out=Cc[u][:, :], in0=U[u][:, 1:513],
                                                                         scalar=cw[:, 1, ch:ch + 1], in1=Cc[u][:, :],
                                                                         op0=ALU.mult, op1=ALU.add),
                          rd=[U_b[u], cw_b], wr=[Cc_b[u]])
                    kb.op(kb.dve, lambda: nc.vector.tensor_tensor(Cc[u][:, :], Cc[u][:, :], T0[u][:, :], ALU.add),
                          rd=[T0_b[u]], wr=[Cc_b[u]])
                    cs.append(u)
                ug, uv = cs
                sgi = j % 2
                kb.op(kb.act, lambda: nc.scalar.activation(out=Sg[sgi][:, :], in_=Cc[ug][:, :], func=AF.Silu),
                      rd=[Cc_b[ug]], wr=[Sg_b[sgi]])
                kb.op(kb.dve, lambda: nc.vector.tensor_tensor(Gw[:, j, :], Sg[sgi][:, :], Cc[uv][:, :], ALU.mult),
                      rd=[Sg_b[sgi], Cc_b[uv]], wr=[Gw_b[j]])
                if tb > 0 and j % 3 == 0 and j // 3 < 8:
                    down_group(tb - 1, j // 3)
            if tb + 1 < NBLK:
                norm_stage(tb + 1)
        for gi in range(8):
            down_group(NBLK - 1, gi)
        kb.barrier()


def host_wup(w):
    a = np.asarray(w, np.float32).reshape(8, 128, 2, NPAIR, 128)
    a = a.transpose(3, 1, 0, 2, 4)
    return np.ascontiguousarray(a).reshape(NPAIR * 128, 2048)


def host_cwb(cw, cb):
    a = np.concatenate([np.asarray(cw, np.float32), np.asarray(cb, np.float32)[None, :]], axis=0)
    a = a.reshape(4, 44, 128).transpose(2, 0, 1)
    return np.ascontiguousarray(a).reshape(128, 4 * 44)


def host_row(g):
    return np.ascontiguousarray(np.broadcast_to(np.asarray(g, np.float32)[None, :], (128, D)))


def host_chunks(w):
    w = np.asarray(w, np.float32)
    n = w.shape[1] // 128
    a = w.reshape(8, 128, n, 128).transpose(2, 1, 0, 3)
    return np.ascontiguousarray(a).reshape(n * 128, 1024)


def host_rows(w):
    w = np.asarray(w, np.float32)
    kc = w.shape[0] // 128
    a = w.reshape(kc, 128, w.shape[1]).transpose(1, 0, 2)
    return np.ascontiguousarray(a).reshape(128, kc * w.shape[1])


def consts_host():
    bf = ml_dtypes.bfloat16
    c = {}
    c["ident"] = np.eye(128, dtype=np.float32).astype(bf)
    c["ident32"] = np.eye(128, dtype=np.float32)
    p = np.arange(128)[:, None]
    f = np.arange(512)[None, :]
    nm = np.zeros((128, 4, 512), np.float32)
    le = np.zeros((128, 4, 512), np.float32)
    wn = np.zeros((128, 4, 512), np.float32)
    for j in range(4):
        nm[:, j, :] = np.where(f <= 128 * j + p, NEGM, 0.0)
        le[:, j, :] = np.where(128 * j + p > f, NEGM, 0.0)
        wn[:, j, :] = np.where(f >= 128 * j + p, NEGM, 0.0)
    c["negmask"] = nm.astype(bf)
    c["negmask_le"] = le.astype(bf)
    c["negmask_win"] = wn.astype(bf)
    cm = np.zeros((128, 5, 512), np.float32)
    for u in range(5):
        cm[:, u, :] = np.where(16 * p + 31 > 512 * u + f, NEGM, 0.0)
    c["negmask_cmp"] = cm.astype(bf)
    x = np.arange(512)[None, :]
    c["cmask_tm"] = np.where(16 * (x - 248) + 31 > p, NEGM, 0.0).astype(np.float32).astype(bf)
    jj = np.arange(128)[:, None]
    ss = np.arange(128)[None, :]
    c["uincneg"] = np.where(jj >= ss, -1.0, 0.0).astype(np.float32).astype(bf)
    os_ = np.zeros((128, 2, 128), np.float32)
    sl = np.zeros((128, 2, 128), np.float32)
    for hh in range(2):
        os_[:, hh, hh] = 1.0
        os_[:, hh, 32 + hh] = 1.0
        sl[hh, hh, :] = 1.0
        sl[32 + hh, hh, :] = 1.0
    c["onesel"] = os_.astype(bf)
    c["sel"] = sl.astype(bf)
    c["zeros512"] = np.zeros((128, 512), np.float32).astype(bf)
    cc = np.arange(256)[:, None] * 16
    s0 = np.arange(64)[None, :] * 64
    ov = np.clip(np.minimum(cc + 32, s0 + 64) - np.maximum(cc, s0), 0, None) / 32.0
    ov[255, :] = 0.0
    c["overlap"] = np.ascontiguousarray(ov.reshape(2, 128, 64).transpose(1, 0, 2)).astype(np.float32)
    bon = np.zeros((128, 192), np.float32)
    y = np.arange(128)[None, :]
    npr = y - 62
    cur = p // 64
    bon[:, 0:128] = np.where(npr > cur, -1.0e9, np.where((npr == cur) | (npr == cur - 1), 1.0e4, 0.0))
    bon[:, 128] = 1.0e4
    c["bonus"] = bon
    o64 = np.zeros((128, 128), np.float32)
    o64[64, :] = 1.0
    c["onesrow64"] = o64.astype(bf)
    gs = np.zeros((128, 48, 128), np.float32)
    for r in range(48):
        gs[r, r, :] = 1.0
    c["gsel"] = gs.astype(bf)
    half = 32
    inv = (10000.0 ** (-np.arange(half, dtype=np.float32) / half)).astype(np.float32)
    rc = np.zeros((128, 2), np.float32)
    rc[:, 0] = inv[np.arange(128) % 32]
    rc[:, 1] = np.where((np.arange(128) % 64) < 32, -1.0, 1.0)
    c["ropec"] = rc
    ef = (np.arange(S)[None, :] // 64 == np.arange(64)[:, None]).astype(np.float32)
    c["efull"] = ef.astype(bf)
    return c


CONST_SHAPES = {"ident": ([128, 128], BF16), "ident32": ([128, 128], F32), "negmask": ([128, 4, 512], BF16),
                "negmask_le": ([128, 4, 512], BF16), "negmask_win": ([128, 4, 512], BF16),
                "negmask_cmp": ([128, 5, 512], BF16), "cmask_tm": ([128, 512], BF16),
                "uincneg": ([128, 128], BF16), "onesel": ([128, 2, 128], BF16), "sel": ([128, 2, 128], BF16),
                "zeros512": ([128, 512], BF16),
                "overlap": ([128, 2, 64], F32), "bonus": ([128, 192], F32), "onesrow64": ([128, 128], BF16),
                "gsel": ([128, 48, 128], BF16), "ropec": ([128, 2], F32)}


SBA_CONSTS = ("ident", "negmask", "uincneg", "onesel", "sel", "zeros512")
NSA_CONSTS = ("ident", "ident32", "negmask_le", "negmask_win", "negmask_cmp", "cmask_tm", "overlap", "bonus", "onesrow64",
              "gsel", "ropec")
FFN_CONSTS = ("ident",)


def load_consts(kb, names=None, stack=None):
    nc = kb.nc
    if not hasattr(kb, "cdram"):
        kb.cdram = {}
        for n, (shp, dt) in CONST_SHAPES.items():
            kb.cdram[n] = nc.dram_tensor("c_" + n, shp, dt, kind="ExternalInput").ap()
        kb.cdram["efull_d"] = nc.dram_tensor("c_efull", [64, S], BF16, kind="ExternalInput").ap()
    stack = stack if stack is not None else kb.root
    C = {}
    sl = kb.slot(kb.name("consts"))
    for n, (shp, dt) in CONST_SHAPES.items():
        if names is not None and n not in names:
            continue
        d = kb.cdram[n]
        t = stack.enter_context(nc.sbuf_tensor(kb.name("cs_" + n), shp, dt))
        if len(shp) == 2:
            kb.dma(kb.sp, sl, t[:, :], d)
            C[n] = t[:, :]
        else:
            kb.dma(kb.sp, sl, t[:, :, :], d)
            C[n] = t[:, :, :]
    C["efull_d"] = kb.cdram["efull_d"]
    kb.barrier()
    return C


def stage_norm_all(kb, C, xin, xin_bufs, grow_d, hnT, hnT_b):
    nc = kb.nc
    with ExitStack() as st:
        sb = lambda n, shp, dt: st.enter_context(nc.sbuf_tensor(kb.name(n), shp, dt))
        ps = lambda n, shp, dt: st.enter_context(nc.psum_tensor(kb.name(n), shp, dt))
        grow = sb("grow", [128, 1024], F32)
        grow_b = Buf()
        xt = [sb("xt", [128, 1024], F32) for _ in range(8)]
        xt_b = [Buf() for _ in range(8)]
        xt_s = [kb.slot(kb.name("xt")) for _ in range(8)]
        hn = [sb("hn", [128, 1024], BF16) for _ in range(4)]
        hn_b = [Buf() for _ in range(4)]
        sq = sb("sq", [128, 1024], BF16)
        sq_b = Buf()
        st2 = [sb("st2", [128, 12], F32) for _ in range(2)]
        st2_b = [Buf() for _ in range(2)]
        pst = [ps("pst", [128, 8, 128], BF16) for _ in range(2)]
        pst_b = [Buf() for _ in range(2)]
        s_misc = kb.slot(kb.name("misc"))
        kb.dma(kb.pool, s_misc, grow[:, :], grow_d, wr=[grow_b])
        for tb in range(NT // 4):
            o = (tb % 2) * 4
            for t4 in range(4):
                tt = tb * 4 + t4
                kb.dma(kb.sp, xt_s[o + t4], xt[o + t4][:, :], xin[tt * 128:(tt + 1) * 128, :], rd=[xin_bufs[tt]],
                       wr=[xt_b[o + t4]])
            norm_block(kb, [xt[o + i][:, :] for i in range(4)], xt_b[o:o + 4], grow[:, :], grow_b,
                       [hn[i][:, :] for i in range(4)], hn_b[0:4], sq[:, :], sq_b, st2[tb % 2][:, :],
                       st2_b[tb % 2])
            for t4 in range(4):
                tt = tb * 4 + t4
                transpose_tile(kb, C, hn[t4], hn_b[t4], pst[tt % 2], pst_b[tt % 2],
                               hnT[:, :, tt * 128:(tt + 1) * 128], hnT_b[tt],
                               kb.act if tt % 2 == 0 else kb.dve)
        kb.barrier()


def stage_outproj(kb, C, oT, oT_b, wo_d, wo_buf, xin, xin_bufs, xout, xout_bufs):
    nc = kb.nc
    with ExitStack() as st:
        sb = lambda n, shp, dt: st.enter_context(nc.sbuf_tensor(kb.name(n), shp, dt))
        ps = lambda n, shp, dt: st.enter_context(nc.psum_tensor(kb.name(n), shp, dt))
        wo = sb("wo", [128, 8, 1024], BF16)
        wo_b = Buf()
        xr = [sb("xr", [128, 1024], F32) for _ in range(3)]
        xr_b = [Buf() for _ in range(3)]
        xr_s = [kb.slot(kb.name("xr")) for _ in range(3)]
        xo = [sb("xo", [128, 1024], F32) for _ in range(3)]
        xo_b = [Buf() for _ in range(3)]
        xo_s = [kb.slot(kb.name("xo")) for _ in range(3)]
        po = [ps("po", [128, 512], F32) for _ in range(4)]
        po_b = [Buf() for _ in range(4)]
        s_misc = kb.slot(kb.name("misc"))
        wo_v = wo_d.rearrange("p (c n) -> p c n", c=8)
        kb.dma(kb.pool, s_misc, wo[:, 0:4, :], wo_v[:, 0:4, :], rd=[wo_buf], wr=[wo_b])
        kb.dma(kb.pool, s_misc, wo[:, 4:8, :], wo_v[:, 4:8, :], rd=[wo_buf], wr=[wo_b])
        kb.batch_end(s_misc, [wo_b])
        for tt in range(NT):
            k = tt % 3
            kb.dma(kb.sp, xr_s[k], xr[k][:, :], xin[tt * 128:(tt + 1) * 128, :], rd=[xin_bufs[tt]], wr=[xr_b[k]])
            for nh in range(2):
                pi = (tt * 2 + nh) % 4
                terms = [(oT[:, c, tt * 128:(tt + 1) * 128], wo[:, c, nh * 512:(nh + 1) * 512], [oT_b[tt // 4], wo_b])
                         for c in range(8)]
                kb.mm_group(po[pi][:, :], po_b[pi], terms)
                kb.op(kb.dve, lambda: nc.vector.tensor_tensor(xo[k][:, nh * 512:(nh + 1) * 512], po[pi][:, :],
                                                              xr[k][:, nh * 512:(nh + 1) * 512], ALU.add),
                      rd=[po_b[pi], xr_b[k]], wr=[xo_b[k]])
            kb.dma(kb.pool, xo_s[k], xout[tt * 128:(tt + 1) * 128, :], xo[k][:, :], rd=[xo_b[k]], wr=[xout_bufs[tt]])
        kb.barrier()


def phase_sba(kb, C, xin, xin_bufs, xout, xout_bufs, grow_d, wqk_d, wqk_buf, wv_d, wv_buf, wo_d, wo_buf,
              ngroups=8, nq=8, bg=None):
    nc = kb.nc
    with ExitStack() as st0:
        sb0 = lambda n, shp, dt: st0.enter_context(nc.sbuf_tensor(kb.name(n), shp, dt))
        oT = sb0("oT", [128, 8, S], BF16)
        oT_b = [Buf() for _ in range(8)]
        if ngroups < 8 or nq < 8:
            kb.op(kb.pool, lambda: nc.gpsimd.memset(oT[:, :, :], 0.0), wr=oT_b)
        with ExitStack() as st1:
            sb1 = lambda n, shp, dt: st1.enter_context(nc.sbuf_tensor(kb.name(n), shp, dt))
            hnT = sb1("hnT", [128, 8, S], BF16)
            hnT_b = [Buf() for _ in range(NT)]
            stage_norm_all(kb, C, xin, xin_bufs, grow_d, hnT, hnT_b)
            with ExitStack() as st:
                sb = lambda n, shp, dt: st.enter_context(nc.sbuf_tensor(kb.name(n), shp, dt))
                ps = lambda n, shp, dt: st.enter_context(nc.psum_tensor(kb.name(n), shp, dt))
                wg = [sb("wg", [128, 3, 8, 128], BF16) for _ in range(1)]
                wg_b = [Buf() for _ in range(1)]
                wg_s = [kb.slot(kb.name("wg")) for _ in range(1)]
                qz = sb("qz", [128, 2, S], BF16)
                kT = sb("kT", [128, S], BF16)
                Vt = sb("Vt", [128, NT, 2, 128], BF16)
                qz_b = [Buf() for _ in range(8)]
                kT_b = [Buf() for _ in range(8)]
                Vt_b = [Buf() for _ in range(NT)]
                NB3 = 3
                E = [sb("E", [128, 2, 512], F32) for _ in range(2)]
                E_b = [Buf() for _ in range(2)]
                SP = [sb("SP", [128, 2, 512], BF16) for _ in range(2)]
                SP_b = [Buf() for _ in range(2)]
                A = [sb("A", [128, 2, 512], BF16) for _ in range(2)]
                A_b = [Buf() for _ in range(2)]
                R34 = sb("R34", [34, 512], F32)
                R34_b = Buf()
                RHL = [sb("RHL", [128, 512], BF16) for _ in range(2)]
                RHL_b = [Buf() for _ in range(2)]
                pz = [ps("pz", [128, 2, 512], F32) for _ in range(NB3)]
                pz_b = [Buf() for _ in range(NB3)]
                pr = ps("pr", [128, 512], F32)
                pr_b = Buf()
                po = ps("po", [128, 512], F32)
                po_b = Buf()
                pq = [pz[0][:, 0, :], pz[0][:, 1, :], pz[1][:, 0, :], pz[1][:, 1, :]]
                pq_b = [pz_b[0], pz_b[0], pz_b[1], pz_b[1]]

                def load_w(c):
                    k = 0
                    kb.dma(kb.sp, wg_s[k], wg[k][:, 0, :, :], wqk_d[c].rearrange("p (kc n) -> p kc n", kc=8),
                           rd=[wqk_buf], wr=[wg_b[k]])
                    kb.dma(kb.sp, wg_s[k], wg[k][:, 1, :, :], wqk_d[8 + c].rearrange("p (kc n) -> p kc n", kc=8),
                           rd=[wqk_buf], wr=[wg_b[k]])
                    wv_v = wv_d.rearrange("p (kc n) -> p kc n", kc=8)
                    for k0_ in range(0, 8, 2):
                        kb.dma(kb.sp, wg_s[k], wg[k][:, 2, k0_:k0_ + 2, :], wv_v[:, k0_:k0_ + 2, c * 128:(c + 1) * 128],
                               rd=[wv_buf], wr=[wg_b[k]])

                kb.op(kb.pool, lambda: nc.gpsimd.memset(Vt[:, :, :, :], 0.0), wr=Vt_b)
                kb.op(kb.pool, lambda: nc.gpsimd.memset(qz[:, :, :], 0.0), wr=qz_b)
                for i_ in range(2):
                    kb.op(kb.pool, lambda: nc.gpsimd.memset(RHL[i_][:, :], 0.0), wr=[RHL_b[i_]])
                load_w(0)
                qi = 0
                for c in range(ngroups):
                    k = 0
                    for which in (0, 1):
                        for tg in range(8):
                            p = qi % 4
                            qi += 1
                            ts_ = slice(tg * 512, (tg + 1) * 512)
                            terms = [(wg[k][:, which, kc, :], hnT[:, kc, ts_],
                                      [wg_b[k]] + hnT_b[tg * 4:(tg + 1) * 4]) for kc in range(8)]
                            kb.mm_group(pq[p], pq_b[p], terms)
                            if which == 0:
                                kb.op(kb.act, lambda: nc.scalar.mul(qz[0:64, 0, ts_], pq[p][0:64, :], 0.125),
                                      rd=[pq_b[p]], wr=[qz_b[tg]])
                                kb.op(kb.act, lambda: nc.scalar.mul(qz[64:128, 1, ts_], pq[p][64:128, :], 0.125),
                                      rd=[pq_b[p]], wr=[qz_b[tg]])
                            else:
                                kb.op(kb.dve, lambda: nc.vector.tensor_copy(kT[:, ts_], pq[p]),
                                      rd=[pq_b[p]], wr=[kT_b[tg]])
                    for tt4 in range(NT // 4):
                        p = qi % 4
                        qi += 1
                        pe = kb.pe
                        pe.wait(pq_b[p].wr_deps(), wg_b[k].w, [hnT_b[tt4 * 4 + i].w for i in range(4)])
                        ins = None
                        for i in range(4):
                            tt = tt4 * 4 + i
                            for kc in range(8):
                                ins = nc.tensor.matmul(pq[p][:, i * 128:(i + 1) * 128], hnT[:, kc, tt * 128:(tt + 1) * 128],
                                                       wg[k][:, 2, kc, :], start=(kc == 0), stop=(kc == 7))
                        tok = pe.done(ins)
                        wg_b[k].note_read(tok)
                        pq_b[p].note_write(tok)
                        src = pq[p].rearrange("p (i n) -> p i n", i=4)
                        kb.op(kb.act, lambda: nc.scalar.copy(Vt[:, tt4 * 4:(tt4 + 1) * 4, 0, 0:64], src[:, :, 0:64]),
                              rd=[pq_b[p]], wr=Vt_b[tt4 * 4:(tt4 + 1) * 4])
                        kb.op(kb.act, lambda: nc.scalar.copy(Vt[:, tt4 * 4:(tt4 + 1) * 4, 1, 64:128], src[:, :, 64:128]),
                              rd=[pq_b[p]], wr=Vt_b[tt4 * 4:(tt4 + 1) * 4])
                    if c + 1 < ngroups:
                        load_w(c + 1)
                    def make_group(g):
                        qs = slice(g * 512, (g + 1) * 512)
                        nsteps = 4 * g + 4
                        kts = [4 * g + 3 - i for i in range(nsteps)]
                        zbase = 2 * g * (g + 1)

                        def prologue():
                            kb.op(kb.dve, lambda: nc.vector.memset(R34[:, :], 0.0), wr=[R34_b])
                            kb.op(kb.dve, lambda: nc.vector.memset(RHL[0][0:34, :], 0.0), wr=[RHL_b[0]])

                        def c0_of(i):
                            return 128 * (3 - i) if i < 4 else 0

                        def emit_Z(i):
                            kt = kts[i]
                            ks = slice(kt * 128, (kt + 1) * 128)
                            b3 = (zbase + i) % NB3
                            c0 = c0_of(i)
                            pe = kb.pe
                            pe.wait(pz_b[b3].wr_deps(), kT_b[kt // 4].w, qz_b[g].w)
                            ins = None
                            diag = kt >= 4 * g
                            for hh in range(2):
                                ins = nc.tensor.matmul(pz[b3][:, hh, c0:512], kT[:, ks], qz[:, hh, g * 512 + c0:(g + 1) * 512],
                                                       start=True, stop=not diag)
                                if diag:
                                    ins = nc.tensor.matmul(pz[b3][:, hh, c0:512], C["ident"],
                                                           C["negmask"][:, kt - 4 * g, c0:512], start=False, stop=True)
                            tok = pe.done(ins)
                            kT_b[kt // 4].note_read(tok)
                            qz_b[g].note_read(tok)
                            pz_b[b3].note_write(tok)
                            e = i % 2
                            kb.op(kb.act, lambda: nc.scalar.activation(out=E[e][:, :, c0:512], in_=pz[b3][:, :, c0:512],
                                                                       func=AF.Exp),
                                  rd=[pz_b[b3]], wr=[E_b[e]])
                            kb.op(kb.act, lambda: nc.scalar.activation(out=SP[e][:, :, c0:512], in_=E[e][:, :, c0:512],
                                                                       func=AF.Ln, bias=1.0),
                                  rd=[E_b[e]], wr=[SP_b[e]])

                        def emit_R(i):
                            b3 = i % 2
                            c0 = c0_of(i)
                            terms = [(C["onesel"][:, hh, :], SP[b3][:, hh, c0:512], [SP_b[b3]]) for hh in range(2)]
                            kb.mm_group(pr[:, c0:512], pr_b, terms)
                            kb.op(kb.dve, lambda: nc.vector.tensor_tensor(R34[:, c0:512], R34[:, c0:512], pr[0:34, c0:512],
                                                                          ALU.subtract),
                                  rd=[pr_b], wr=[R34_b])
                            nx = RHL[(i + 1) % 2]
                            nx_b = RHL_b[(i + 1) % 2]
                            kb.op(kb.dve, lambda: nc.vector.tensor_copy(nx[0:34, :], R34[:, :]), rd=[R34_b], wr=[nx_b])
                            kb.op(kb.dve, lambda: nc.vector.tensor_tensor(nx[32:34, :], R34[32:34, :], nx[32:34, :],
                                                                          ALU.subtract),
                                  rd=[R34_b], wr=[nx_b])

                        def emit_C(i):
                            b3 = (zbase + i) % NB3
                            b2 = i % 2
                            c0 = c0_of(i)
                            pe = kb.pe
                            pe.wait(pz_b[b3].wr_deps(), SP_b[b2].w, RHL_b[i % 2].w)
                            ins = None
                            for hh in range(2):
                                nc.tensor.matmul(pz[b3][:, hh, c0:512], C["uincneg"], SP[b2][:, hh, c0:512], start=False,
                                                 stop=False, skip_group_check=True)
                                ins = nc.tensor.matmul(pz[b3][:, hh, c0:512], C["sel"][:, hh, :], RHL[i % 2][:, c0:512],
                                                       start=False, stop=True, skip_group_check=True)
                            tok = pe.done(ins)
                            SP_b[b2].note_read(tok)
                            RHL_b[i % 2].note_read(tok)
                            pz_b[b3].note_write(tok)
                            kb.op(kb.act, lambda: nc.scalar.activation(out=A[b2][:, :, c0:512], in_=pz[b3][:, :, c0:512],
                                                                       func=AF.Exp),
                                  rd=[pz_b[b3]], wr=[A_b[b2]])

                        def emit_AV(i):
                            kt = kts[i]
                            b3 = i % 2
                            c0 = c0_of(i)
                            pe = kb.pe
                            deps = [A_b[b3].w, Vt_b[kt].w]
                            if i == 0:
                                deps += po_b.wr_deps()
                            pe.wait(deps)
                            ins = None
                            if i == 0:
                                nc.tensor.matmul(po[:, :], C["ident"], C["zeros512"], start=True, stop=False,
                                                 skip_group_check=True)
                            for hh in range(2):
                                ins = nc.tensor.matmul(po[:, c0:512], Vt[:, kt, hh, :], A[b3][:, hh, c0:512],
                                                       start=False, stop=(i == nsteps - 1 and hh == 1),
                                                       skip_group_check=True)
                            tok = pe.done(ins)
                            A_b[b3].note_read(tok)
                            Vt_b[kt].note_read(tok)
                            if i == nsteps - 1:
                                po_b.note_write(tok)

                        return dict(g=g, qs=qs, nsteps=nsteps, prologue=prologue, Z=emit_Z, R=emit_R, C=emit_C, AV=emit_AV)

                    cur = make_group(0)
                    cur["prologue"]()
                    cur["Z"](0)
                    for g in range(nq):
                        nxt = make_group(g + 1) if g + 1 < nq else None
                        n_ = cur["nsteps"]
                        for i in range(n_):
                            if i + 1 < n_:
                                cur["Z"](i + 1)
                                cur["R"](i)
                            cur["C"](i)
                            if i >= 1:
                                cur["AV"](i - 1)
                        if nxt is not None:
                            nxt["prologue"]()
                            nxt["Z"](0)
                        cur["AV"](n_ - 1)
                        qs_ = cur["qs"]
                        kb.op(kb.dve, lambda: nc.vector.tensor_copy(oT[:, c, qs_], po[:, :]), rd=[po_b], wr=[oT_b[g]])
                        if bg is not None:
                            bg.emit(6)
                        cur = nxt
                if bg is not None:
                    bg.flush()
                kb.barrier()
        stage_outproj(kb, C, oT, oT_b, wo_d, wo_buf, xin, xin_bufs, xout, xout_bufs)


TWO_PI = 6.283185307179586
NCMP = 255


def nsa_stage_proj(kb, C, hnT, hnT_b, posb_d, wfm_d, wfm_buf, wgt_d, wgt_buf, wtm_d, wtm_buf,
                   q_s, q_sb, kv_s, kv_sb, Vtm, Vtm_b, sigT, sigT_b, cmpw, kcmp, kcmp_b, vcmp, vcmp_b):
    nc = kb.nc
    with ExitStack() as st:
        sb = lambda n, shp, dt: st.enter_context(nc.sbuf_tensor(kb.name(n), shp, dt))
        ps = lambda n, shp, dt: st.enter_context(nc.psum_tensor(kb.name(n), shp, dt))
        cosF = sb("cosF", [128, S], F32)
        sinS = sb("sinS", [128, S], F32)
        cs_b = Buf()
        t_b = Buf()
        s_misc = kb.slot(kb.name("misc"))
        s_miscp = kb.slot(kb.name("miscp"))
        st_tmp = ExitStack()
        HS = S // 2
        posi = st_tmp.enter_context(nc.sbuf_tensor(kb.name("posi"), [128, HS], I32))
        ang = st_tmp.enter_context(nc.sbuf_tensor(kb.name("ang"), [128, HS], F32))
        tmp = st_tmp.enter_context(nc.sbuf_tensor(kb.name("tmpang"), [128, HS], F32))
        kf = st_tmp.enter_context(nc.sbuf_tensor(kb.name("kfang"), [128, HS], F32))
        C1 = 6.28125
        C2 = TWO_PI - 6.28125

        def reduce_sin(dst, shift, post_scale):
            if shift != 0.0:
                kb.op(kb.dve, lambda: nc.vector.tensor_scalar(tmp[:, :], ang[:, :], shift, None, ALU.add),
                      rd=[t_b], wr=[t_b])
                src = tmp
            else:
                src = ang
            kb.op(kb.dve, lambda: nc.vector.tensor_scalar(kf[:, :], src[:, :], 1.0 / TWO_PI, None, ALU.mult),
                  rd=[t_b], wr=[t_b])
            kb.op(kb.dve, lambda: nc.vector.tensor_copy(posi[:, :], kf[:, :]), rd=[t_b], wr=[t_b])
            kb.op(kb.dve, lambda: nc.vector.tensor_copy(kf[:, :], posi[:, :]), rd=[t_b], wr=[t_b])
            kb.op(kb.dve, lambda: nc.vector.scalar_tensor_tensor(out=tmp[:, :], in0=kf[:, :], scalar=-C1, in1=src[:, :],
                                                                 op0=ALU.mult, op1=ALU.add), rd=[t_b], wr=[t_b])
            kb.op(kb.dve, lambda: nc.vector.scalar_tensor_tensor(out=tmp[:, :], in0=kf[:, :], scalar=-C2, in1=tmp[:, :],
                                                                 op0=ALU.mult, op1=ALU.add), rd=[t_b], wr=[t_b])
            kb.op(kb.dve, lambda: nc.vector.tensor_scalar(kf[:, :], tmp[:, :], float(np.pi), TWO_PI, ALU.is_gt, ALU.mult),
                  rd=[t_b], wr=[t_b])
            kb.op(kb.dve, lambda: nc.vector.tensor_tensor(tmp[:, :], tmp[:, :], kf[:, :], ALU.subtract),
                  rd=[t_b], wr=[t_b])
            kb.op(kb.dve, lambda: nc.vector.tensor_scalar(kf[:, :], tmp[:, :], -float(np.pi), TWO_PI, ALU.is_lt, ALU.mult),
                  rd=[t_b], wr=[t_b])
            kb.op(kb.dve, lambda: nc.vector.tensor_tensor(tmp[:, :], tmp[:, :], kf[:, :], ALU.add),
                  rd=[t_b], wr=[t_b])
            kb.op(kb.act, lambda: nc.scalar.activation(out=tmp[:, :], in_=tmp[:, :], func=AF.Sin), rd=[t_b], wr=[t_b])
            if post_scale is None:
                kb.op(kb.dve, lambda: nc.vector.tensor_copy(dst, tmp[:, :]), rd=[t_b], wr=[cs_b])
            else:
                kb.op(kb.dve, lambda: nc.vector.tensor_scalar(dst, tmp[:, :], post_scale, None, ALU.mult),
                      rd=[t_b], wr=[cs_b])

        for hf in range(2):
            cols = slice(hf * HS, (hf + 1) * HS)
            kb.dma(kb.sp, s_misc, posi[:, :], posb_d[:, cols], wr=[t_b])
            kb.op(kb.dve, lambda: nc.vector.tensor_copy(ang[:, :], posi[:, :]), rd=[t_b], wr=[t_b])
            kb.op(kb.dve, lambda: nc.vector.tensor_scalar(ang[:, :], ang[:, :], C["ropec"][:, 0:1], None, ALU.mult),
                  rd=[t_b], wr=[t_b])
            reduce_sin(sinS[:, cols], 0.0, C["ropec"][:, 1:2])
            reduce_sin(cosF[:, cols], float(np.pi / 2), None)
        kb.barrier()
        st_tmp.close()

        st_p = ExitStack()
        sbp = lambda n, shp, dt: st_p.enter_context(nc.sbuf_tensor(kb.name(n), shp, dt))
        wch = [sbp("wch", [128, 2, 8, 128], BF16) for _ in range(2)]
        wch_b = [Buf() for _ in range(2)]
        wch_s = [kb.slot(kb.name("wch")) for _ in range(2)]
        wgt = sbp("wgt", [128, 8, 48], BF16)
        wtm = sbp("wtm", [128, 8, 256], BF16)
        wgt_b = Buf()
        wtm_b = Buf()
        kb.dma(kb.pool, s_miscp, wgt[:, :, :], wgt_d.rearrange("p (kc n) -> p kc n", kc=8), rd=[wgt_buf], wr=[wgt_b])
        kb.dma(kb.pool, s_miscp, wtm[:, :, :], wtm_d.rearrange("p (kc n) -> p kc n", kc=8), rd=[wtm_buf], wr=[wtm_b])
        kb.batch_end(s_miscp, [wgt_b, wtm_b])
        pa = [ps("pa", [128, 512], F32) for _ in range(2)]
        pa_b = [Buf() for _ in range(2)]
        pb = [ps("pb", [128, 512], F32) for _ in range(2)]
        pb_b = [Buf() for _ in range(2)]
        t1 = [sbp("t1", [128, 512], F32) for _ in range(2)]
        t1_b = [Buf() for _ in range(2)]
        t2 = [sbp("t2", [128, 512], F32) for _ in range(2)]
        t2_b = [Buf() for _ in range(2)]
        ro = [sbp("ro", [128, 512], BF16) for _ in range(3)]
        ro_b = [Buf() for _ in range(3)]
        ro_s = [kb.slot(kb.name("ro")) for _ in range(3)]
        jobs = [(c, 8 + c, ("q", c)) for c in range(8)]
        jobs += [(16, None, ("kv", 0)), (17, None, ("kv", 1)), (18, 19, ("kv", 2)), (20, 21, ("kv", 3))]
        ci = 0
        ri = 0
        PL = DBG.get("proj", 9)
        for (cp, cq, dest) in (jobs if PL >= 2 else []):
            k = ci % 2
            ci += 1
            kb.dma(kb.sp, wch_s[k], wch[k][:, 0, :, :], wfm_d[cp].rearrange("p (kc n) -> p kc n", kc=8),
                   rd=[wfm_buf], wr=[wch_b[k]])
            if cq is not None:
                kb.dma(kb.sp, wch_s[k], wch[k][:, 1, :, :], wfm_d[cq].rearrange("p (kc n) -> p kc n", kc=8),
                       rd=[wfm_buf], wr=[wch_b[k]])
            for tg in range(8):
                ts_ = slice(tg * 512, (tg + 1) * 512)
                p = tg % 2
                r = ri % 3
                ri += 1
                terms = [(wch[k][:, 0, kc, :], hnT[:, kc, ts_], [wch_b[k]] + hnT_b[tg * 4:(tg + 1) * 4]) for kc in range(8)]
                kb.mm_group(pa[p][:, :], pa_b[p], terms)
                if cq is not None:
                    terms = [(wch[k][:, 1, kc, :], hnT[:, kc, ts_], [wch_b[k]] + hnT_b[tg * 4:(tg + 1) * 4])
                             for kc in range(8)]
                    kb.mm_group(pb[p][:, :], pb_b[p], terms)
                    kb.op(kb.dve, lambda: nc.vector.tensor_tensor(t1[p][:, :], pa[p][:, :], cosF[:, ts_], ALU.mult),
                          rd=[pa_b[p], cs_b], wr=[t1_b[p]])
                    kb.op(kb.dve, lambda: nc.vector.tensor_tensor(t2[p][:, :], pb[p][:, :], sinS[:, ts_], ALU.mult),
                          rd=[pb_b[p], cs_b], wr=[t2_b[p]])
                    kb.op(kb.pool, lambda: nc.gpsimd.tensor_tensor(ro[r][:, :], t1[p][:, :], t2[p][:, :], ALU.add),
                          rd=[t1_b[p], t2_b[p]], wr=[ro_b[r]])
                else:
                    kb.op(kb.act, lambda: nc.scalar.copy(ro[r][:, :], pa[p][:, :]), rd=[pa_b[p]], wr=[ro_b[r]])
                if dest[0] == "q":
                    c = dest[1]
                    kb.dma(kb.pool, ro_s[r], q_s[2 * c:2 * c + 2, :, ts_].rearrange("h d t -> (h d) t"), ro[r][:, :],
                           rd=[ro_b[r]], wr=[q_sb])
                else:
                    kb.dma(kb.pool, ro_s[r], kv_s[dest[1], :, ts_], ro[r][:, :], rd=[ro_b[r]], wr=[kv_sb])
        kb.op(kb.pool, lambda: nc.gpsimd.memset(sigT[:, :], 0.0), wr=[sigT_b])
        for tg in range(8 if PL >= 3 else 0):
            ts_ = slice(tg * 512, (tg + 1) * 512)
            p = tg % 2
            terms = [(wgt[:, kc, :], hnT[:, kc, ts_], [wgt_b] + hnT_b[tg * 4:(tg + 1) * 4]) for kc in range(8)]
            kb.mm_group(pa[p][0:48, :], pa_b[p], terms)
            kb.op(kb.act, lambda: nc.scalar.activation(out=sigT[0:48, ts_], in_=pa[p][0:48, :], func=AF.Sigmoid),
                  rd=[pa_b[p]], wr=[sigT_b])
        kb.op(kb.pool, lambda: nc.gpsimd.memset(Vtm[:, :, :, :], 1.0), wr=Vtm_b)
        for tt2 in range(NT // 2 if PL >= 4 else 0):
            p = tt2 % 2
            pe = kb.pe
            pe.wait(pb_b[p].wr_deps(), wtm_b.w, [hnT_b[tt2 * 2 + i].w for i in range(2)])
            ins = None
            for i in range(2):
                tt = tt2 * 2 + i
                for kc in range(8):
                    ins = nc.tensor.matmul(pb[p][:, i * 256:(i + 1) * 256], hnT[:, kc, tt * 128:(tt + 1) * 128],
                                           wtm[:, kc, :], start=(kc == 0), stop=(kc == 7))
            tok = pe.done(ins)
            wtm_b.note_read(tok)
            pb_b[p].note_write(tok)
            src = pb[p][:, :].rearrange("p (i g d) -> p i g d", i=2, g=4)
            eng = kb.act if tt2 % 2 == 0 else kb.dve
            for i in range(2):
                if True:
                    kb.op(kb.act, lambda: nc.scalar.copy(Vtm[:, tt2 * 2 + i, :, 0:64], src[:, i, :, :]),
                          rd=[pb_b[p]], wr=[Vtm_b[tt2 * 2 + i]])
                else:
                    kb.op(kb.dve, lambda: nc.vector.tensor_copy(Vtm[:, tt2 * 2 + i, :, 0:64], src[:, i, :, :]),
                          rd=[pb_b[p]], wr=[Vtm_b[tt2 * 2 + i]])
        kb.barrier()
        st_p.close()
        t1 = [sb("t1c", [64, 256], F32)]
        t2 = [sb("t2c", [64, 256], F32)]
        t1_b = [Buf()]
        t2_b = [Buf()]
        (w1_d, w1_buf, w2_d, w2_buf, pe_d, pe_buf) = cmpw
        W2 = sb("W2", [128, 192], BF16)
        peT = sb("peT", [64, 2, 32], BF16)
        w_b = Buf()
        kb.dma(kb.sp, s_misc, W2[:, :], w2_d, rd=[w2_buf], wr=[w_b])
        kb.dma(kb.sp, s_misc, peT[:, :, :], pe_d.rearrange("p (k l) -> p k l", k=2), rd=[pe_buf], wr=[w_b])
        Xs = [sb("Xc", [64, S], BF16) for _ in range(2)]
        X_bs = [Buf() for _ in range(2)]
        X_ss = [kb.slot(kb.name("Xc")) for _ in range(2)]
        W1s = [sb("W1", [64, 32, 128], BF16) for _ in range(2)]
        hb = sb("hb", [128, 1], F32)
        u = sb("cu", [128, NCMP], F32)
        u2 = sb("cu2", [128, NCMP], F32)
        sg = sb("csg", [128, NCMP], F32)
        hd = sb("chd", [128, 256], BF16)
        c_b = Buf()
        ph = ps("ph", [128, 256], F32)
        ph_b = Buf()
        phb = ps("phb", [128, 2], F32)
        phb_b = Buf()
        kb.op(kb.pool, lambda: nc.gpsimd.memset(kcmp[:, :, :], 0.0), wr=[kcmp_b])
        kb.op(kb.pool, lambda: nc.gpsimd.memset(vcmp[:, :, :, :], 1.0), wr=[vcmp_b])
        for kv in range(2 if PL >= 5 else 0):
            W1 = W1s[kv]
            kb.dma(kb.sp, s_misc, W1[:, :, :], w1_d[kv].rearrange("p (l n) -> p l n", l=32), rd=[w1_buf], wr=[w_b])
            for g in range(2):
                X = Xs[g]
                X_b = X_bs[g]
                kb.dma(kb.sp, X_ss[g], X[:, :], kv_s[kv, g * 64:(g + 1) * 64, :], rd=[kv_sb], wr=[X_b])
                terms = [(W1[:, l, :], X[:, l:l + 16 * (NCMP - 1) + 1:16], [w_b, X_b]) for l in range(32)]
                kb.mm_group(ph[:, 0:NCMP], ph_b, terms)
                terms = [(W1[:, l, :], peT[:, kv, l:l + 1], [w_b]) for l in range(32)]
                kb.mm_group(phb[:, 0:1], phb_b, terms)
                kb.op(kb.dve, lambda: nc.vector.tensor_copy(hb[:, :], phb[:, 0:1]), rd=[phb_b], wr=[c_b])
                kb.op(kb.act, lambda: nc.scalar.activation(out=u[:, :], in_=ph[:, 0:NCMP], func=AF.Identity,
                                                           bias=hb[:, 0:1]), rd=[ph_b, c_b], wr=[c_b])
                kb.op(kb.dve, lambda: nc.vector.tensor_tensor(u2[:, :], u[:, :], u[:, :], ALU.mult), rd=[c_b], wr=[c_b])
                kb.op(kb.dve, lambda: nc.vector.tensor_scalar(u2[:, :], u2[:, :], 0.044715, 1.0, ALU.mult, ALU.add),
                      rd=[c_b], wr=[c_b])
                kb.op(kb.dve, lambda: nc.vector.tensor_tensor(u2[:, :], u2[:, :], u[:, :], ALU.mult), rd=[c_b], wr=[c_b])
                kb.op(kb.act, lambda: nc.scalar.activation(out=sg[:, :], in_=u2[:, :], func=AF.Sigmoid,
                                                           scale=1.5957691216057308), rd=[c_b], wr=[c_b])
                kb.op(kb.dve, lambda: nc.vector.tensor_tensor(hd[:, 0:NCMP], u[:, :], sg[:, :], ALU.mult),
                      rd=[c_b], wr=[c_b])
                if kv == 0:
                    kb.mm_group(pa[0][0:64, 0:NCMP], pa_b[0], [(W2[:, 0:64], hd[:, 0:NCMP], [w_b, c_b])])
                    kb.mm_group(pb[0][0:64, 0:NCMP], pb_b[0], [(W2[:, 64:128], hd[:, 0:NCMP], [w_b, c_b])])
                    cs_sl = slice(31, 31 + 16 * (NCMP - 1) + 1, 16)
                    kb.op(kb.dve, lambda: nc.vector.tensor_tensor(t1[0][0:64, 0:NCMP], pa[0][0:64, 0:NCMP],
                                                                  cosF[0:64, cs_sl], ALU.mult),
                          rd=[pa_b[0], cs_b], wr=[t1_b[0]])
                    kb.op(kb.dve, lambda: nc.vector.tensor_tensor(t2[0][0:64, 0:NCMP], pb[0][0:64, 0:NCMP],
                                                                  sinS[0:64, cs_sl], ALU.mult),
                          rd=[pb_b[0], cs_b], wr=[t2_b[0]])
                    kb.op(kb.dve, lambda: nc.vector.tensor_tensor(kcmp[0:64, g, 0:NCMP], t1[0][0:64, 0:NCMP],
                                                                  t2[0][0:64, 0:NCMP], ALU.add),
                          rd=[t1_b[0], t2_b[0]], wr=[kcmp_b])
                else:
                    for ct in range(2):
                        rows = 128 if ct == 0 else NCMP - 128
                        kb.mm_group(pa[1][0:rows, 0:64], pa_b[1],
                                    [(hd[:, ct * 128:ct * 128 + rows], W2[:, 128:192], [w_b, c_b])])
                        kb.op(kb.dve, lambda: nc.vector.tensor_copy(vcmp[0:rows, ct, g, 0:64], pa[1][0:rows, 0:64]),
                              rd=[pa_b[1]], wr=[vcmp_b])
        kb.barrier()


def nsa_stage_select(kb, C, q_s, q_sb, kcmp, kcmp_b, mb_s, mb_sb, nqt=NT):
    nc = kb.nc
    with ExitStack() as st:
        sb = lambda n, shp, dt: st.enter_context(nc.sbuf_tensor(kb.name(n), shp, dt))
        ps = lambda n, shp, dt: st.enter_context(nc.psum_tensor(kb.name(n), shp, dt))
        Q8 = sb("Q8", [64, 8, S], BF16)
        Q8_b = Buf()
        MBT = sb("MBT", [64, S], BF16)
        MBT_b = Buf()
        P = [[sb("P1", [128, NCMP], F32) for _ in range(8)] for _ in range(3)]
        P_b = [[Buf() for _ in range(8)] for _ in range(3)]
        den = [sb("den1", [128, 16], F32) for _ in range(3)]
        den_b = [Buf() for _ in range(3)]
        accs = [sb("acc1", [128, 256], F32) for _ in range(2)]
        accs_b = [Buf() for _ in range(2)]
        accTs = [sb("accT", [128, 2, 128], F32) for _ in range(2)]
        accTs_b = [Buf() for _ in range(2)]
        score = sb("score", [128, 64], F32)
        work = sb("work", [128, 64], F32)
        m8 = sb("m8", [128, 16], F32)
        thr = sb("thr", [128, 1], F32)
        mbq = sb("mbq", [128, 64], BF16)
        sc_b = Buf()
        NPS = 3
        pss = [ps("pss", [128, 256], F32) for _ in range(NPS)]
        pss_b = [Buf() for _ in range(NPS)]
        pts = [ps("pt1", [128, 2, 128], F32) for _ in range(2)]
        pts_b = [Buf() for _ in range(2)]
        pimps = [ps("pimp", [128, 64], F32) for _ in range(2)]
        pimps_b = [Buf() for _ in range(2)]
        pmt = ps("pmt", [64, 128], BF16)
        pmt_b = Buf()
        s_misc = kb.slot(kb.name("misc"))
        s_out = kb.slot(kb.name("mbout"))
        hi = 0
        for g in range(2):
            for j in range(8):
                kb.dma(kb.sp, s_misc, Q8[:, j, :], q_s[g * 8 + j], rd=[q_sb], wr=[Q8_b])
            for i_ in range(2):
                kb.op(kb.dve, lambda: nc.vector.memset(accs[i_][:, :], 0.0), wr=[accs_b[i_]])

            def stage_a(qt):
                nonlocal hi
                qs = slice(qt * 128, (qt + 1) * 128)
                b2 = qt % 3
                for j in range(8):
                    p4 = hi % NPS
                    hi += 1
                    terms = [(Q8[:, j, qs], kcmp[0:64, g, 0:NCMP], [Q8_b, kcmp_b]),
                             (C["ident"], C["cmask_tm"][:, 248 - 8 * qt:248 - 8 * qt + NCMP], [])]
                    kb.mm_group(pss[p4][:, 0:NCMP], pss_b[p4], terms)
                    kb.op(kb.act, lambda: nc.scalar.activation(out=P[b2][j][:, :], in_=pss[p4][:, 0:NCMP], func=AF.Exp,
                                                               scale=0.125, accum_out=den[b2][:, j:j + 1]),
                          rd=[pss_b[p4]], wr=[P_b[b2][j], den_b[b2]])

            def stage_b1(qt):
                qs = slice(qt * 128, (qt + 1) * 128)
                b2 = qt % 2
                acc, acc_b, accT, accT_b = accs[b2], accs_b[b2], accTs[b2], accTs_b[b2]
                pt, pt_b, pimp, pimp_b = pts[b2], pts_b[b2], pimps[b2], pimps_b[b2]
                b2 = qt % 3
                kb.op(kb.dve, lambda: nc.vector.tensor_scalar(den[b2][:, 8:16], den[b2][:, 0:8], 1e-30, None, ALU.add),
                      rd=[den_b[b2]], wr=[den_b[b2]])
                kb.op(kb.dve, lambda: nc.vector.reciprocal(den[b2][:, 8:16], den[b2][:, 8:16]),
                      rd=[den_b[b2]], wr=[den_b[b2]])
                for j in range(8):
                    if j == 0:
                        kb.op(kb.dve, lambda: nc.vector.tensor_scalar(acc[:, 0:NCMP], P[b2][j][:, :], den[b2][:, 8:9], None,
                                                                      ALU.mult),
                              rd=[P_b[b2][j], den_b[b2]], wr=[acc_b])
                    else:
                        kb.op(kb.dve, lambda: nc.vector.scalar_tensor_tensor(out=acc[:, 0:NCMP], in0=P[b2][j][:, :],
                                                                             scalar=den[b2][:, 8 + j:9 + j],
                                                                             in1=acc[:, 0:NCMP],
                                                                             op0=ALU.mult, op1=ALU.add),
                              rd=[P_b[b2][j], den_b[b2]], wr=[acc_b])
                pe = kb.pe
                pe.wait(pt_b.wr_deps(), acc_b.w)
                nc.tensor.transpose(pt[:, 0, :], acc[:, 0:128], C["ident32"])
                ins = nc.tensor.transpose(pt[:, 1, :], acc[:, 128:256], C["ident32"])
                tok = pe.done(ins)
                acc_b.note_read(tok)
                pt_b.note_write(tok)
                kb.op(kb.act, lambda: nc.scalar.copy(accT[:, :, :], pt[:, :, :]), rd=[pt_b], wr=[accT_b])
                terms = [(accT[:, 0, :], C["overlap"][:, 0, :], [accT_b]),
                         (accT[0:127, 1, :], C["overlap"][0:127, 1, :], [accT_b])]
                kb.mm_group(pimp[:, :], pimp_b, terms)

            def stage_b2(qt):
                qs = slice(qt * 128, (qt + 1) * 128)
                b2 = qt % 2
                pimp, pimp_b = pimps[b2], pimps_b[b2]
                pe = kb.pe
                kb.op(kb.dve, lambda: nc.vector.tensor_tensor(score[:, :], pimp[:, :],
                                                              C["bonus"][:, 62 - 2 * qt:62 - 2 * qt + 64], ALU.add),
                      rd=[pimp_b], wr=[sc_b])
                kb.op(kb.dve, lambda: nc.vector.tensor_tensor(score[:, :], score[:, :], C["bonus"][:, 128:192], ALU.add),
                      rd=[sc_b], wr=[sc_b])
                kb.op(kb.dve, lambda: nc.vector.max(out=m8[:, 0:8], in_=score[:, :]), rd=[sc_b], wr=[sc_b])
                kb.op(kb.dve, lambda: nc.vector.match_replace(out=work[:, :], in_to_replace=m8[:, 0:8],
                                                              in_values=score[:, :], imm_value=-3.0e38),
                      rd=[sc_b], wr=[sc_b])
                kb.op(kb.dve, lambda: nc.vector.max(out=m8[:, 8:16], in_=work[:, :]), rd=[sc_b], wr=[sc_b])
                kb.op(kb.dve, lambda: nc.vector.tensor_reduce(out=thr[:, :], in_=m8[:, 8:16], axis=AX.X, op=ALU.min),
                      rd=[sc_b], wr=[sc_b])
                kb.op(kb.dve, lambda: nc.vector.tensor_scalar(mbq[:, :], score[:, :], thr[:, 0:1], NEGM, ALU.is_lt,
                                                              ALU.mult),
                      rd=[sc_b], wr=[sc_b])
                pe.wait(pmt_b.wr_deps(), sc_b.w)
                ins = nc.tensor.transpose(pmt[:, :], mbq[:, :], C["ident"])
                tok = pe.done(ins)
                sc_b.note_read(tok)
                pmt_b.note_write(tok)
                kb.op(kb.act, lambda: nc.scalar.copy(MBT[:, qs], pmt[:, :]), rd=[pmt_b], wr=[MBT_b])

            stage_a(0)
            if nqt > 1:
                stage_a(1)
            stage_b1(0)
            for qt in range(nqt):
                if qt + 2 < nqt:
                    stage_a(qt + 2)
                if qt + 1 < nqt:
                    stage_b1(qt + 1)
                stage_b2(qt)
            kb.dma(kb.pool, s_out, mb_s[g], MBT[:, :], rd=[MBT_b], wr=[mb_sb])
        kb.barrier()


def nsa_stage_attn(kb, C, q_s, q_sb, kv_s, kv_sb, mb_s, mb_sb, efull_d, kcmp, kcmp_b, vcmp, vcmp_b, Vtm, Vtm_b,
                   sigT, sigT_b, o_s, o_sb, heads=range(16), nqg=8):
    nc = kb.nc
    with ExitStack() as st:
        sb = lambda n, shp, dt: st.enter_context(nc.sbuf_tensor(kb.name(n), shp, dt))
        ps = lambda n, shp, dt: st.enter_context(nc.psum_tensor(kb.name(n), shp, dt))
        QM = [sb("QM", [128, S], BF16) for _ in range(2)]
        QM_b = [Buf() for _ in range(2)]
        QM_s = [kb.slot(kb.name("QM")) for _ in range(2)]
        ksE = sb("ksE", [128, S], BF16)
        kwT = sb("kwT", [128, S], BF16)
        kk_b = Buf()
        s_kk = kb.slot(kb.name("kk"))
        NPT = 6
        PT = [sb("PT", [128, 512], BF16) for _ in range(NPT)]
        PT_b = [Buf() for _ in range(NPT)]
        dhl = [sb("dhl", [128, 2, 512], BF16) for _ in range(2)]
        dhl_b = [Buf() for _ in range(2)]
        rd = [sb("rd", [64, 512], F32) for _ in range(2)]
        rd_b = [Buf() for _ in range(2)]
        wv = sb("wv", [64, 512], F32)
        tm = sb("tm", [64, 512], F32)
        e_b = Buf()
        oacc = [sb("oacc", [64, 512], F32) for _ in range(2)]
        oacc_b = [Buf() for _ in range(2)]
        obf = [sb("obf", [64, 512], BF16) for _ in range(2)]
        obf_b = [Buf() for _ in range(2)]
        obf_s = [kb.slot(kb.name("obf")) for _ in range(2)]
        NZ = 3
        pz = [ps("pz", [128, 512], F32) for _ in range(NZ)]
        pz_b = [Buf() for _ in range(NZ)]
        po = [ps("po2", [128, 512], F32) for _ in range(3)]
        po_b = [Buf() for _ in range(3)]
        pg = [ps("pg", [128, 512], F32) for _ in range(2)]
        pg_b = [Buf() for _ in range(2)]
        kb.op(kb.pool, lambda: nc.gpsimd.memset(kwT[:, :], 0.0), wr=[kk_b])
        for i_ in range(2):
            kb.op(kb.pool, lambda: nc.gpsimd.memset(dhl[i_][:, :, :], 0.0), wr=[dhl_b[i_]])
        cur_g = -1
        zi = 0
        pi = 0
        oi = 0
        ei = 0
        defer = []

        def run_deferred(force=False):
            keep = []
            for item in defer:
                item[0] -= 1
                if item[0] <= 0 or force:
                    item[1]()
                else:
                    keep.append(item)
            defer[:] = keep

        for hn_, h in enumerate(heads):
            g = h // 8
            qm = QM[hn_ % 2]
            qm_b = QM_b[hn_ % 2]
            kb.dma(kb.sp, QM_s[hn_ % 2], qm[0:64, :], q_s[h], rd=[q_sb], wr=[qm_b])
            kb.dma(kb.sp, QM_s[hn_ % 2], qm[64:128, :], mb_s[g], rd=[mb_sb], wr=[qm_b])
            if g != cur_g:
                cur_g = g
                kb.dma(kb.sp, s_kk, ksE[0:64, :], kv_s[2, g * 64:(g + 1) * 64, :], rd=[kv_sb], wr=[kk_b])
                kb.dma(kb.sp, s_kk, ksE[64:128, :], efull_d, wr=[kk_b])
                kb.dma(kb.sp, s_kk, kwT[0:64, :], kv_s[3, g * 64:(g + 1) * 64, :], rd=[kv_sb], wr=[kk_b])
            for qg in range(nqg):
                qs = slice(qg * 512, (qg + 1) * 512)
                tiles = []
                m0 = C["negmask_cmp"][:, qg, :] if qg <= 4 else None
                tiles.append((0, kcmp[:, g, 0:128], m0, 128, vcmp[:, 0, g, 0:65], [kcmp_b, vcmp_b], 0, 512))
                if qg >= 4:
                    tiles.append((0, kcmp[:, g, 128:NCMP], C["negmask_cmp"][0:127, qg - 4, :], 127,
                                  vcmp[0:127, 1, g, 0:65], [kcmp_b, vcmp_b], 0, 512))
                for kt in range(0, 4 * qg + 4):
                    j = kt - 4 * qg
                    m = C["negmask_le"][:, j, :] if j >= 0 else None
                    tiles.append((1, ksE[:, kt * 128:(kt + 1) * 128], m, 128, Vtm[:, kt, g, 0:65], [kk_b, Vtm_b[kt]],
                                  128 * j if j >= 0 else 0, 512))
                wl = []
                for kt in range(max(0, 4 * qg - 4), 4 * qg):
                    jp = kt - (4 * qg - 4)
                    wl.append((2, kwT[:, kt * 128:(kt + 1) * 128], C["negmask_win"][:, jp, :], 128,
                               Vtm[:, kt, 2 + g, 0:65], [kk_b, Vtm_b[kt]], 0, 128 * (jp + 1)))
                wl.reverse()
                for kt in range(4 * qg, 4 * qg + 4):
                    j = kt - 4 * qg
                    wl.append((2, kwT[:, kt * 128:(kt + 1) * 128], C["negmask_le"][:, j, :], 128,
                               Vtm[:, kt, 2 + g, 0:65], [kk_b, Vtm_b[kt]], 128 * j, 512))
                tiles += wl
                nt_ = len(tiles)
                first = {}
                last = {}
                for i, t in enumerate(tiles):
                    first.setdefault(t[0], i)
                    last[t[0]] = i
                slots = {}
                oa = oacc[oi % 2]
                oa_b = oacc_b[oi % 2]
                ob = obf[oi % 2]
                ob_b = obf_b[oi % 2]
                ob_s = obf_s[oi % 2]
                oi += 1
                done_br = []

                def emit_qk(i, tiles=tiles, slots=slots, qm=qm, qm_b=qm_b, qs=qs, qg=qg):
                    nonlocal zi, pi
                    br, l, m, rows, va, bufs, c0, c1 = tiles[i]
                    z = zi % NZ
                    zi += 1
                    p = pi % NPT
                    pi += 1
                    slots[i] = p
                    terms = [(l, qm[:, qg * 512 + c0:qg * 512 + c1], [bufs[0], qm_b])]
                    if m is not None:
                        terms.append((C["ident"][0:rows, 0:rows], m[:, c0:c1], []))
                    kb.mm_group(pz[z][0:rows, c0:c1], pz_b[z], terms)
                    kb.op(kb.act, lambda: nc.scalar.activation(out=PT[p][0:rows, c0:c1], in_=pz[z][0:rows, c0:c1],
                                                               func=AF.Exp, scale=0.125), rd=[pz_b[z]], wr=[PT_b[p]])

                def make_epilogue(br, h=h, qs=qs, oa=oa, oa_b=oa_b, ob=ob, ob_b=ob_b, ob_s=ob_s, done_br=done_br):
                    nonlocal ei
                    e = ei % 2
                    ei += 1
                    kb.op(kb.dve, lambda: nc.vector.tensor_copy(dhl[e][64:65, 0, :], po[br][64:65, :]), rd=[po_b[br]],
                          wr=[dhl_b[e]])
                    kb.op(kb.dve, lambda: nc.vector.scalar_tensor_tensor(out=dhl[e][64:65, 1, :], in0=po[br][64:65, :],
                                                                         scalar=1e-30, in1=dhl[e][64:65, 0, :],
                                                                         op0=ALU.add, op1=ALU.subtract),
                          rd=[po_b[br]], wr=[dhl_b[e]])

                    def part_b():
                        kb.mm_group(pg[0][:, :], pg_b[0], [(C["onesrow64"], dhl[e][:, 0, :], [dhl_b[e]]),
                                                           (C["onesrow64"], dhl[e][:, 1, :], [dhl_b[e]])])
                        kb.mm_group(pg[1][:, :], pg_b[1], [(C["gsel"][:, h * 3 + br, :], sigT[:, qs], [sigT_b])])
                        kb.op(kb.act, lambda: nc.scalar.activation(out=rd[e][:, :], in_=pg[0][0:64, :], func=AF.Ln),
                              rd=[pg_b[0]], wr=[rd_b[e]])
                        kb.op(kb.act, lambda: nc.scalar.activation(out=rd[e][:, :], in_=rd[e][:, :], func=AF.Exp,
                                                                   scale=-1.0), rd=[rd_b[e]], wr=[rd_b[e]])
                        kb.op(kb.dve, lambda: nc.vector.tensor_tensor(wv[:, :], pg[1][0:64, :], rd[e][:, :], ALU.mult),
                              rd=[pg_b[1], rd_b[e]], wr=[e_b])
                        if not done_br:
                            kb.op(kb.dve, lambda: nc.vector.tensor_tensor(oa[:, :], po[br][0:64, :], wv[:, :], ALU.mult),
                                  rd=[po_b[br], e_b], wr=[oa_b])
                        else:
                            kb.op(kb.dve, lambda: nc.vector.tensor_tensor(tm[:, :], po[br][0:64, :], wv[:, :], ALU.mult),
                                  rd=[po_b[br], e_b], wr=[e_b])
                            kb.op(kb.dve, lambda: nc.vector.tensor_tensor(oa[:, :], oa[:, :], tm[:, :], ALU.add),
                                  rd=[e_b], wr=[oa_b])
                        done_br.append(br)
                        if len(done_br) == 3:
                            kb.op(kb.pool, lambda: nc.gpsimd.tensor_copy(ob[:, :], oa[:, :]), rd=[oa_b], wr=[ob_b])
                            kb.dma(kb.pool, ob_s, o_s[h, :, qs], ob[:, :], rd=[ob_b], wr=[o_sb])
                    defer.append([4, part_b])

                def emit_av(i, tiles=tiles, slots=slots):
                    br, l, m, rows, va, bufs, c0, c1 = tiles[i]
                    p = slots[i]
                    pe = kb.pe
                    deps = [PT_b[p].w, bufs[1].w]
                    if i == first[br]:
                        deps += po_b[br].wr_deps()
                        assert c0 == 0 and c1 == 512
                    pe.wait(deps)
                    ins = nc.tensor.matmul(po[br][0:65, c0:c1], va, PT[p][0:rows, c0:c1], start=(i == first[br]),
                                           stop=(i == last[br]), skip_group_check=True)
                    tok = pe.done(ins)
                    PT_b[p].note_read(tok)
                    bufs[1].note_read(tok)
                    if i == last[br]:
                        po_b[br].note_write(tok)
                        make_epilogue(br)

                emit_qk(0)
                if nt_ > 1:
                    emit_qk(1)
                for i in range(nt_):
                    if i + 2 < nt_:
                        emit_qk(i + 2)
                    emit_av(i)
                    run_deferred()
        run_deferred(force=True)
        run_deferred(force=True)
        kb.barrier()


def phase_nsa(kb, C, xin, xin_bufs, xout, xout_bufs, grow_d, posb_d, W, scr, heads=range(16), nqg=8, nqt=NT):
    nc = kb.nc
    q_s, kv_s, mb_s, o_s = scr["q_s"], scr["kv_s"], scr["mb_s"], scr["o_s"]
    q_sb, kv_sb, mb_sb, o_sb = Buf(), Buf(), Buf(), Buf()
    with ExitStack() as st0:
        sb0 = lambda n, shp, dt: st0.enter_context(nc.sbuf_tensor(kb.name(n), shp, dt))
        Vtm = sb0("Vtm", [128, NT, 4, 80], BF16)
        Vtm_b = [Buf() for _ in range(NT)]
        sigT = sb0("sigT", [128, S], BF16)
        sigT_b = Buf()
        kcmp = sb0("kcmp", [128, 2, 256], BF16)
        kcmp_b = Buf()
        vcmp = sb0("vcmp", [128, 2, 2, 80], BF16)
        vcmp_b = Buf()
        with ExitStack() as st1:
            hnT = st1.enter_context(nc.sbuf_tensor(kb.name("hnT"), [128, 8, S], BF16))
            hnT_b = [Buf() for _ in range(NT)]
            stage_norm_all(kb, C, xin, xin_bufs, grow_d, hnT, hnT_b)
            nsa_stage_proj(kb, C, hnT, hnT_b, posb_d, W["wfm"][0], W["wfm"][1], W["wgt"][0], W["wgt"][1],
                           W["wtm"][0], W["wtm"][1], q_s, q_sb, kv_s, kv_sb, Vtm, Vtm_b, sigT, sigT_b,
                           (W["w1"][0], W["w1"][1], W["w2"][0], W["w2"][1], W["pe"][0], W["pe"][1]),
                           kcmp, kcmp_b, vcmp, vcmp_b)
        if DBG.get("nsa", 9) >= 2:
            nsa_stage_select(kb, C, q_s, q_sb, kcmp, kcmp_b, mb_s, mb_sb, nqt=nqt)
        if DBG.get("nsa", 9) >= 3:
          nsa_stage_attn(kb, C, q_s, q_sb, kv_s, kv_sb, mb_s, mb_sb, C["efull_d"], kcmp, kcmp_b, vcmp, vcmp_b,
                       Vtm, Vtm_b, sigT, sigT_b, o_s, o_sb, heads=heads, nqg=nqg)
    with ExitStack() as st2:
        oT = st2.enter_context(nc.sbuf_tensor(kb.name("oT"), [128, 8, S], BF16))
        oT_b = [Buf() for _ in range(8)]
        sl = kb.slot(kb.name("oTl"))
        for c in range(8):
            kb.dma(kb.sp, sl, oT[:, c, :], o_s[2 * c:2 * c + 2].rearrange("h d t -> (h d) t"), rd=[o_sb], wr=oT_b)
        stage_outproj(kb, C, oT, oT_b, W["wo"][0], W["wo"][1], xin, xin_bufs, xout, xout_bufs)


def host_nsa_weights(w_in, pe_k, pe_v, k_w1, k_w2, v_w1, v_w2, w_out):
    w_in = np.asarray(w_in, np.float32)
    perm64 = (np.arange(64) + 32) % 64
    qperm = (np.arange(1024) // 64) * 64 + perm64[np.arange(1024) % 64]
    kperm = (np.arange(128) // 64) * 64 + perm64[np.arange(128) % 64]
    q = w_in[:, 0:1024]
    blk = lambda i: w_in[:, 1024 + 128 * i:1024 + 128 * (i + 1)]
    kc, vc, ks, vs, kw, vw = [blk(i) for i in range(6)]
    fm = np.concatenate([q, q[:, qperm], kc, vc, ks, ks[:, kperm], kw, kw[:, kperm]], axis=1)
    out = {}
    out["wfm_h"] = host_chunks(fm)
    out["wgt_h"] = host_rows(w_in[:, 1792:1840])
    out["wtm_h"] = host_rows(np.concatenate([vs, vw], axis=1))
    w1 = lambda w: np.ascontiguousarray(np.asarray(w, np.float32).reshape(32, 64, 128).transpose(1, 0, 2)).reshape(64, 4096)
    out["w1_h"] = np.stack([w1(k_w1), w1(v_w1)], axis=0)
    k_w2 = np.asarray(k_w2, np.float32)
    out["w2_h"] = np.ascontiguousarray(np.concatenate([k_w2, k_w2[:, perm64], np.asarray(v_w2, np.float32)], axis=1))
    out["pe_h"] = np.ascontiguousarray(np.concatenate([np.asarray(pe_k, np.float32).T, np.asarray(pe_v, np.float32).T],
                                                      axis=1))
    out["wo_h"] = host_rows(w_out)
    return out


W_SHAPES = {
    "sba_wqk": ([16 * 128, 1024], 128), "sba_wv": ([128, 8192], 128), "sba_wo": ([128, 8192], 128),
    "ffn0_wup": ([NPAIR * 128, 2048], 128), "ffn0_wdn": ([DFF, D], 128),
    "ffn1_wup": ([NPAIR * 128, 2048], 128), "ffn1_wdn": ([DFF, D], 128),
    "nsa_wfm": ([22 * 128, 1024], 128), "nsa_wgt": ([128, 384], 128), "nsa_wtm": ([128, 2048], 128),
    "nsa_w1": ([128, 4096], 64), "nsa_w2": ([128, 192], 128), "nsa_pe": ([64, 64], 64), "nsa_wo": ([128, 8192], 128),
}


def build_full():
    kb = KB()
    nc = kb.nc
    x_d = nc.dram_tensor("x", [S, D], F32, kind="ExternalInput").ap()
    posb_d = nc.dram_tensor("posb", [128, S], I32, kind="ExternalInput").ap()
    g_d = {n: nc.dram_tensor(n, [128, D], F32, kind="ExternalInput").ap()
           for n in ("g_mix0", "g_ffn0", "g_mix1", "g_ffn1", "g_fin")}
    cwb_d = [nc.dram_tensor(f"cwb{l}", [128, 4 * 44], F32, kind="ExternalInput").ap() for l in range(2)]
    y_d = nc.dram_tensor("y", [S, D], F32, kind="ExternalOutput").ap()
    H, Sx, Wb = {}, {}, {}
    for n, (shp, rc) in W_SHAPES.items():
        H[n] = nc.dram_tensor(n + "_h", shp, F32, kind="ExternalInput").ap()
        Sx[n] = nc.dram_tensor(n + "_s", shp, BF16).ap()
        Wb[n] = Buf()
    xa = nc.dram_tensor("xa", [S, D], F32).ap()
    xb = nc.dram_tensor("xb", [S, D], F32).ap()
    xc = nc.dram_tensor("xc", [S, D], F32).ap()
    scr = {"q_s": nc.dram_tensor("q_s", [16, 64, S], BF16).ap(), "kv_s": nc.dram_tensor("kv_s", [4, 128, S], BF16).ap(),
           "mb_s": nc.dram_tensor("mb_s", [2, 64, S], BF16).ap(), "o_s": nc.dram_tensor("o_s", [16, 64, S], BF16).ap()}
    jobs = []
    jobs_bg = []
    for n, (shp, rc) in W_SHAPES.items():
        for r0 in range(0, shp[0], rc):
            (jobs if n.startswith("sba_") else jobs_bg).append((H[n][r0:r0 + rc, :], Sx[n][r0:r0 + rc, :], Wb[n]))
    phase_convert(kb, jobs)
    chunk = lambda ap, p: ap.rearrange("(c p) n -> c p n", p=p)
    x_b = [Buf() for _ in range(NT)]
    xa_b = [Buf() for _ in range(NT)]
    xb_b = [Buf() for _ in range(NT)]
    xc_b = [Buf() for _ in range(NT)]
    y_b = [Buf() for _ in range(NT)]
    with ExitStack() as cst:
        C = load_consts(kb, SBA_CONSTS, cst)
        bg = BgConv(kb, cst, jobs_bg)
        phase_sba(kb, C, x_d, x_b, xa, xa_b, g_d["g_mix0"], chunk(Sx["sba_wqk"], 128), Wb["sba_wqk"],
                  Sx["sba_wv"], Wb["sba_wv"], Sx["sba_wo"], Wb["sba_wo"], bg=bg)
    with ExitStack() as cst:
        C = load_consts(kb, FFN_CONSTS, cst)
        phase_ffn(kb, C, 0, xa, xa_b, xb, xb_b, chunk(Sx["ffn0_wup"], 128), Wb["ffn0_wup"],
                  chunk(Sx["ffn0_wdn"], 128), Wb["ffn0_wdn"], cwb_d[0], g_d["g_ffn0"])
    W = {"wfm": (chunk(Sx["nsa_wfm"], 128), Wb["nsa_wfm"]), "wgt": (Sx["nsa_wgt"], Wb["nsa_wgt"]),
         "wtm": (Sx["nsa_wtm"], Wb["nsa_wtm"]), "w1": (chunk(Sx["nsa_w1"], 64), Wb["nsa_w1"]),
         "w2": (Sx["nsa_w2"], Wb["nsa_w2"]), "pe": (Sx["nsa_pe"], Wb["nsa_pe"]), "wo": (Sx["nsa_wo"], Wb["nsa_wo"])}
    with ExitStack() as cst:
        C = load_consts(kb, NSA_CONSTS, cst)
        phase_nsa(kb, C, xb, xb_b, xc, xc_b, g_d["g_mix1"], posb_d, W, scr)
    with ExitStack() as cst:
        C = load_consts(kb, FFN_CONSTS, cst)
        phase_ffn(kb, C, 1, xc, xc_b, y_d, y_b, chunk(Sx["ffn1_wup"], 128), Wb["ffn1_wup"],
                  chunk(Sx["ffn1_wdn"], 128), Wb["ffn1_wdn"], cwb_d[1], g_d["g_ffn1"], final_grow_d=g_d["g_fin"])
    return kb


def kernel(x, positions, norm_mix, sba_w_in, sba_w_out, nsa_w_in, nsa_cmp_pos_k, nsa_cmp_pos_v, nsa_cmp_k_w1,
           nsa_cmp_k_w2, nsa_cmp_v_w1, nsa_cmp_v_w2, nsa_w_out, norm_ffn, ffn_w_up, ffn_conv_w, ffn_conv_b,
           ffn_w_down, norm_final):
    x = np.asarray(x, np.float32)
    positions = np.asarray(positions)
    B = x.shape[0]
    shared = {}
    sw = np.asarray(sba_w_in[0], np.float32)
    shared["sba_wqk_h"] = host_chunks(sw[:, :2048])
    shared["sba_wv_h"] = host_rows(sw[:, 2048:])
    shared["sba_wo_h"] = host_rows(sba_w_out[0])
    for l in range(2):
        shared[f"ffn{l}_wup_h"] = host_wup(ffn_w_up[l])
        shared[f"ffn{l}_wdn_h"] = np.ascontiguousarray(np.asarray(ffn_w_down[l], np.float32))
        shared[f"cwb{l}"] = host_cwb(ffn_conv_w[l], ffn_conv_b[l])
    hw = host_nsa_weights(nsa_w_in[0], nsa_cmp_pos_k[0], nsa_cmp_pos_v[0], nsa_cmp_k_w1[0], nsa_cmp_k_w2[0],
                          nsa_cmp_v_w1[0], nsa_cmp_v_w2[0], nsa_w_out[0])
    for k_ in ("wfm", "wgt", "wtm", "w1", "w2", "pe", "wo"):
        shared["nsa_" + k_ + "_h"] = np.ascontiguousarray(hw[k_ + "_h"].reshape(W_SHAPES["nsa_" + k_][0]))
    shared["g_mix0"] = host_row(norm_mix[0])
    shared["g_mix1"] = host_row(norm_mix[1])
    shared["g_ffn0"] = host_row(norm_ffn[0])
    shared["g_ffn1"] = host_row(norm_ffn[1])
    shared["g_fin"] = host_row(norm_final)
    for k_, v_ in consts_host().items():
        shared["c_" + k_] = v_
    in_maps = []
    for b in range(B):
        m = dict(shared)
        m["x"] = np.ascontiguousarray(x[b])
        m["posb"] = np.ascontiguousarray(np.broadcast_to(positions[b].astype(np.int32)[None, :], (128, S)))
        in_maps.append(m)
    kb = build_full()
    res = run_bass_kernel_spmd(kb.nc, in_maps, core_ids=list(range(B)))
    return np.stack([np.asarray(r["y"], np.float32) for r in res.results], axis=0)
```
